# Optimizing a Trainium2 kernel written in Bass

```python
import jax, jax.numpy as jnp
from jax import lax
import numpy as np

D_MODEL = 1024
BATCH = 2
SEQ = 16384
DEPTH = 4

GRID_W = 64
CTX_LEN = 256
D_BRANCH = D_MODEL // 2
NA_HEAD_DIM = 64
NA_HEADS = D_BRANCH // NA_HEAD_DIM
WIN_ROWS = 8
WIN_COLS = 16
Q_COLS = 16
BAND_COLS = min(Q_COLS + WIN_COLS, GRID_W)
SHORT_CONV_W = 3
CONF_CONV_W = 31
N_BRANCH = 3
EPS = 1e-6
NEG_INF = -1e30

A_Q = 0 * D_BRANCH
A_K = 1 * D_BRANCH
A_V = 2 * D_BRANCH
A_Z = 3 * D_BRANCH
B_B = 4 * D_BRANCH
B_C = 5 * D_BRANCH
B_V = 6 * D_BRANCH
B_Z = 7 * D_BRANCH
C_GLU = 8 * D_BRANCH
C_Z = 10 * D_BRANCH
GATES = 11 * D_BRANCH
D_IN = GATES + N_BRANCH * D_MODEL

kernel_name = 'hybrid_na_shortconv_conformer_dit'


def rms_norm(x, g):
    x32 = x.astype(jnp.float32)
    y = x32 * lax.rsqrt(jnp.mean(x32 * x32, axis=-1, keepdims=True) + EPS)
    return (y * g.astype(jnp.float32)).astype(x.dtype)


def layer_norm(x, g, b):
    x32 = x.astype(jnp.float32)
    mu = jnp.mean(x32, axis=-1, keepdims=True)
    var = jnp.mean(jnp.square(x32 - mu), axis=-1, keepdims=True)
    y = (x32 - mu) * lax.rsqrt(var + EPS) * g.astype(jnp.float32) + b.astype(jnp.float32)
    return y.astype(x.dtype)


def dw_conv(x, w):
    k = w.shape[0]
    return lax.conv_general_dilated(
        x, w[:, None, :].astype(x.dtype), window_strides=(1,),
        padding=[((k - 1) // 2, k // 2)],
        dimension_numbers=('NWC', 'WIO', 'NWC'),
        feature_group_count=x.shape[-1])


def modulate(x, cond, norm_g, w_ada, b_ada):
    shift, scale, gate = jnp.split(jax.nn.silu(cond) @ w_ada + b_ada, 3, axis=-1)
    if cond.ndim == 2:
        shift, scale, gate = shift[:, None], scale[:, None], gate[:, None]
    return rms_norm(x, norm_g) * (1 + scale) + shift, gate


def heads(t):
    return t.reshape(t.shape[:-1] + (NA_HEADS, NA_HEAD_DIM))


def na_query(z, qn_g):
    return rms_norm(heads(z[..., A_Q:A_K]), qn_g) * NA_HEAD_DIM ** -0.5


def na_kv(zkv, kn_g):
    return rms_norm(heads(zkv[..., :D_BRANCH]), kn_g), heads(zkv[..., D_BRANCH:])


def neighbourhood_attention(q, k, v, kc, vc, rpb):
    bsz, seq, n_h, hd = q.shape
    rows = seq // GRID_W
    win_r = min(WIN_ROWS, rows)
    n_cb = GRID_W // Q_COLS
    qg = q.reshape(bsz, rows, n_cb, Q_COLS, n_h, hd)
    kg = k.reshape(bsz, rows, GRID_W, n_h, hd)
    vg = v.reshape(bsz, rows, GRID_W, n_h, hd)
    qcol = jnp.arange(GRID_W).reshape(n_cb, Q_COLS)
    col_start = jnp.clip(qcol - WIN_COLS // 2, 0, GRID_W - WIN_COLS)
    band_start = jnp.clip(jnp.arange(n_cb) * Q_COLS - WIN_COLS // 2, 0, GRID_W - BAND_COLS)
    band_cols = band_start[:, None] + jnp.arange(BAND_COLS)
    rel = band_cols[:, None, :] - col_start[:, :, None]
    col_valid = (rel >= 0) & (rel < WIN_COLS)
    dcol = jnp.clip(band_cols[:, None, :] - qcol[:, :, None], -(WIN_COLS - 1), WIN_COLS - 1) + WIN_COLS - 1
    n_loc = win_r * BAND_COLS

    def row_block(r):
        rs = jnp.clip(r - win_r // 2, 0, rows - win_r)
        q_r = lax.dynamic_index_in_dim(qg, r, axis=1, keepdims=False)
        k_r = lax.dynamic_slice_in_dim(kg, rs, win_r, axis=1)
        v_r = lax.dynamic_slice_in_dim(vg, rs, win_r, axis=1)
        kb = k_r[:, :, band_cols]
        vb = v_r[:, :, band_cols]
        s_loc = jnp.einsum('bcqhd,brcmhd->bhcqrm', q_r, kb).astype(jnp.float32)
        drow = rs + jnp.arange(win_r) - r + WIN_ROWS - 1
        bias = jnp.transpose(rpb[:, drow][:, :, dcol], (0, 2, 3, 1, 4))
        s_loc = jnp.where(col_valid[:, :, None, :], s_loc + bias.astype(jnp.float32), NEG_INF)
        s_ctx = jnp.einsum('bcqhd,blhd->bhcql', q_r, kc).astype(jnp.float32)
        s = jnp.concatenate([s_loc.reshape(bsz, n_h, n_cb, Q_COLS, n_loc), s_ctx], axis=-1)
        p = jax.nn.softmax(s, axis=-1).astype(v.dtype)
        p_loc = p[..., :n_loc].reshape(bsz, n_h, n_cb, Q_COLS, win_r, BAND_COLS)
        p_ctx = p[..., n_loc:]
        return (jnp.einsum('bhcqrm,brcmhd->bcqhd', p_loc, vb)
                + jnp.einsum('bhcql,blhd->bcqhd', p_ctx, vc))

    out = lax.map(row_block, jnp.arange(rows))
    return jnp.moveaxis(out, 0, 1).reshape(bsz, seq, n_h * hd)


def context_attention(qc, kc, vc):
    s = jnp.einsum('blhd,bmhd->bhlm', qc, kc).astype(jnp.float32)
    p = jax.nn.softmax(s, axis=-1).astype(vc.dtype)
    o = jnp.einsum('bhlm,bmhd->blhd', p, vc)
    return o.reshape(o.shape[:2] + (NA_HEADS * NA_HEAD_DIM,))


def short_conv_branch(z, w_conv):
    bg, cg, val, gz = z[..., B_B:B_C], z[..., B_C:B_V], z[..., B_V:B_Z], z[..., B_Z:C_GLU]
    return bg * dw_conv(cg * val, w_conv) * jax.nn.silu(gz)


def conformer_branch(z, w_conv, b_conv, ln_g, ln_b):
    a, g = z[..., C_GLU:C_GLU + D_BRANCH], z[..., C_GLU + D_BRANCH:C_Z]
    u = dw_conv(a * jax.nn.sigmoid(g), w_conv) + b_conv
    u = jax.nn.silu(layer_norm(u, ln_g, ln_b))
    return u * jax.nn.silu(z[..., C_Z:GATES])


def merge_branches(z, ya, yb, yc, w_out_a, w_out_b, w_out_c, w_o):
    ga, gb, gc = jnp.split(jax.nn.sigmoid(z[..., GATES:]), N_BRANCH, axis=-1)
    m = ga * (ya @ w_out_a) + gb * (yb @ w_out_b) + gc * (yc @ w_out_c)
    return m @ w_o


def hybrid_layer(x, xc, c, c_ctx, lp, update_ctx):
    h, gate = modulate(x, c, lp['norm_g'], lp['w_ada'], lp['b_ada'])
    hc, gate_c = modulate(xc, c_ctx, lp['norm_g'], lp['w_ada'], lp['b_ada'])
    z = h @ lp['w_in']
    if update_ctx:
        zc = hc @ lp['w_in']
        kc, vc = na_kv(zc[..., A_K:A_Z], lp['k_norm_g'])
    else:
        kc, vc = na_kv(hc @ lp['w_in'][:, A_K:A_Z], lp['k_norm_g'])

    q = na_query(z, lp['q_norm_g'])
    k, v = na_kv(z[..., A_K:A_Z], lp['k_norm_g'])
    ya = neighbourhood_attention(q, k, v, kc, vc, lp['rpb']) * jax.nn.silu(z[..., A_Z:B_B])
    yb = short_conv_branch(z, lp['conv_short_w'])
    yc = conformer_branch(z, lp['conv_conf_w'], lp['conv_conf_b'], lp['ln_conf_g'], lp['ln_conf_b'])
    x = x + gate * merge_branches(z, ya, yb, yc, lp['w_out_a'], lp['w_out_b'], lp['w_out_c'], lp['w_o'])

    if update_ctx:
        qc = na_query(zc, lp['q_norm_g'])
        yac = context_attention(qc, kc, vc) * jax.nn.silu(zc[..., A_Z:B_B])
        ybc = short_conv_branch(zc, lp['conv_short_w'])
        ycc = conformer_branch(zc, lp['conv_conf_w'], lp['conv_conf_b'], lp['ln_conf_g'], lp['ln_conf_b'])
        xc = xc + gate_c * merge_branches(zc, yac, ybc, ycc, lp['w_out_a'], lp['w_out_b'], lp['w_out_c'], lp['w_o'])
    return x, xc


def setup_inputs(seed: int = 0) -> dict:
    key = jax.random.key(seed)
    ks = jax.random.split(key, 20)

    def nrm(k, shape, scale):
        return jax.random.normal(k, shape, jnp.float32) * scale

    return {
        'x': nrm(ks[0], (BATCH, SEQ, D_MODEL), 1.0),
        'c': nrm(ks[1], (BATCH, D_MODEL), 1.0),
        'ctx': nrm(ks[2], (BATCH, CTX_LEN, D_MODEL), 1.0),
        'c_ctx': nrm(ks[3], (D_MODEL,), 1.0),
        'norm_g': 1.0 + nrm(ks[4], (DEPTH, D_MODEL), 0.05),
        'w_ada': nrm(ks[5], (DEPTH, D_MODEL, 3 * D_MODEL), D_MODEL ** -0.5),
        'b_ada': nrm(ks[6], (DEPTH, 3 * D_MODEL), 0.02),
        'w_in': nrm(ks[7], (DEPTH, D_MODEL, D_IN), D_MODEL ** -0.5),
        'q_norm_g': 1.0 + nrm(ks[8], (DEPTH, NA_HEAD_DIM), 0.05),
        'k_norm_g': 1.0 + nrm(ks[9], (DEPTH, NA_HEAD_DIM), 0.05),
        'rpb': nrm(ks[10], (DEPTH, NA_HEADS, 2 * WIN_ROWS - 1, 2 * WIN_COLS - 1), 0.1),
        'conv_short_w': nrm(ks[11], (DEPTH, SHORT_CONV_W, D_BRANCH), SHORT_CONV_W ** -0.5),
        'conv_conf_w': nrm(ks[12], (DEPTH, CONF_CONV_W, D_BRANCH), CONF_CONV_W ** -0.5),
        'conv_conf_b': nrm(ks[13], (DEPTH, D_BRANCH), 0.02),
        'ln_conf_g': 1.0 + nrm(ks[14], (DEPTH, D_BRANCH), 0.05),
        'ln_conf_b': nrm(ks[15], (DEPTH, D_BRANCH), 0.02),
        'w_out_a': nrm(ks[16], (DEPTH, D_BRANCH, D_MODEL), D_BRANCH ** -0.5),
        'w_out_b': nrm(ks[17], (DEPTH, D_BRANCH, D_MODEL), D_BRANCH ** -0.5),
        'w_out_c': nrm(ks[18], (DEPTH, D_BRANCH, D_MODEL), D_BRANCH ** -0.5),
        'w_o': nrm(ks[19], (DEPTH, D_MODEL, D_MODEL), D_MODEL ** -0.5),
    }


def reference(x, c, ctx, c_ctx, norm_g, w_ada, b_ada, w_in, q_norm_g, k_norm_g, rpb,
              conv_short_w, conv_conf_w, conv_conf_b, ln_conf_g, ln_conf_b,
              w_out_a, w_out_b, w_out_c, w_o):
    xc = ctx
    for l in range(DEPTH):
        lp = {
            'norm_g': norm_g[l], 'w_ada': w_ada[l], 'b_ada': b_ada[l], 'w_in': w_in[l],
            'q_norm_g': q_norm_g[l], 'k_norm_g': k_norm_g[l], 'rpb': rpb[l],
            'conv_short_w': conv_short_w[l], 'conv_conf_w': conv_conf_w[l],
            'conv_conf_b': conv_conf_b[l], 'ln_conf_g': ln_conf_g[l], 'ln_conf_b': ln_conf_b[l],
            'w_out_a': w_out_a[l], 'w_out_b': w_out_b[l], 'w_out_c': w_out_c[l], 'w_o': w_o[l],
        }
        x, xc = hybrid_layer(x, xc, c, c_ctx, lp, update_ctx=(l < DEPTH - 1))
    return x
```

```python
import numpy as np
from contextlib import ExitStack
import concourse.bass as bass
import concourse.mybir as mybir
from concourse.bass_utils import run_bass_kernel_spmd

F32 = mybir.dt.float32
BF16 = mybir.dt.bfloat16
AF = mybir.ActivationFunctionType
ALU = mybir.AluOpType
EPOCH = 30000

D = 1024
DB = 512
NH = 8
HD = 64
GW = 64
ROWS = 256
CTX = 256
DEPTH = 4
D_IN = 8704
A_Q, A_K, A_V, A_Z, B_B, B_C, B_V, B_Z, C_A, C_G, C_Z, GATES = [i * 512 for i in range(12)]
EPS = 1e-6
NEG = -30000.0
SROWS = 96
NPAIR = SROWS // 2
NST = NPAIR // 2
STOK = SROWS * GW
NTOK = STOK + CTX
TOP = 16
OFFS = [-3, -2, -1, 0, 1, 2, 3]
P_NG, P_BADA, P_GQ, P_GK, P_CSW, P_CCW, P_CCB, P_LNG, P_LNB = 0, 8, 32, 33, 34, 46, 170, 174, 178
NPAR = 182


class Buf:
    __slots__ = ("name", "t", "w", "r", "dsem", "dcnt", "sb")

    def __init__(self, name, t, sb):
        self.name = name; self.t = t; self.w = []; self.r = {}; self.dsem = None; self.dcnt = 0; self.sb = sb

    def __getitem__(self, k):
        return self.t[k]


class Prog:
    def __init__(self, nc):
        self.nc = nc
        self.es = ExitStack()
        self.streams = {k: [] for k in ("pe", "act", "dve", "pool", "sp")}
        self.cnt = {k: 0 for k in self.streams}
        self.known = {k: {} for k in self.streams}

    def sbuf(self, name, shape, dtype):
        return Buf(name, self.es.enter_context(self.nc.sbuf_tensor(name, list(shape), dtype)), True)

    def psum(self, name, shape, dtype=F32):
        return Buf(name, self.es.enter_context(self.nc.psum_tensor(name, list(shape), dtype)), True)

    def dram(self, name, shape, dtype, kind="Internal"):
        return Buf(name, self.nc.dram_tensor(name, list(shape), dtype, kind=kind), False)

    def track(self, name):
        return Buf(name, None, False)

    def _wait(self, e, ev):
        k, v = ev
        if self.known[e].get(k, 0) >= v:
            return
        self.known[e][k] = v
        self.streams[e].append(("w", k, v))

    def _deps(self, e, reads, writes, skip_self=False):
        for b in reads:
            for ev in b.w:
                if not (skip_self and ev[0][0] == e):
                    self._wait(e, ev)
        for b in writes:
            for ev in b.w:
                if not (skip_self and ev[0][0] == e):
                    self._wait(e, ev)
            for ev in b.r.values():
                if not (skip_self and ev[0][0] == e):
                    self._wait(e, ev)

    def op(self, e, fn, reads=(), writes=()):
        self._deps(e, reads, writes, skip_self=(e == "pe"))
        n = self.cnt[e]; self.cnt[e] += 1
        key = (e, n // EPOCH)
        ev = (key, n % EPOCH + 1)
        self.streams[e].append(("o", fn, key))
        for b in reads:
            b.r[e] = ev
        for b in writes:
            b.w = [ev]; b.r = {}
        return ev

    def dma(self, q, out_ap, in_ap, dsts, srcs, **kw):
        self._deps(q, srcs, dsts)
        owner = None
        for b in list(dsts) + list(srcs):
            if b.sb:
                owner = b; break
        if owner is None:
            owner = dsts[0]
        if owner.dsem is None:
            owner.dsem = {}; owner.dcnt = {}
        sw = (q == "pool")
        if sw not in owner.dsem:
            owner.dsem[sw] = ("d", owner.name, sw); owner.dcnt[sw] = 0
        owner.dcnt[sw] += 16
        ev = (owner.dsem[sw], owner.dcnt[sw])
        self.streams[q].append(("d", out_ap, in_ap, owner.dsem[sw], kw))
        for b in srcs:
            b.r["dma:" + owner.name] = ev
        for b in dsts:
            b.w = [ev]; b.r = {}
        return ev

    def emit(self, final_waits=()):
        nc = self.nc
        for ev in final_waits:
            self._wait("sp", ev)
        keys = []
        seen = set()
        for e, st in self.streams.items():
            for it in st:
                k = it[1] if it[0] == "w" else (it[2] if it[0] == "o" else it[3])
                if k not in seen:
                    seen.add(k); keys.append(k)
        sems = {k: self.es.enter_context(nc.semaphore("s%d" % i)) for i, k in enumerate(keys)}
        self.n_sems = len(sems)
        streams = self.streams
        with nc.Block() as block:
            def run(e):
                def f(eng):
                    for it in streams[e]:
                        if it[0] == "w":
                            eng.wait_ge(sems[it[1]], it[2])
                        elif it[0] == "o":
                            it[1](eng).then_inc(sems[it[2]], 1)
                        else:
                            eng.dma_start(out=it[1], in_=it[2], **it[4]).then_inc(sems[it[3]], 16)
                return f
            block.tensor(run("pe")); block.scalar(run("act")); block.vector(run("dve"))
            block.gpsimd(run("pool")); block.sync(run("sp"))
        self.es.close()


def layer_ranges(L):
    out = []
    for l in range(L):
        m = L - 1 - l
        o0, o1 = TOP - 4 * m, TOP + 64 + 3 * m
        k0, k1 = o0 - 4, o1 + 3
        out.append(((k0 // 2, (k1 + 1) // 2), (o0 // 2, (o1 + 1) // 2)))
    return out


def build_program(L=DEPTH, stop=None, bcut=99):
    nc = bass.Bass("TRN2", target_bir_lowering=False)
    P = Prog(nc)
    op, dma = P.op, P.dma

    x0 = P.dram("x0", [D, NTOK], F32, kind="ExternalInput")
    cond = P.dram("cond", [128, 16], F32, kind="ExternalInput")
    par = P.dram("par", [128, DEPTH * NPAR], F32, kind="ExternalInput")
    cst = P.dram("cst", [128, 512], F32, kind="ExternalInput")
    rm_in = P.dram("rm", [128, NPAIR * 14], F32, kind="ExternalInput")
    flg_in = P.dram("flg", [128, 2], F32, kind="ExternalInput")
    rbt = P.dram("rbt", [L, 128, 7 * NH * 128], F32, kind="ExternalInput")
    w_ada = P.dram("w_ada", [L, D, 3 * D], F32, kind="ExternalInput")
    w_in = P.dram("w_in", [L, D, D_IN], F32, kind="ExternalInput")
    w_oa = P.dram("w_out_a", [L, DB, D], F32, kind="ExternalInput")
    w_ob = P.dram("w_out_b", [L, DB, D], F32, kind="ExternalInput")
    w_oc = P.dram("w_out_c", [L, DB, D], F32, kind="ExternalInput")
    w_o = P.dram("w_o", [L, D, D], F32, kind="ExternalInput")
    y = P.dram("y", [D, 64 * GW], F32, kind="ExternalOutput")
    xs = [P.dram("xs0", [D, NTOK], F32), P.dram("xs1", [D, NTOK], F32)]
    wb_in = P.dram("wb_in", [L, D, D_IN], BF16)
    wb_o = P.dram("wb_o", [L, 2560, D], BF16)

    xtrk = {}

    def xt(bufid, st):
        k = (bufid, st)
        if k not in xtrk:
            xtrk[k] = P.track("xt_%s_%s" % k)
        return xtrk[k]

    cst_f = P.sbuf("cst_f", [128, 512], F32)
    ident = P.sbuf("ident", [128, 128], BF16)
    par_sb = P.sbuf("par_sb", [128, DEPTH * NPAR], F32)
    gk8 = P.sbuf("gk8", [128, DEPTH], F32)
    rm = P.sbuf("rm_sb", [128, NPAIR * 14], F32)
    flg = P.sbuf("flg_sb", [128, 2], F32)
    cnd = P.sbuf("cnd", [128, 16], F32)
    epsc = P.sbuf("epsc", [128, 4], F32)
    ada = P.sbuf("ada", [128, DEPTH, 24, 2], F32)
    gs = P.sbuf("gs", [128, DEPTH, 8, 2], F32)
    WR = [P.sbuf("WR%d" % i, [128, 8, 512], BF16) for i in range(3)]
    WO = P.sbuf("WO", [128, 20, D], BF16)
    RB = P.sbuf("RB", [128, 7, NH, 128], BF16)
    xa = P.sbuf("xa", [128, 8, 256], F32)
    xb = P.sbuf("xb", [128, 8, 256], F32)
    scr = P.sbuf("scr", [128, 4, 256], F32)
    wa = [xb, xa]
    tmpA = scr
    rstd = P.sbuf("rstd", [128, 256], F32)
    r2 = P.sbuf("r2", [128, 256], F32)
    hT = [P.sbuf("hT%d" % i, [128, 8, 256], BF16) for i in range(2)]
    hTc = hT[0]
    qT = [P.sbuf("qT%d" % i, [128, 4, 256], BF16) for i in range(2)]
    qTc = qT[0]
    kT = P.sbuf("kT", [128, 2, 4, 768], BF16)
    kTc = P.sbuf("kTc", [128, 2, 4, 256], BF16)
    v1 = P.sbuf("v1", [128, 6, NH, 65], BF16)
    v1c = P.sbuf("v1c", [128, 2, NH, 65], BF16)
    cg = P.sbuf("cg", [128, 4, 800], BF16)
    glu = P.sbuf("glu", [128, 4, 800], BF16)
    cgc = P.sbuf("cgc", [128, 4, 288], BF16)
    gluc = P.sbuf("gluc", [128, 4, 288], BF16)
    sq2 = [P.sbuf("sq2_%d" % i, [128, 256], F32) for i in range(2)]
    sAZ = P.sbuf("sAZ", [128, 2, 512], BF16)
    Sb = [P.sbuf("Sb%d" % i, [128, 512], F32) for i in range(2)]
    PT = P.sbuf("PT", [128, 8, 512], BF16)
    rec = P.sbuf("rec", [128, 4], F32)
    yat = P.sbuf("yat", [128, 512], BF16)
    yaT = P.sbuf("yaT", [128, 4, 256], BF16)
    ybT = P.sbuf("ybT", [128, 4, 256], BF16)
    ycT = P.sbuf("ycT", [128, 4, 256], BF16)
    mT = P.sbuf("mT", [128, 8, 256], BF16)
    sg = [P.sbuf("sg%d" % i, [128, 256], F32) for i in range(2)]
    acc = P.sbuf("acc", [128, 4, 256], F32)
    u = acc
    gts = P.sbuf("gts", [128, 12, 256], BF16)
    mt = [P.sbuf("mt%d" % i, [128, 256], F32) for i in range(2)]

    zp = [P.psum("zp%d" % i, [128, 512]) for i in range(3)]
    stp = P.psum("stp", [128, 512])
    sp_ = [P.psum("sps%d" % i, [128, 512]) for i in range(2)]
    obp = P.psum("obp", [128, 512])
    trp = P.psum("trp", [128, 1024], BF16)
    zpi = [0]

    def nzp():
        zpi[0] = (zpi[0] + 1) % 3
        return zp[zpi[0]]

    ones_f = lambda: cst_f[:, 128:256]
    blk_f = lambda: cst_f[:, 256:384]
    o512_f = lambda: cst_f[:, 384:512]

    def mm(out, lhsT, rhs, start, stop, rd, wr):
        op("pe", lambda e: e.matmul(out, lhsT=lhsT, rhs=rhs, start=start, stop=stop), rd, wr)

    dma("sp", cst_f[:, :], cst[:, :], [cst_f], [cst])
    dma("sp", par_sb[:, :], par[:, :], [par_sb], [par])
    dma("sp", rm[:, :], rm_in[:, :], [rm], [rm_in])
    dma("sp", flg[:, :], flg_in[:, :], [flg], [flg_in])
    dma("sp", cnd[:, :], cond[:, :], [cnd], [cond])
    op("dve", lambda e: e.tensor_copy(out=ident[:, :], in_=cst_f[:, 0:128]), [cst_f], [ident])
    for i_, v_ in enumerate((D * EPS, HD * EPS, EPS)):
        op("dve", lambda e, i_=i_, v_=v_: e.memset(epsc[:, i_:i_ + 1], v_), [], [epsc])
    for b_ in (v1, v1c):
        op("dve", lambda e, b_=b_: e.memset(b_[:, :, :, :], 0.0), [], [b_])
        op("dve", lambda e, b_=b_: e.memset(b_[:, :, :, 64:65], 1.0), [], [b_])
    for b_ in (kT, kTc):
        op("dve", lambda e, b_=b_: e.memset(b_[:, :, :, :], 0.0), [], [b_])
    for b_ in (qT[0], qT[1]):
        op("dve", lambda e, b_=b_: e.memset(b_[:, :, :], 0.0), [], [b_])
    op("dve", lambda e: e.memset(cgc[:, :, :], 0.0), [], [cgc])
    op("dve", lambda e: e.memset(gluc[:, :, :], 0.0), [], [gluc])
    op("dve", lambda e: e.memset(cg[:, :, :], 0.0), [], [cg])
    op("dve", lambda e: e.memset(glu[:, :, :], 0.0), [], [glu])
    for l in range(L):
        op("dve", lambda e, l=l: e.tensor_scalar(out=gk8[:, l:l + 1], in0=par_sb[:, l * NPAR + P_GK:l * NPAR + P_GK + 1],
                                                  scalar1=8.0, scalar2=None, op0=ALU.mult), [par_sb], [gk8])
    op("act", lambda e: e.activation(out=cnd[:, :], in_=cnd[:, :], func=AF.Silu), [cnd], [cnd])

    wbt_in = [P.track("wbin%d" % l) for l in range(L)]
    wbt_o = [P.track("wbo%d" % l) for l in range(L)]
    wsrc = P.track("wsrc")

    def cast_layer(l):
        for r in range(8):
            dma("pool", wb_in[l, r * 128:(r + 1) * 128, :], w_in[l, r * 128:(r + 1) * 128, :], [wbt_in[l]], [wsrc])
        for i, wsrc_t in enumerate((w_oa, w_ob, w_oc)):
            for r in range(2):
                dma("pool", wb_o[l, i * 512 + r * 256:i * 512 + (r + 1) * 256, :], wsrc_t[l, r * 256:(r + 1) * 256, :],
                    [wbt_o[l]], [wsrc])
        for r in range(4):
            dma("pool", wb_o[l, 1536 + r * 256:1536 + (r + 1) * 256, :], w_o[l, r * 256:(r + 1) * 256, :], [wbt_o[l]], [wsrc])

    for l in range(L):
        cast_layer(l)

    wai = 0
    for l in range(L):
        for g in range(12):
            wbuf = wa[wai % 2]; wai += 1
            dma("sp", wbuf[:, :, :], w_ada[l, :, g * 256:(g + 1) * 256].rearrange("(k p) n -> p k n", p=128), [wbuf], [wsrc])
            for cc in range(2):
                ch = g * 2 + cc
                for kc in range(8):
                    mm(stp[:, ch * 2:ch * 2 + 2], wbuf[:, kc, cc * 128:(cc + 1) * 128], cnd[:, kc * 2:kc * 2 + 2],
                       kc == 0, kc == 7, [wbuf, cnd], [stp])
        for j in range(2):
            op("dve", lambda e, l=l, j=j: e.tensor_tensor(
                out=ada[:, l, :, j], in0=stp[:, j:48:2], in1=par_sb[:, l * NPAR + P_BADA:l * NPAR + P_BADA + 24], op=ALU.add),
               [stp, par_sb], [ada])
            op("dve", lambda e, l=l, j=j: e.scalar_tensor_tensor(
                out=gs[:, l, :, j], in0=ada[:, l, 8:16, j], scalar=1.0, in1=par_sb[:, l * NPAR + P_NG:l * NPAR + P_NG + 8],
                op0=ALU.add, op1=ALU.mult), [ada, par_sb], [gs])
            op("dve", lambda e, l=l, j=j: e.tensor_scalar(out=gs[:, l, :, j], in0=gs[:, l, :, j], scalar1=32.0, scalar2=None,
                                                           op0=ALU.mult), [gs], [gs])

    class WStream:
        def __init__(self):
            self.plan = []
            self.issued = 0
            self.used = 0

        def issue_upto(self, n):
            while self.issued < min(n, len(self.plan)):
                l, col0 = self.plan[self.issued]
                buf = WR[self.issued % 3]
                dma("sp", buf[:, :, :], wb_in[l, :, col0:col0 + 512].rearrange("(k p) n -> p k n", p=128), [buf], [wbt_in[l]])
                self.issued += 1

        def next(self):
            i = self.used
            self.issue_upto(i + 3)
            self.used += 1
            return WR[i % 3]

    ws = WStream()

    A_GROUPS = [B_C, B_V, C_A, C_G, A_V, A_K, A_Q]
    B_GROUPS = [A_Z, B_B, B_Z, C_Z] + [GATES + i * 512 for i in range(6)]

    def phase_a(l, j, p0, p1, ctx):
        cj = 1 if ctx else 0
        if ctx:
            t0, T, tl = STOK, CTX, 0
            h = hTc
        else:
            t0, T, tl = p0 * 128, (p1 - p0) * 128, (p0 % 2) * 128
            h = hT[j % 2]
        src = x0 if l == 0 else xs[l % 2]
        strk = xt("in" if l == 0 else l % 2, "c" if ctx else j)
        po = l * NPAR
        dma("sp", xa[:, :, 0:T], src[:, t0:t0 + T].rearrange("(c p) t -> p c t", p=128), [xa], [strk])
        for c in range(8):
            s2 = sq2[c % 2]
            op("act", lambda e, c=c, s2=s2: e.activation(out=s2[:, 0:T], in_=xa[:, c, 0:T], func=AF.Square), [xa], [s2])
            mm(stp[:, 0:T], ones_f(), s2[:, 0:T], c == 0, c == 7, [s2, cst_f], [stp])
        op("act", lambda e: e.activation(out=rstd[:, 0:T], in_=stp[:, 0:T], func=AF.Sqrt, bias=epsc[:, 0:1], scale=1.0), [stp, epsc], [rstd])
        op("dve", lambda e: e.reciprocal(out=rstd[:, 0:T], in_=rstd[:, 0:T]), [rstd], [rstd])
        for c in range(8):
            op("dve", lambda e, c=c: e.scalar_tensor_tensor(out=xa[:, c, 0:T], in0=xa[:, c, 0:T], scalar=gs[:, l, c, cj:cj + 1],
                                                            in1=rstd[:, 0:T], op0=ALU.mult, op1=ALU.mult), [xa, rstd, gs], [xa])
            op("act", lambda e, c=c: e.activation(out=h[:, c, tl:tl + T], in_=xa[:, c, 0:T], func=AF.Identity,
                                                   bias=ada[:, l, c, cj:cj + 1], scale=1.0), [xa, ada], [h])
        if ctx:
            qd, kd, cgd, gld = qTc, kTc, cgc, gluc
            qo, ko, co = 0, 0, 16
        else:
            qd, kd, cgd, gld = qT[j % 2], kT, cg, glu
            qo, ko, co = tl, (j % 3) * 256 + tl, 16 + (j % 3) * 256 + tl
        for gcol in A_GROUPS:
            wg = ws.next()
            if gcol == A_V:
                for pl in range(T // 128):
                    z = nzp()
                    for kc in range(8):
                        mm(z[:, :], h[:, kc, tl + pl * 128:tl + (pl + 1) * 128], wg[:, kc, :], kc == 0, kc == 7, [h, wg], [z])
                    if ctx:
                        vdst, vb = v1c[:, pl, :, 0:64], v1c
                    else:
                        vdst, vb = v1[:, (p0 + pl) % 6, :, 0:64], v1
                    op("act", lambda e, z=z, vdst=vdst: e.activation(out=vdst, in_=z[:, :].rearrange("p (h d) -> p h d", d=64),
                                                                      func=AF.Copy), [z], [vb])
                continue
            for cc in range(4):
                z = nzp()
                for kc in range(8):
                    mm(z[:, 0:T], wg[:, kc, cc * 128:(cc + 1) * 128], h[:, kc, tl:tl + T], kc == 0, kc == 7, [h, wg], [z])
                if gcol in (A_Q, A_K):
                    s2 = sq2[cc % 2]
                    op("act", lambda e, z=z, s2=s2: e.activation(out=s2[:, 0:T], in_=z[:, 0:T], func=AF.Square), [z], [s2])
                    mm(stp[:, 0:T], blk_f(), s2[:, 0:T], True, True, [s2, cst_f], [stp])
                    op("act", lambda e: e.activation(out=r2[:, 0:T], in_=stp[:, 0:T], func=AF.Sqrt, bias=epsc[:, 1:2], scale=1.0), [stp, epsc], [r2])
                    op("dve", lambda e: e.reciprocal(out=r2[:, 0:T], in_=r2[:, 0:T]), [r2], [r2])
                    if gcol == A_Q:
                        op("dve", lambda e, z=z, cc=cc: e.scalar_tensor_tensor(
                            out=qd[:, cc, qo:qo + T], in0=z[:, 0:T], scalar=par_sb[:, po + P_GQ:po + P_GQ + 1], in1=r2[:, 0:T],
                            op0=ALU.mult, op1=ALU.mult), [z, r2, par_sb], [qd])
                    else:
                        for hh in range(2):
                            ps_ = slice(hh * 64, (hh + 1) * 64)
                            op("dve", lambda e, z=z, cc=cc, hh=hh, ps_=ps_: e.scalar_tensor_tensor(
                                out=kd[ps_, hh, cc, ko:ko + T], in0=z[ps_, 0:T], scalar=gk8[ps_, l:l + 1], in1=r2[ps_, 0:T],
                                op0=ALU.mult, op1=ALU.mult), [z, r2, gk8], [kd])
                elif gcol in (B_C, C_A):
                    op("act", lambda e, z=z, cc=cc: e.activation(out=tmpA[:, cc, 0:T], in_=z[:, 0:T], func=AF.Copy), [z], [tmpA])
                elif gcol == B_V:
                    op("dve", lambda e, z=z, cc=cc: e.tensor_tensor(out=cgd[:, cc, co:co + T], in0=z[:, 0:T], in1=tmpA[:, cc, 0:T],
                                                                     op=ALU.mult), [z, tmpA], [cgd])
                elif gcol == C_G:
                    s2 = sq2[cc % 2]
                    op("act", lambda e, z=z, s2=s2: e.activation(out=s2[:, 0:T], in_=z[:, 0:T], func=AF.Sigmoid), [z], [s2])
                    op("dve", lambda e, s2=s2, cc=cc: e.tensor_tensor(out=gld[:, cc, co:co + T], in0=s2[:, 0:T], in1=tmpA[:, cc, 0:T],
                                                                       op=ALU.mult), [s2, tmpA], [gld])
        if not ctx:
            for fj, fi in ((TOP // 4 - 1, 0), ((TOP + 64) // 4, 1)):
                if j == fj:
                    for b_ in (cg, glu):
                        op("pool", lambda e, b_=b_, fi=fi: e.tensor_scalar(
                            out=b_[:, :, co:co + T], in0=b_[:, :, co:co + T], scalar1=flg[:, fi:fi + 1], scalar2=None, op0=ALU.mult),
                           [b_, flg], [b_])
            if j % 3 == 2:
                for b_ in (cg, glu):
                    op("pool", lambda e, b_=b_: e.tensor_copy(out=b_[:, :, 0:16], in_=b_[:, :, 768:784]), [b_], [b_])
            if j % 3 == 0:
                for b_ in (cg, glu):
                    op("pool", lambda e, b_=b_: e.tensor_copy(out=b_[:, :, 784:800], in_=b_[:, :, 16:32]), [b_], [b_])

    def phase_b(l, j, p0, p1, ctx, last):
        cj = 1 if ctx else 0
        po = l * NPAR
        if ctx:
            t0, T, tl = STOK, CTX, 0
            h, qd, cgd, gld, co = hTc, qTc, cgc, gluc, 16
            npl = 2
        else:
            t0, T, tl = p0 * 128, (p1 - p0) * 128, (p0 % 2) * 128
            h, qd, cgd, gld, co = hT[j % 2], qT[j % 2], cg, glu, 16 + (j % 3) * 256 + tl
            npl = p1 - p0
        src = x0 if l == 0 else xs[l % 2]
        strk = xt("in" if l == 0 else l % 2, "c" if ctx else j)
        dma("sp", xb[:, :, 0:T], src[:, t0:t0 + T].rearrange("(c p) t -> p c t", p=128), [xb], [strk])

        for gcol in B_GROUPS[:4]:
            wg = ws.next()
            if gcol == A_Z:
                for pl in range(npl):
                    z = nzp()
                    for kc in range(8):
                        mm(z[:, :], h[:, kc, tl + pl * 128:tl + (pl + 1) * 128], wg[:, kc, :], kc == 0, kc == 7, [h, wg], [z])
                    op("act", lambda e, z=z, pl=pl: e.activation(out=sAZ[:, pl, :], in_=z[:, :], func=AF.Silu), [z], [sAZ])
                if bcut <= 1:
                    return
                for pl in range(npl):
                    pi = p0 + pl
                    if ctx:
                        chunks = [("c", 0), ("c", 1)]
                    else:
                        offs = [-2, -1, 0, 1, 2]
                        if pi == TOP // 2:
                            offs.append(3)
                        if pi == (TOP + 64) // 2 - 1:
                            offs.insert(0, -3)
                        chunks = [("l", o) for o in offs] + [("c", 0), ("c", 1)]
                    qs = tl + pl * 128
                    for half in range(2):
                        for ci, (kind, o) in enumerate(chunks):
                            s_ps = sp_[ci % 2]
                            for hh in range(4):
                                hd = half * 4 + hh
                                cc = hd // 2
                                if kind == "c":
                                    kap, kb = kTc[:, hd % 2, cc, o * 128:(o + 1) * 128], kTc
                                else:
                                    ks = ((pi + o) % 6) * 128
                                    kap, kb = kT[:, hd % 2, cc, ks:ks + 128], kT
                                mm(s_ps[:, hh * 128:(hh + 1) * 128], kap, qd[:, cc, qs:qs + 128], True, True, [kb, qd], [s_ps])
                            if kind == "c":
                                op("act", lambda e, s_ps=s_ps, ci=ci: e.activation(out=PT[:, ci, :], in_=s_ps[:, :], func=AF.Exp),
                                   [s_ps], [PT])
                            else:
                                sb = Sb[ci % 2]
                                oi = o + 3
                                op("dve", lambda e, s_ps=s_ps, sb=sb, oi=oi, half=half: e.tensor_tensor(
                                    out=sb[:, :].rearrange("p (h q) -> p h q", q=128), in0=s_ps[:, :].rearrange("p (h q) -> p h q", q=128),
                                    in1=RB[:, oi, half * 4:half * 4 + 4, :], op=ALU.add), [s_ps, RB], [sb])
                                for q2 in range(2):
                                    ri = pi * 14 + oi * 2 + q2
                                    op("act", lambda e, sb=sb, ci=ci, q2=q2, ri=ri: e.activation(
                                        out=PT[:, ci, :].rearrange("p (h q) -> p h q", q=128)[:, :, q2 * 64:(q2 + 1) * 64],
                                        in_=sb[:, :].rearrange("p (h q) -> p h q", q=128)[:, :, q2 * 64:(q2 + 1) * 64],
                                        func=AF.Exp, bias=rm[:, ri:ri + 1], scale=1.0), [sb, rm], [PT])
                        nch = len(chunks)
                        if bcut <= 2:
                            return
                        for hh in range(4):
                            hd = half * 4 + hh
                            for ci, (kind, o) in enumerate(chunks):
                                if kind == "c":
                                    vap, vb_ = v1c[:, o, hd, :], v1c
                                else:
                                    vap, vb_ = v1[:, (pi + o) % 6, hd, :], v1
                                mm(obp[:, hh * 65:(hh + 1) * 65], PT[:, ci, hh * 128:(hh + 1) * 128], vap, ci == 0, ci == nch - 1,
                                   [PT, vb_], [obp])
                        op("dve", lambda e: e.reciprocal(out=rec[:, 0:4], in_=obp[:, 64:260:65]), [obp], [rec])
                        for hh in range(4):
                            hd = half * 4 + hh
                            op("dve", lambda e, hh=hh, hd=hd, pl=pl: e.scalar_tensor_tensor(
                                out=yat[:, hd * 64:(hd + 1) * 64], in0=obp[:, hh * 65:hh * 65 + 64], scalar=rec[:, hh:hh + 1],
                                in1=sAZ[:, pl, hd * 64:(hd + 1) * 64], op0=ALU.mult, op1=ALU.mult), [obp, rec, sAZ], [yat])
                    if bcut <= 3:
                        return
                    for c in range(4):
                        op("pe", lambda e, c=c: e.transpose(out=trp[:, c * 128:(c + 1) * 128], in_=yat[:, c * 128:(c + 1) * 128],
                                                            identity=ident[:, :]), [yat, ident], [trp])
                    op("act", lambda e, pl=pl: e.activation(out=yaT[:, :, pl * 128:(pl + 1) * 128],
                                                            in_=trp[:, 0:512].rearrange("p (c t) -> p c t", t=128), func=AF.Copy),
                       [trp], [yaT])
                if bcut <= 4:
                    return
                continue
            if gcol == C_Z and bcut <= 5:
                return
            if gcol == C_Z:
                conformer(l, T, gld, co)
            for cc in range(4):
                z = nzp()
                for kc in range(8):
                    mm(z[:, 0:T], wg[:, kc, cc * 128:(cc + 1) * 128], h[:, kc, tl:tl + T], kc == 0, kc == 7, [h, wg], [z])
                if gcol == B_B:
                    wofs = po + P_CSW + cc * 3
                    op("dve", lambda e, cc=cc, wofs=wofs: e.tensor_scalar(
                        out=acc[:, cc, 0:T], in0=cgd[:, cc, co - 1:co - 1 + T], scalar1=par_sb[:, wofs:wofs + 1], scalar2=None,
                        op0=ALU.mult), [cgd, par_sb], [acc])
                    for k in (1, 2):
                        op("dve", lambda e, cc=cc, wofs=wofs, k=k: e.scalar_tensor_tensor(
                            out=acc[:, cc, 0:T], in0=cgd[:, cc, co - 1 + k:co - 1 + k + T], scalar=par_sb[:, wofs + k:wofs + k + 1],
                            in1=acc[:, cc, 0:T], op0=ALU.mult, op1=ALU.add), [cgd, par_sb, acc], [acc])
                    op("dve", lambda e, cc=cc, z=z: e.tensor_tensor(out=acc[:, cc, 0:T], in0=z[:, 0:T], in1=acc[:, cc, 0:T], op=ALU.mult),
                       [z, acc], [acc])
                elif gcol == B_Z:
                    s_ = sg[cc % 2]
                    op("act", lambda e, z=z, s_=s_: e.activation(out=s_[:, 0:T], in_=z[:, 0:T], func=AF.Silu), [z], [s_])
                    op("dve", lambda e, cc=cc, s_=s_: e.tensor_tensor(out=ybT[:, cc, 0:T], in0=acc[:, cc, 0:T], in1=s_[:, 0:T], op=ALU.mult),
                       [acc, s_], [ybT])
                elif gcol == C_Z:
                    s_ = sg[cc % 2]
                    op("act", lambda e, z=z, s_=s_: e.activation(out=s_[:, 0:T], in_=z[:, 0:T], func=AF.Silu), [z], [s_])
                    op("dve", lambda e, cc=cc, s_=s_: e.tensor_tensor(out=ycT[:, cc, 0:T], in0=u[:, cc, 0:T], in1=s_[:, 0:T], op=ALU.mult),
                       [u, s_], [ycT])
        if bcut <= 6:
            return
        for F in range(2):
            for br in range(3):
                wg = ws.next()
                for fi in range(4):
                    z = nzp()
                    for kc in range(8):
                        mm(z[:, 0:T], wg[:, kc, fi * 128:(fi + 1) * 128], h[:, kc, tl:tl + T], kc == 0, kc == 7, [h, wg], [z])
                    op("act", lambda e, z=z, br=br, fi=fi: e.activation(out=gts[:, br * 4 + fi, 0:T], in_=z[:, 0:T], func=AF.Sigmoid),
                       [z], [gts])
            for fi in range(4):
                f = F * 4 + fi
                zs = []
                for br, ysrc in enumerate((yaT, ybT, ycT)):
                    z = nzp(); zs.append(z)
                    for kc in range(4):
                        mm(z[:, 0:T], WO[:, br * 4 + kc, f * 128:(f + 1) * 128], ysrc[:, kc, 0:T], kc == 0, kc == 3, [WO, ysrc], [z])
                m0, m1 = mt
                op("dve", lambda e, z=zs[0], fi=fi: e.tensor_tensor(out=m0[:, 0:T], in0=z[:, 0:T], in1=gts[:, fi, 0:T], op=ALU.mult),
                   [zs[0], gts], [m0])
                op("dve", lambda e, z=zs[1], fi=fi: e.tensor_tensor(out=m1[:, 0:T], in0=z[:, 0:T], in1=gts[:, 4 + fi, 0:T], op=ALU.mult),
                   [zs[1], gts], [m1])
                op("pool", lambda e: e.tensor_tensor(out=m0[:, 0:T], in0=m0[:, 0:T], in1=m1[:, 0:T], op=ALU.add), [m0, m1], [m0])
                op("dve", lambda e, z=zs[2], fi=fi: e.tensor_tensor(out=m1[:, 0:T], in0=z[:, 0:T], in1=gts[:, 8 + fi, 0:T], op=ALU.mult),
                   [zs[2], gts], [m1])
                op("pool", lambda e, f=f: e.tensor_tensor(out=mT[:, f, 0:T], in0=m0[:, 0:T], in1=m1[:, 0:T], op=ALU.add), [m0, m1], [mT])
        if bcut <= 7:
            return
        for f2 in range(8):
            z = nzp()
            for f in range(8):
                mm(z[:, 0:T], WO[:, 12 + f, f2 * 128:(f2 + 1) * 128], mT[:, f, 0:T], f == 0, f == 7, [WO, mT], [z])
            op("dve", lambda e, z=z, f2=f2: e.scalar_tensor_tensor(
                out=xb[:, f2, 0:T], in0=z[:, 0:T], scalar=ada[:, l, 16 + f2, cj:cj + 1], in1=xb[:, f2, 0:T], op0=ALU.mult, op1=ALU.add),
               [z, ada, xb], [xb])
        if last:
            lo = max(p0, TOP // 2); hi = min(p1, (TOP + 64) // 2)
            if hi > lo:
                a0 = (lo - p0) * 128; n = (hi - lo) * 128
                yo = (lo - TOP // 2) * 128
                ev = dma("pool", y[:, yo:yo + n].rearrange("(c p) t -> p c t", p=128), xb[:, :, a0:a0 + n], [ytrk], [xb])
                yevs.append(ev)
        else:
            dtrk = xt((l + 1) % 2, "c" if ctx else j)
            dma("pool", xs[(l + 1) % 2][:, t0:t0 + T].rearrange("(c p) t -> p c t", p=128), xb[:, :, 0:T], [dtrk], [xb])

    ytrk = P.track("ytrk")
    yevs = []

    def conformer(l, T, gld, co):
        po = l * NPAR
        for k in range(31):
            for cc in range(4):
                wofs = po + P_CCW + cc * 31 + k
                src_ap = gld[:, cc, co - 15 + k:co - 15 + k + T]
                if k == 0:
                    op("dve", lambda e, cc=cc, wofs=wofs, src_ap=src_ap: e.tensor_scalar(
                        out=u[:, cc, 0:T], in0=src_ap, scalar1=par_sb[:, wofs:wofs + 1],
                        scalar2=par_sb[:, po + P_CCB + cc:po + P_CCB + cc + 1], op0=ALU.mult, op1=ALU.add), [gld, par_sb], [u])
                else:
                    op("dve", lambda e, cc=cc, wofs=wofs, src_ap=src_ap: e.scalar_tensor_tensor(
                        out=u[:, cc, 0:T], in0=src_ap, scalar=par_sb[:, wofs:wofs + 1], in1=u[:, cc, 0:T],
                        op0=ALU.mult, op1=ALU.add), [gld, par_sb, u], [u])
        for cc in range(4):
            mm(stp[:, 0:T], o512_f(), u[:, cc, 0:T], cc == 0, cc == 3, [u, cst_f], [stp])
        for cc in range(4):
            op("dve", lambda e, cc=cc: e.tensor_tensor(out=u[:, cc, 0:T], in0=u[:, cc, 0:T], in1=stp[:, 0:T], op=ALU.subtract),
               [u, stp], [u])
        op("act", lambda e: e.activation(out=scr[:, 0:4, 0:T], in_=u[:, :, 0:T], func=AF.Square), [u], [scr])
        for cc in range(4):
            mm(stp[:, 0:T], o512_f(), scr[:, cc, 0:T], cc == 0, cc == 3, [scr, cst_f], [stp])
        op("act", lambda e: e.activation(out=r2[:, 0:T], in_=stp[:, 0:T], func=AF.Sqrt, bias=epsc[:, 2:3], scale=1.0), [stp, epsc], [r2])
        op("dve", lambda e: e.reciprocal(out=r2[:, 0:T], in_=r2[:, 0:T]), [r2], [r2])
        for cc in range(4):
            op("dve", lambda e, cc=cc: e.tensor_tensor(out=u[:, cc, 0:T], in0=u[:, cc, 0:T], in1=r2[:, 0:T], op=ALU.mult), [u, r2], [u])
            op("act", lambda e, cc=cc: e.activation(out=u[:, cc, 0:T], in_=u[:, cc, 0:T], func=AF.Silu,
                                                     bias=par_sb[:, po + P_LNB + cc:po + P_LNB + cc + 1],
                                                     scale=par_sb[:, po + P_LNG + cc:po + P_LNG + cc + 1]), [u, par_sb], [u])

    rng = layer_ranges(L)
    for l in range(L):
        (k0, k1), (o0, o1) = rng[l]
        last = (l == L - 1)
        seq = [("A", "c")]
        if not last:
            seq.append(("B", "c"))
        jsA = list(range(k0 // 2, (k1 + 1) // 2))
        jsB = list(range(o0 // 2, (o1 + 1) // 2))
        seq.append(("A", jsA[0]))
        for j in jsA[1:]:
            seq.append(("A", j))
            if j - 1 in jsB:
                seq.append(("B", j - 1))
        if jsA[-1] in jsB:
            seq.append(("B", jsA[-1]))
        for ph, j in seq:
            if ph == "A":
                ws.plan += [(l, c) for c in A_GROUPS]
            else:
                ws.plan += [(l, c) for c in B_GROUPS[:4]]
                for F in range(2):
                    for br in range(3):
                        ws.plan.append((l, GATES + br * 1024 + F * 512))
        rng[l] = (rng[l][0], rng[l][1], seq)

    nph = [0]
    for l in range(L):
        (k0, k1), (o0, o1), seq = rng[l]
        last = (l == L - 1)
        dma("sp", WO[:, :, :], wb_o[l, :, :].rearrange("(k p) n -> p k n", p=128), [WO], [wbt_o[l]])
        dma("pool", RB[:, :, :, :], rbt[l, :, :].rearrange("p (o h q) -> p o h q", o=7, h=NH), [RB], [wsrc])
        for ph, j in seq:
            if stop is not None and nph[0] >= stop:
                break
            nph[0] += 1
            ctx = (j == "c")
            if ctx:
                p0, p1 = 0, 2
            elif ph == "A":
                p0, p1 = max(2 * j, k0), min(2 * j + 2, k1)
            else:
                p0, p1 = max(2 * j, o0), min(2 * j + 2, o1)
            if ph == "A":
                phase_a(l, j, p0, p1, ctx)
            else:
                phase_b(l, j, p0, p1, ctx, last)

    if not yevs:
        yevs.append(dma("pool", y[:, 0:256].rearrange("(c p) t -> p c t", p=128), xb[:, :, 0:256], [ytrk], [xb]))
    P.emit(yevs)
    return nc, P


def _consts():
    c = np.zeros((128, 512), np.float32)
    c[:, 0:128] = np.eye(128, dtype=np.float32)
    c[:, 128:256] = 1.0
    c[0:64, 256:320] = 1.0
    c[64:128, 320:384] = 1.0
    c[:, 384:512] = 1.0 / 512.0
    return c


def _rb_tables(rpb):
    kc = np.arange(GW)[:, None]
    qc = np.arange(GW)[None, :]
    cs = np.clip(qc - 8, 0, GW - 16)
    colv = (kc >= cs) & (kc < cs + 16)
    dcol = np.clip(kc - qc, -15, 15) + 15
    out = np.full((DEPTH, 2, GW, 7, NH, 2, GW), NEG, np.float32)
    for oi, o in enumerate(OFFS):
        for kr2 in range(2):
            for qr2 in range(2):
                d = 2 * o + kr2 - qr2 + 7
                if 0 <= d <= 14:
                    blk = rpb[:, :, d, :][:, :, dcol]
                    blk = np.where(colv[None, None], blk, np.float32(NEG))
                    out[:, kr2, :, oi, :, qr2, :] = np.transpose(blk, (0, 2, 1, 3))
    return np.ascontiguousarray(out.reshape(DEPTH, 128, 7 * NH * 128))


def _row_masks(a):
    m = np.full((2, NPAIR, 7, 2), NEG, np.float32)
    for pi in range(NPAIR):
        for q2 in range(2):
            qg = a - TOP + 2 * pi + q2
            if qg < 0 or qg >= ROWS:
                continue
            rs = min(max(qg - 4, 0), ROWS - 8)
            for oi, o in enumerate(OFFS):
                for k2 in range(2):
                    kg = a - TOP + 2 * (pi + o) + k2
                    if rs <= kg < rs + 8:
                        m[k2, pi, oi, q2] = 0.0
    return np.ascontiguousarray(np.repeat(m, 64, axis=0).reshape(128, NPAIR * 14))


def _params(inp):
    p = np.zeros((128, DEPTH, NPAR), np.float32)
    for l in range(DEPTH):
        p[:, l, P_NG:P_NG + 8] = inp["norm_g"][l].reshape(8, 128).T
        p[:, l, P_BADA:P_BADA + 24] = inp["b_ada"][l].reshape(24, 128).T
        p[:, l, P_GQ] = np.tile(inp["q_norm_g"][l], 2)
        p[:, l, P_GK] = np.tile(inp["k_norm_g"][l], 2)
        p[:, l, P_CSW:P_CSW + 12] = inp["conv_short_w"][l].reshape(3, 4, 128).transpose(2, 1, 0).reshape(128, 12)
        p[:, l, P_CCW:P_CCW + 124] = inp["conv_conf_w"][l].reshape(31, 4, 128).transpose(2, 1, 0).reshape(128, 124)
        p[:, l, P_CCB:P_CCB + 4] = inp["conv_conf_b"][l].reshape(4, 128).T
        p[:, l, P_LNG:P_LNG + 4] = inp["ln_conf_g"][l].reshape(4, 128).T
        p[:, l, P_LNB:P_LNB + 4] = inp["ln_conf_b"][l].reshape(4, 128).T
    return np.ascontiguousarray(p.reshape(128, DEPTH * NPAR))


_CACHE = {}


def make_in_maps(inp, L=DEPTH):
    inp = {k: np.asarray(v, dtype=np.float32) for k, v in inp.items()}
    x = inp["x"].reshape(2, ROWS, GW, D)
    shared = dict(par=_params(inp), cst=_consts(), rbt=np.ascontiguousarray(_rb_tables(inp["rpb"])[:L]),
                  w_ada=np.ascontiguousarray(inp["w_ada"][:L]), w_in=np.ascontiguousarray(inp["w_in"][:L]),
                  w_out_a=np.ascontiguousarray(inp["w_out_a"][:L]), w_out_b=np.ascontiguousarray(inp["w_out_b"][:L]),
                  w_out_c=np.ascontiguousarray(inp["w_out_c"][:L]), w_o=np.ascontiguousarray(inp["w_o"][:L]))
    maps = []
    for c in range(8):
        b, a = c // 4, 64 * (c % 4)
        slab = np.zeros((SROWS, GW, D), np.float32)
        g0, g1 = max(a - TOP, 0), min(a - TOP + SROWS, ROWS)
        slab[g0 - (a - TOP):g1 - (a - TOP)] = x[b, g0:g1]
        x0 = np.empty((D, NTOK), np.float32)
        x0[:, :STOK] = slab.reshape(STOK, D).T
        x0[:, STOK:] = inp["ctx"][b].T
        cond = np.stack([inp["c"][b].reshape(8, 128).T, inp["c_ctx"].reshape(8, 128).T], axis=-1).reshape(128, 16)
        flg = np.zeros((128, 2), np.float32)
        flg[:, 0] = 0.0 if a == 0 else 1.0
        flg[:, 1] = 0.0 if a + 64 == ROWS else 1.0
        m = dict(shared)
        m.update(x0=x0, cond=np.ascontiguousarray(cond), rm=_row_masks(a), flg=flg)
        maps.append(m)
    return maps


def assemble(results):
    out = np.empty((2, ROWS, GW, D), np.float32)
    for c in range(8):
        b, a = c // 4, 64 * (c % 4)
        out[b, a:a + 64] = results[c]["y"].T.reshape(64, GW, D)
    return out.reshape(2, ROWS * GW, D)


def kernel(**inputs):
    if "nc" not in _CACHE:
        _CACHE["nc"] = build_program(DEPTH)[0]
    nc = _CACHE["nc"]
    maps = make_in_maps(inputs)
    res = run_bass_kernel_spmd(nc, maps, core_ids=list(range(8)))
    return assemble(res.results)
```

```python
import numpy as np
from contextlib import ExitStack
import concourse.bass as bass
import concourse.mybir as mybir
from concourse.bass_utils import run_bass_kernel_spmd

F32 = mybir.dt.float32
BF16 = mybir.dt.bfloat16
AF = mybir.ActivationFunctionType
ALU = mybir.AluOpType
EPOCH = 30000

D = 1024
DB = 512
NH = 8
HD = 64
GW = 64
ROWS = 256
CTX = 256
DEPTH = 4
D_IN = 8704
A_Q, A_K, A_V, A_Z, B_B, B_C, B_V, B_Z, C_A, C_G, C_Z, GATES = [i * 512 for i in range(12)]
EPS = 1e-6
NEG = -30000.0
SROWS = 96
NPAIR = SROWS // 2
NST = NPAIR // 2
STOK = SROWS * GW
NTOK = STOK + CTX
TOP = 16
OFFS = [-3, -2, -1, 0, 1, 2, 3]
P_NG, P_BADA, P_GQ, P_GK, P_CSW, P_CCW, P_CCB, P_LNG, P_LNB = 0, 8, 32, 33, 34, 46, 170, 174, 178
NPAR = 182


class Buf:
    __slots__ = ("name", "t", "w", "r", "dsem", "dcnt", "sb")

    def __init__(self, name, t, sb):
        self.name = name; self.t = t; self.w = []; self.r = {}; self.dsem = None; self.dcnt = 0; self.sb = sb

    def __getitem__(self, k):
        return self.t[k]


class Prog:
    def __init__(self, nc):
        self.nc = nc
        self.es = ExitStack()
        self.streams = {k: [] for k in ("pe", "act", "dve", "pool", "sp")}
        self.cnt = {k: 0 for k in self.streams}
        self.known = {k: {} for k in self.streams}

    def sbuf(self, name, shape, dtype):
        return Buf(name, self.es.enter_context(self.nc.sbuf_tensor(name, list(shape), dtype)), True)

    def psum(self, name, shape, dtype=F32):
        return Buf(name, self.es.enter_context(self.nc.psum_tensor(name, list(shape), dtype)), True)

    def dram(self, name, shape, dtype, kind="Internal"):
        return Buf(name, self.nc.dram_tensor(name, list(shape), dtype, kind=kind), False)

    def track(self, name):
        return Buf(name, None, False)

    def _wait(self, e, ev):
        k, v = ev
        if self.known[e].get(k, 0) >= v:
            return
        self.known[e][k] = v
        self.streams[e].append(("w", k, v))

    def _deps(self, e, reads, writes, skip_self=False):
        for b in reads:
            for ev in b.w:
                if not (skip_self and ev[0][0] == e):
                    self._wait(e, ev)
        for b in writes:
            for ev in b.w:
                if not (skip_self and ev[0][0] == e):
                    self._wait(e, ev)
            for ev in b.r.values():
                if not (skip_self and ev[0][0] == e):
                    self._wait(e, ev)

    def op(self, e, fn, reads=(), writes=()):
        self._deps(e, reads, writes, skip_self=(e == "pe"))
        n = self.cnt[e]; self.cnt[e] += 1
        key = (e, n // EPOCH)
        ev = (key, n % EPOCH + 1)
        self.streams[e].append(("o", fn, key))
        for b in reads:
            b.r[e] = ev
        for b in writes:
            b.w = [ev]; b.r = {}
        return ev

    def dma(self, q, out_ap, in_ap, dsts, srcs, **kw):
        self._deps(q, srcs, dsts)
        owner = None
        for b in list(dsts) + list(srcs):
            if b.sb:
                owner = b; break
        if owner is None:
            owner = dsts[0]
        if owner.dsem is None:
            owner.dsem = {}; owner.dcnt = {}
        sw = (q == "pool")
        if sw not in owner.dsem:
            owner.dsem[sw] = ("d", owner.name, sw); owner.dcnt[sw] = 0
        owner.dcnt[sw] += 16
        ev = (owner.dsem[sw], owner.dcnt[sw])
        self.streams[q].append(("d", out_ap, in_ap, owner.dsem[sw], kw))
        for b in srcs:
            b.r["dma:" + owner.name] = ev
        for b in dsts:
            b.w = [ev]; b.r = {}
        return ev

    def emit(self, final_waits=()):
        nc = self.nc
        for ev in final_waits:
            self._wait("sp", ev)
        keys = []
        seen = set()
        for e, st in self.streams.items():
            for it in st:
                k = it[1] if it[0] == "w" else (it[2] if it[0] == "o" else it[3])
                if k not in seen:
                    seen.add(k); keys.append(k)
        sems = {k: self.es.enter_context(nc.semaphore("s%d" % i)) for i, k in enumerate(keys)}
        self.n_sems = len(sems)
        streams = self.streams
        with nc.Block() as block:
            def run(e):
                def f(eng):
                    for it in streams[e]:
                        if it[0] == "w":
                            eng.wait_ge(sems[it[1]], it[2])
                        elif it[0] == "o":
                            it[1](eng).then_inc(sems[it[2]], 1)
                        else:
                            eng.dma_start(out=it[1], in_=it[2], **it[4]).then_inc(sems[it[3]], 16)
                return f
            block.tensor(run("pe")); block.scalar(run("act")); block.vector(run("dve"))
            block.gpsimd(run("pool")); block.sync(run("sp"))
        self.es.close()


def layer_ranges(L):
    out = []
    for l in range(L):
        m = L - 1 - l
        o0, o1 = TOP - 4 * m, TOP + 64 + 3 * m
        k0, k1 = o0 - 4, o1 + 3
        out.append(((k0 // 2, (k1 + 1) // 2), (o0 // 2, (o1 + 1) // 2)))
    return out


def build_program(L=DEPTH, stop=None, bcut=99):
    nc = bass.Bass("TRN2", target_bir_lowering=False)
    P = Prog(nc)
    op, dma = P.op, P.dma

    x0 = P.dram("x0", [D, NTOK], F32, kind="ExternalInput")
    cond = P.dram("cond", [128, 16], F32, kind="ExternalInput")
    par = P.dram("par", [128, DEPTH * NPAR], F32, kind="ExternalInput")
    cst = P.dram("cst", [128, 512], F32, kind="ExternalInput")
    rm_in = P.dram("rm", [128, NPAIR * 14], F32, kind="ExternalInput")
    flg_in = P.dram("flg", [128, 2], F32, kind="ExternalInput")
    rbt = P.dram("rbt", [L, 128, 7 * NH * 128], F32, kind="ExternalInput")
    w_ada = P.dram("w_ada", [L, D, 3 * D], F32, kind="ExternalInput")
    w_in = P.dram("w_in", [L, D, D_IN], F32, kind="ExternalInput")
    w_oa = P.dram("w_out_a", [L, DB, D], F32, kind="ExternalInput")
    w_ob = P.dram("w_out_b", [L, DB, D], F32, kind="ExternalInput")
    w_oc = P.dram("w_out_c", [L, DB, D], F32, kind="ExternalInput")
    w_o = P.dram("w_o", [L, D, D], F32, kind="ExternalInput")
    y = P.dram("y", [D, 64 * GW], F32, kind="ExternalOutput")
    xs = [P.dram("xs0", [D, NTOK], F32), P.dram("xs1", [D, NTOK], F32)]
    wb_in = P.dram("wb_in", [L, D, D_IN], BF16)
    wb_o = P.dram("wb_o", [L, 2560, D], BF16)

    xtrk = {}

    def xt(bufid, st):
        k = (bufid, st)
        if k not in xtrk:
            xtrk[k] = P.track("xt_%s_%s" % k)
        return xtrk[k]

    cst_f = P.sbuf("cst_f", [128, 512], F32)
    ident = P.sbuf("ident", [128, 128], BF16)
    par_sb = P.sbuf("par_sb", [128, DEPTH * NPAR], F32)
    gk8 = P.sbuf("gk8", [128, DEPTH], F32)
    rm = P.sbuf("rm_sb", [128, NPAIR * 14], F32)
    flg = P.sbuf("flg_sb", [128, 2], F32)
    cnd = P.sbuf("cnd", [128, 16], F32)
    epsc = P.sbuf("epsc", [128, 4], F32)
    ada = P.sbuf("ada", [128, DEPTH, 24, 2], F32)
    gs = P.sbuf("gs", [128, DEPTH, 8, 2], F32)
    WR = [P.sbuf("WR%d" % i, [128, 8, 512], BF16) for i in range(3)]
    WO = P.sbuf("WO", [128, 20, D], BF16)
    RB = P.sbuf("RB", [128, 7, NH, 128], BF16)
    xa = P.sbuf("xa", [128, 8, 256], F32)
    xb = P.sbuf("xb", [128, 8, 256], F32)
    scr = P.sbuf("scr", [128, 4, 256], F32)
    wa = [xb, xa]
    tmpA = scr
    rstd = P.sbuf("rstd", [128, 256], F32)
    r2 = P.sbuf("r2", [128, 256], F32)
    hT = [P.sbuf("hT%d" % i, [128, 8, 256], BF16) for i in range(2)]
    hTc = hT[0]
    qT = [P.sbuf("qT%d" % i, [128, 4, 256], BF16) for i in range(2)]
    qTc = qT[0]
    kT = P.sbuf("kT", [128, 2, 4, 768], BF16)
    kTc = P.sbuf("kTc", [128, 2, 4, 256], BF16)
    v1 = P.sbuf("v1", [128, 6, NH, 65], BF16)
    v1c = P.sbuf("v1c", [128, 2, NH, 65], BF16)
    cg = P.sbuf("cg", [128, 4, 800], BF16)
    glu = P.sbuf("glu", [128, 4, 800], BF16)
    cgc = P.sbuf("cgc", [128, 4, 288], BF16)
    gluc = P.sbuf("gluc", [128, 4, 288], BF16)
    sq2 = [P.sbuf("sq2_%d" % i, [128, 256], F32) for i in range(2)]
    sAZ = P.sbuf("sAZ", [128, 2, 512], BF16)
    Sb = [P.sbuf("Sb%d" % i, [128, 512], F32) for i in range(2)]
    PT = P.sbuf("PT", [128, 8, 512], BF16)
    rec = P.sbuf("rec", [128, 4], F32)
    yat = P.sbuf("yat", [128, 512], BF16)
    yaT = P.sbuf("yaT", [128, 4, 256], BF16)
    ybT = P.sbuf("ybT", [128, 4, 256], BF16)
    ycT = P.sbuf("ycT", [128, 4, 256], BF16)
    mT = P.sbuf("mT", [128, 8, 256], BF16)
    sg = [P.sbuf("sg%d" % i, [128, 256], F32) for i in range(2)]
    acc = P.sbuf("acc", [128, 4, 256], F32)
    u = P.sbuf("u", [128, 4, 256], F32)
    gts = P.sbuf("gts", [128, 12, 256], BF16)
    mt = [P.sbuf("mt%d" % i, [128, 256], F32) for i in range(2)]

    zp = [P.psum("zp%d" % i, [128, 512]) for i in range(3)]
    stp = P.psum("stp", [128, 512])
    sp_ = [P.psum("sps%d" % i, [128, 512]) for i in range(2)]
    obp = P.psum("obp", [128, 512])
    trp = P.psum("trp", [128, 1024], BF16)
    zpi = [0]

    def nzp():
        zpi[0] = (zpi[0] + 1) % 3
        return zp[zpi[0]]

    ones_f = lambda: cst_f[:, 128:256]
    blk_f = lambda: cst_f[:, 256:384]
    o512_f = lambda: cst_f[:, 384:512]

    def mm(out, lhsT, rhs, start, stop, rd, wr):
        op("pe", lambda e: e.matmul(out, lhsT=lhsT, rhs=rhs, start=start, stop=stop), rd, wr)

    dma("sp", cst_f[:, :], cst[:, :], [cst_f], [cst])
    dma("sp", par_sb[:, :], par[:, :], [par_sb], [par])
    dma("sp", rm[:, :], rm_in[:, :], [rm], [rm_in])
    dma("sp", flg[:, :], flg_in[:, :], [flg], [flg_in])
    dma("sp", cnd[:, :], cond[:, :], [cnd], [cond])
    op("dve", lambda e: e.tensor_copy(out=ident[:, :], in_=cst_f[:, 0:128]), [cst_f], [ident])
    for i_, v_ in enumerate((D * EPS, HD * EPS, EPS)):
        op("dve", lambda e, i_=i_, v_=v_: e.memset(epsc[:, i_:i_ + 1], v_), [], [epsc])
    for b_ in (v1, v1c):
        op("dve", lambda e, b_=b_: e.memset(b_[:, :, :, :], 0.0), [], [b_])
        op("dve", lambda e, b_=b_: e.memset(b_[:, :, :, 64:65], 1.0), [], [b_])
    for b_ in (kT, kTc):
        op("dve", lambda e, b_=b_: e.memset(b_[:, :, :, :], 0.0), [], [b_])
    for b_ in (qT[0], qT[1]):
        op("dve", lambda e, b_=b_: e.memset(b_[:, :, :], 0.0), [], [b_])
    op("dve", lambda e: e.memset(cgc[:, :, :], 0.0), [], [cgc])
    op("dve", lambda e: e.memset(gluc[:, :, :], 0.0), [], [gluc])
    op("dve", lambda e: e.memset(cg[:, :, :], 0.0), [], [cg])
    op("dve", lambda e: e.memset(glu[:, :, :], 0.0), [], [glu])
    for l in range(L):
        op("dve", lambda e, l=l: e.tensor_scalar(out=gk8[:, l:l + 1], in0=par_sb[:, l * NPAR + P_GK:l * NPAR + P_GK + 1],
                                                  scalar1=8.0, scalar2=None, op0=ALU.mult), [par_sb], [gk8])
    op("act", lambda e: e.activation(out=cnd[:, :], in_=cnd[:, :], func=AF.Silu), [cnd], [cnd])

    wbt_in = [P.track("wbin%d" % l) for l in range(L)]
    wbt_o = [P.track("wbo%d" % l) for l in range(L)]
    wsrc = P.track("wsrc")

    def cast_layer(l):
        for r in range(8):
            dma("pool", wb_in[l, r * 128:(r + 1) * 128, :], w_in[l, r * 128:(r + 1) * 128, :], [wbt_in[l]], [wsrc])
        for i, wsrc_t in enumerate((w_oa, w_ob, w_oc)):
            for r in range(2):
                dma("pool", wb_o[l, i * 512 + r * 256:i * 512 + (r + 1) * 256, :], wsrc_t[l, r * 256:(r + 1) * 256, :],
                    [wbt_o[l]], [wsrc])
        for r in range(4):
            dma("pool", wb_o[l, 1536 + r * 256:1536 + (r + 1) * 256, :], w_o[l, r * 256:(r + 1) * 256, :], [wbt_o[l]], [wsrc])

    for l in range(L):
        cast_layer(l)

    wai = 0
    for l in range(L):
        for g in range(12):
            wbuf = wa[wai % 2]; wai += 1
            dma("sp", wbuf[:, :, :], w_ada[l, :, g * 256:(g + 1) * 256].rearrange("(k p) n -> p k n", p=128), [wbuf], [wsrc])
            for cc in range(2):
                ch = g * 2 + cc
                for kc in range(8):
                    mm(stp[:, ch * 2:ch * 2 + 2], wbuf[:, kc, cc * 128:(cc + 1) * 128], cnd[:, kc * 2:kc * 2 + 2],
                       kc == 0, kc == 7, [wbuf, cnd], [stp])
        for j in range(2):
            op("dve", lambda e, l=l, j=j: e.tensor_tensor(
                out=ada[:, l, :, j], in0=stp[:, j:48:2], in1=par_sb[:, l * NPAR + P_BADA:l * NPAR + P_BADA + 24], op=ALU.add),
               [stp, par_sb], [ada])
            op("dve", lambda e, l=l, j=j: e.scalar_tensor_tensor(
                out=gs[:, l, :, j], in0=ada[:, l, 8:16, j], scalar=1.0, in1=par_sb[:, l * NPAR + P_NG:l * NPAR + P_NG + 8],
                op0=ALU.add, op1=ALU.mult), [ada, par_sb], [gs])
            op("dve", lambda e, l=l, j=j: e.tensor_scalar(out=gs[:, l, :, j], in0=gs[:, l, :, j], scalar1=32.0, scalar2=None,
                                                           op0=ALU.mult), [gs], [gs])

    class WStream:
        def __init__(self):
            self.plan = []
            self.issued = 0
            self.used = 0

        def issue_upto(self, n):
            while self.issued < min(n, len(self.plan)):
                l, col0 = self.plan[self.issued]
                buf = WR[self.issued % 3]
                dma("sp", buf[:, :, :], wb_in[l, :, col0:col0 + 512].rearrange("(k p) n -> p k n", p=128), [buf], [wbt_in[l]])
                self.issued += 1

        def next(self):
            i = self.used
            self.issue_upto(i + 3)
            self.used += 1
            return WR[i % 3]

    ws = WStream()

    A_GROUPS = [B_C, B_V, C_A, C_G, A_V, A_K, A_Q]
    B_GROUPS = [A_Z, B_B, B_Z, C_Z] + [GATES + i * 512 for i in range(6)]

    def phase_a(l, j, p0, p1, ctx):
        cj = 1 if ctx else 0
        if ctx:
            t0, T, tl = STOK, CTX, 0
            h = hTc
        else:
            t0, T, tl = p0 * 128, (p1 - p0) * 128, (p0 % 2) * 128
            h = hT[j % 2]
        src = x0 if l == 0 else xs[l % 2]
        strk = xt("in" if l == 0 else l % 2, "c" if ctx else j)
        po = l * NPAR
        dma("sp", xa[:, :, 0:T], src[:, t0:t0 + T].rearrange("(c p) t -> p c t", p=128), [xa], [strk])
        for c in range(8):
            s2 = sq2[c % 2]
            op("act", lambda e, c=c, s2=s2: e.activation(out=s2[:, 0:T], in_=xa[:, c, 0:T], func=AF.Square), [xa], [s2])
            mm(stp[:, 0:T], ones_f(), s2[:, 0:T], c == 0, c == 7, [s2, cst_f], [stp])
        op("act", lambda e: e.activation(out=rstd[:, 0:T], in_=stp[:, 0:T], func=AF.Sqrt, bias=epsc[:, 0:1], scale=1.0), [stp, epsc], [rstd])
        op("dve", lambda e: e.reciprocal(out=rstd[:, 0:T], in_=rstd[:, 0:T]), [rstd], [rstd])
        for c in range(8):
            op("dve", lambda e, c=c: e.scalar_tensor_tensor(out=xa[:, c, 0:T], in0=xa[:, c, 0:T], scalar=gs[:, l, c, cj:cj + 1],
                                                            in1=rstd[:, 0:T], op0=ALU.mult, op1=ALU.mult), [xa, rstd, gs], [xa])
            op("act", lambda e, c=c: e.activation(out=h[:, c, tl:tl + T], in_=xa[:, c, 0:T], func=AF.Identity,
                                                   bias=ada[:, l, c, cj:cj + 1], scale=1.0), [xa, ada], [h])
        if ctx:
            qd, kd, cgd, gld = qTc, kTc, cgc, gluc
            qo, ko, co = 0, 0, 16
        else:
            qd, kd, cgd, gld = qT[j % 2], kT, cg, glu
            qo, ko, co = tl, (j % 3) * 256 + tl, 16 + (j % 3) * 256 + tl
        for gcol in A_GROUPS:
            wg = ws.next()
            if gcol == A_V:
                for pl in range(T // 128):
                    z = nzp()
                    for kc in range(8):
                        mm(z[:, :], h[:, kc, tl + pl * 128:tl + (pl + 1) * 128], wg[:, kc, :], kc == 0, kc == 7, [h, wg], [z])
                    if ctx:
                        vdst, vb = v1c[:, pl, :, 0:64], v1c
                    else:
                        vdst, vb = v1[:, (p0 + pl) % 6, :, 0:64], v1
                    op("act", lambda e, z=z, vdst=vdst: e.activation(out=vdst, in_=z[:, :].rearrange("p (h d) -> p h d", d=64),
                                                                      func=AF.Copy), [z], [vb])
                continue
            for cc in range(4):
                z = nzp()
                for kc in range(8):
                    mm(z[:, 0:T], wg[:, kc, cc * 128:(cc + 1) * 128], h[:, kc, tl:tl + T], kc == 0, kc == 7, [h, wg], [z])
                if gcol in (A_Q, A_K):
                    s2 = sq2[cc % 2]
                    op("act", lambda e, z=z, s2=s2: e.activation(out=s2[:, 0:T], in_=z[:, 0:T], func=AF.Square), [z], [s2])
                    mm(stp[:, 0:T], blk_f(), s2[:, 0:T], True, True, [s2, cst_f], [stp])
                    op("act", lambda e: e.activation(out=r2[:, 0:T], in_=stp[:, 0:T], func=AF.Sqrt, bias=epsc[:, 1:2], scale=1.0), [stp, epsc], [r2])
                    op("dve", lambda e: e.reciprocal(out=r2[:, 0:T], in_=r2[:, 0:T]), [r2], [r2])
                    if gcol == A_Q:
                        op("dve", lambda e, z=z, cc=cc: e.scalar_tensor_tensor(
                            out=qd[:, cc, qo:qo + T], in0=z[:, 0:T], scalar=par_sb[:, po + P_GQ:po + P_GQ + 1], in1=r2[:, 0:T],
                            op0=ALU.mult, op1=ALU.mult), [z, r2, par_sb], [qd])
                    else:
                        for hh in range(2):
                            ps_ = slice(hh * 64, (hh + 1) * 64)
                            op("dve", lambda e, z=z, cc=cc, hh=hh, ps_=ps_: e.scalar_tensor_tensor(
                                out=kd[ps_, hh, cc, ko:ko + T], in0=z[ps_, 0:T], scalar=gk8[ps_, l:l + 1], in1=r2[ps_, 0:T],
                                op0=ALU.mult, op1=ALU.mult), [z, r2, gk8], [kd])
                elif gcol in (B_C, C_A):
                    op("act", lambda e, z=z, cc=cc: e.activation(out=tmpA[:, cc, 0:T], in_=z[:, 0:T], func=AF.Copy), [z], [tmpA])
                elif gcol == B_V:
                    op("dve", lambda e, z=z, cc=cc: e.tensor_tensor(out=cgd[:, cc, co:co + T], in0=z[:, 0:T], in1=tmpA[:, cc, 0:T],
                                                                     op=ALU.mult), [z, tmpA], [cgd])
                elif gcol == C_G:
                    s2 = sq2[cc % 2]
                    op("act", lambda e, z=z, s2=s2: e.activation(out=s2[:, 0:T], in_=z[:, 0:T], func=AF.Sigmoid), [z], [s2])
                    op("dve", lambda e, s2=s2, cc=cc: e.tensor_tensor(out=gld[:, cc, co:co + T], in0=s2[:, 0:T], in1=tmpA[:, cc, 0:T],
                                                                       op=ALU.mult), [s2, tmpA], [gld])
        if not ctx:
            for fj, fi in ((TOP // 4 - 1, 0), ((TOP + 64) // 4, 1)):
                if j == fj:
                    for b_ in (cg, glu):
                        op("pool", lambda e, b_=b_, fi=fi: e.tensor_scalar(
                            out=b_[:, :, co:co + T], in0=b_[:, :, co:co + T], scalar1=flg[:, fi:fi + 1], scalar2=None, op0=ALU.mult),
                           [b_, flg], [b_])
            if j % 3 == 2:
                for b_ in (cg, glu):
                    op("pool", lambda e, b_=b_: e.tensor_copy(out=b_[:, :, 0:16], in_=b_[:, :, 768:784]), [b_], [b_])
            if j % 3 == 0:
                for b_ in (cg, glu):
                    op("pool", lambda e, b_=b_: e.tensor_copy(out=b_[:, :, 784:800], in_=b_[:, :, 16:32]), [b_], [b_])

    def phase_b(l, j, p0, p1, ctx, last):
        cj = 1 if ctx else 0
        po = l * NPAR
        if ctx:
            t0, T, tl = STOK, CTX, 0
            h, qd, cgd, gld, co = hTc, qTc, cgc, gluc, 16
            npl = 2
        else:
            t0, T, tl = p0 * 128, (p1 - p0) * 128, (p0 % 2) * 128
            h, qd, cgd, gld, co = hT[j % 2], qT[j % 2], cg, glu, 16 + (j % 3) * 256 + tl
            npl = p1 - p0
        src = x0 if l == 0 else xs[l % 2]
        strk = xt("in" if l == 0 else l % 2, "c" if ctx else j)
        dma("sp", xb[:, :, 0:T], src[:, t0:t0 + T].rearrange("(c p) t -> p c t", p=128), [xb], [strk])

        conv_th = conf_conv_thunks(l, T, gld, co)
        n_units = 2 * npl
        per_unit = (len(conv_th) + n_units - 1) // n_units
        for gcol in B_GROUPS[:4]:
            wg = ws.next()
            if gcol == A_Z:
                for pl in range(npl):
                    z = nzp()
                    for kc in range(8):
                        mm(z[:, :], h[:, kc, tl + pl * 128:tl + (pl + 1) * 128], wg[:, kc, :], kc == 0, kc == 7, [h, wg], [z])
                    op("act", lambda e, z=z, pl=pl: e.activation(out=sAZ[:, pl, :], in_=z[:, :], func=AF.Silu), [z], [sAZ])
                if bcut <= 1:
                    return
                for pl in range(npl):
                    pi = p0 + pl
                    if ctx:
                        chunks = [("c", 0), ("c", 1)]
                    else:
                        offs = [-2, -1, 0, 1, 2]
                        if pi == TOP // 2:
                            offs.append(3)
                        if pi == (TOP + 64) // 2 - 1:
                            offs.insert(0, -3)
                        chunks = [("l", o) for o in offs] + [("c", 0), ("c", 1)]
                    qs = tl + pl * 128
                    for half in range(2):
                        for ci, (kind, o) in enumerate(chunks):
                            s_ps = sp_[ci % 2]
                            for hh in range(4):
                                hd = half * 4 + hh
                                cc = hd // 2
                                if kind == "c":
                                    kap, kb = kTc[:, hd % 2, cc, o * 128:(o + 1) * 128], kTc
                                else:
                                    ks = ((pi + o) % 6) * 128
                                    kap, kb = kT[:, hd % 2, cc, ks:ks + 128], kT
                                mm(s_ps[:, hh * 128:(hh + 1) * 128], kap, qd[:, cc, qs:qs + 128], True, True, [kb, qd], [s_ps])
                            if kind == "c":
                                op("act", lambda e, s_ps=s_ps, ci=ci: e.activation(out=PT[:, ci, :], in_=s_ps[:, :], func=AF.Exp),
                                   [s_ps], [PT])
                            else:
                                sb = Sb[ci % 2]
                                oi = o + 3
                                op("dve", lambda e, s_ps=s_ps, sb=sb, oi=oi, half=half: e.tensor_tensor(
                                    out=sb[:, :].rearrange("p (h q) -> p h q", q=128), in0=s_ps[:, :].rearrange("p (h q) -> p h q", q=128),
                                    in1=RB[:, oi, half * 4:half * 4 + 4, :], op=ALU.add), [s_ps, RB], [sb])
                                for q2 in range(2):
                                    ri = pi * 14 + oi * 2 + q2
                                    op("act", lambda e, sb=sb, ci=ci, q2=q2, ri=ri: e.activation(
                                        out=PT[:, ci, :].rearrange("p (h q) -> p h q", q=128)[:, :, q2 * 64:(q2 + 1) * 64],
                                        in_=sb[:, :].rearrange("p (h q) -> p h q", q=128)[:, :, q2 * 64:(q2 + 1) * 64],
                                        func=AF.Exp, bias=rm[:, ri:ri + 1], scale=1.0), [sb, rm], [PT])
                        nch = len(chunks)
                        if bcut <= 2:
                            return
                        for hh in range(4):
                            hd = half * 4 + hh
                            for ci, (kind, o) in enumerate(chunks):
                                if kind == "c":
                                    vap, vb_ = v1c[:, o, hd, :], v1c
                                else:
                                    vap, vb_ = v1[:, (pi + o) % 6, hd, :], v1
                                mm(obp[:, hh * 65:(hh + 1) * 65], PT[:, ci, hh * 128:(hh + 1) * 128], vap, ci == 0, ci == nch - 1,
                                   [PT, vb_], [obp])
                        op("dve", lambda e: e.reciprocal(out=rec[:, 0:4], in_=obp[:, 64:260:65]), [obp], [rec])
                        for hh in range(4):
                            hd = half * 4 + hh
                            op("dve", lambda e, hh=hh, hd=hd, pl=pl: e.scalar_tensor_tensor(
                                out=yat[:, hd * 64:(hd + 1) * 64], in0=obp[:, hh * 65:hh * 65 + 64], scalar=rec[:, hh:hh + 1],
                                in1=sAZ[:, pl, hd * 64:(hd + 1) * 64], op0=ALU.mult, op1=ALU.mult), [obp, rec, sAZ], [yat])
                        for t_ in conv_th[:per_unit]:
                            t_()
                        del conv_th[:per_unit]
                    if bcut <= 3:
                        return
                    for c in range(4):
                        op("pe", lambda e, c=c: e.transpose(out=trp[:, c * 128:(c + 1) * 128], in_=yat[:, c * 128:(c + 1) * 128],
                                                            identity=ident[:, :]), [yat, ident], [trp])
                    op("act", lambda e, pl=pl: e.activation(out=yaT[:, :, pl * 128:(pl + 1) * 128],
                                                            in_=trp[:, 0:512].rearrange("p (c t) -> p c t", t=128), func=AF.Copy),
                       [trp], [yaT])
                if bcut <= 4:
                    return
                continue
            if gcol == C_Z and bcut <= 5:
                return
            if gcol == C_Z:
                conformer(l, T, gld, co, conv_th)
            for cc in range(4):
                z = nzp()
                for kc in range(8):
                    mm(z[:, 0:T], wg[:, kc, cc * 128:(cc + 1) * 128], h[:, kc, tl:tl + T], kc == 0, kc == 7, [h, wg], [z])
                if gcol == B_B:
                    wofs = po + P_CSW + cc * 3
                    op("dve", lambda e, cc=cc, wofs=wofs: e.tensor_scalar(
                        out=acc[:, cc, 0:T], in0=cgd[:, cc, co - 1:co - 1 + T], scalar1=par_sb[:, wofs:wofs + 1], scalar2=None,
                        op0=ALU.mult), [cgd, par_sb], [acc])
                    for k in (1, 2):
                        op("dve", lambda e, cc=cc, wofs=wofs, k=k: e.scalar_tensor_tensor(
                            out=acc[:, cc, 0:T], in0=cgd[:, cc, co - 1 + k:co - 1 + k + T], scalar=par_sb[:, wofs + k:wofs + k + 1],
                            in1=acc[:, cc, 0:T], op0=ALU.mult, op1=ALU.add), [cgd, par_sb, acc], [acc])
                    op("dve", lambda e, cc=cc, z=z: e.tensor_tensor(out=acc[:, cc, 0:T], in0=z[:, 0:T], in1=acc[:, cc, 0:T], op=ALU.mult),
                       [z, acc], [acc])
                elif gcol == B_Z:
                    s_ = sg[cc % 2]
                    op("act", lambda e, z=z, s_=s_: e.activation(out=s_[:, 0:T], in_=z[:, 0:T], func=AF.Silu), [z], [s_])
                    op("dve", lambda e, cc=cc, s_=s_: e.tensor_tensor(out=ybT[:, cc, 0:T], in0=acc[:, cc, 0:T], in1=s_[:, 0:T], op=ALU.mult),
                       [acc, s_], [ybT])
                elif gcol == C_Z:
                    s_ = sg[cc % 2]
                    op("act", lambda e, z=z, s_=s_: e.activation(out=s_[:, 0:T], in_=z[:, 0:T], func=AF.Silu), [z], [s_])
                    op("dve", lambda e, cc=cc, s_=s_: e.tensor_tensor(out=ycT[:, cc, 0:T], in0=u[:, cc, 0:T], in1=s_[:, 0:T], op=ALU.mult),
                       [u, s_], [ycT])
        if bcut <= 6:
            return
        for F in range(2):
            for br in range(3):
                wg = ws.next()
                for fi in range(4):
                    z = nzp()
                    for kc in range(8):
                        mm(z[:, 0:T], wg[:, kc, fi * 128:(fi + 1) * 128], h[:, kc, tl:tl + T], kc == 0, kc == 7, [h, wg], [z])
                    op("act", lambda e, z=z, br=br, fi=fi: e.activation(out=gts[:, br * 4 + fi, 0:T], in_=z[:, 0:T], func=AF.Sigmoid),
                       [z], [gts])
            for fi in range(4):
                f = F * 4 + fi
                zs = []
                for br, ysrc in enumerate((yaT, ybT, ycT)):
                    z = nzp(); zs.append(z)
                    for kc in range(4):
                        mm(z[:, 0:T], WO[:, br * 4 + kc, f * 128:(f + 1) * 128], ysrc[:, kc, 0:T], kc == 0, kc == 3, [WO, ysrc], [z])
                m0, m1 = mt
                op("dve", lambda e, z=zs[0], fi=fi: e.tensor_tensor(out=m0[:, 0:T], in0=z[:, 0:T], in1=gts[:, fi, 0:T], op=ALU.mult),
                   [zs[0], gts], [m0])
                op("dve", lambda e, z=zs[1], fi=fi: e.tensor_tensor(out=m1[:, 0:T], in0=z[:, 0:T], in1=gts[:, 4 + fi, 0:T], op=ALU.mult),
                   [zs[1], gts], [m1])
                op("pool", lambda e: e.tensor_tensor(out=m0[:, 0:T], in0=m0[:, 0:T], in1=m1[:, 0:T], op=ALU.add), [m0, m1], [m0])
                op("dve", lambda e, z=zs[2], fi=fi: e.tensor_tensor(out=m1[:, 0:T], in0=z[:, 0:T], in1=gts[:, 8 + fi, 0:T], op=ALU.mult),
                   [zs[2], gts], [m1])
                op("pool", lambda e, f=f: e.tensor_tensor(out=mT[:, f, 0:T], in0=m0[:, 0:T], in1=m1[:, 0:T], op=ALU.add), [m0, m1], [mT])
        if bcut <= 7:
            return
        for f2 in range(8):
            z = nzp()
            for f in range(8):
                mm(z[:, 0:T], WO[:, 12 + f, f2 * 128:(f2 + 1) * 128], mT[:, f, 0:T], f == 0, f == 7, [WO, mT], [z])
            op("dve", lambda e, z=z, f2=f2: e.scalar_tensor_tensor(
                out=xb[:, f2, 0:T], in0=z[:, 0:T], scalar=ada[:, l, 16 + f2, cj:cj + 1], in1=xb[:, f2, 0:T], op0=ALU.mult, op1=ALU.add),
               [z, ada, xb], [xb])
        if last:
            lo = max(p0, TOP // 2); hi = min(p1, (TOP + 64) // 2)
            if hi > lo:
                a0 = (lo - p0) * 128; n = (hi - lo) * 128
                yo = (lo - TOP // 2) * 128
                ev = dma("pool", y[:, yo:yo + n].rearrange("(c p) t -> p c t", p=128), xb[:, :, a0:a0 + n], [ytrk], [xb])
                yevs.append(ev)
        else:
            dtrk = xt((l + 1) % 2, "c" if ctx else j)
            dma("pool", xs[(l + 1) % 2][:, t0:t0 + T].rearrange("(c p) t -> p c t", p=128), xb[:, :, 0:T], [dtrk], [xb])

    ytrk = P.track("ytrk")
    yevs = []

    def conf_conv_thunks(l, T, gld, co):
        po = l * NPAR
        th = []
        for k in range(31):
            for cc in range(4):
                wofs = po + P_CCW + cc * 31 + k
                src_ap = gld[:, cc, co - 15 + k:co - 15 + k + T]
                if k == 0:
                    th.append(lambda cc=cc, wofs=wofs, src_ap=src_ap: op("dve", lambda e: e.tensor_scalar(
                        out=u[:, cc, 0:T], in0=src_ap, scalar1=par_sb[:, wofs:wofs + 1],
                        scalar2=par_sb[:, po + P_CCB + cc:po + P_CCB + cc + 1], op0=ALU.mult, op1=ALU.add), [gld, par_sb], [u]))
                else:
                    th.append(lambda cc=cc, wofs=wofs, src_ap=src_ap: op("dve", lambda e: e.scalar_tensor_tensor(
                        out=u[:, cc, 0:T], in0=src_ap, scalar=par_sb[:, wofs:wofs + 1], in1=u[:, cc, 0:T],
                        op0=ALU.mult, op1=ALU.add), [gld, par_sb, u], [u]))
        return th

    def conformer(l, T, gld, co, pending):
        po = l * NPAR
        for t_ in pending:
            t_()
        del pending[:]
        for cc in range(4):
            mm(stp[:, 0:T], o512_f(), u[:, cc, 0:T], cc == 0, cc == 3, [u, cst_f], [stp])
        for cc in range(4):
            op("dve", lambda e, cc=cc: e.tensor_tensor(out=u[:, cc, 0:T], in0=u[:, cc, 0:T], in1=stp[:, 0:T], op=ALU.subtract),
               [u, stp], [u])
        op("act", lambda e: e.activation(out=scr[:, 0:4, 0:T], in_=u[:, :, 0:T], func=AF.Square), [u], [scr])
        for cc in range(4):
            mm(stp[:, 0:T], o512_f(), scr[:, cc, 0:T], cc == 0, cc == 3, [scr, cst_f], [stp])
        op("act", lambda e: e.activation(out=r2[:, 0:T], in_=stp[:, 0:T], func=AF.Sqrt, bias=epsc[:, 2:3], scale=1.0), [stp, epsc], [r2])
        op("dve", lambda e: e.reciprocal(out=r2[:, 0:T], in_=r2[:, 0:T]), [r2], [r2])
        for cc in range(4):
            op("dve", lambda e, cc=cc: e.tensor_tensor(out=u[:, cc, 0:T], in0=u[:, cc, 0:T], in1=r2[:, 0:T], op=ALU.mult), [u, r2], [u])
            op("act", lambda e, cc=cc: e.activation(out=u[:, cc, 0:T], in_=u[:, cc, 0:T], func=AF.Silu,
                                                     bias=par_sb[:, po + P_LNB + cc:po + P_LNB + cc + 1],
                                                     scale=par_sb[:, po + P_LNG + cc:po + P_LNG + cc + 1]), [u, par_sb], [u])

    rng = layer_ranges(L)
    for l in range(L):
        (k0, k1), (o0, o1) = rng[l]
        last = (l == L - 1)
        seq = [("A", "c")]
        if not last:
            seq.append(("B", "c"))
        jsA = list(range(k0 // 2, (k1 + 1) // 2))
        jsB = list(range(o0 // 2, (o1 + 1) // 2))
        seq.append(("A", jsA[0]))
        for j in jsA[1:]:
            seq.append(("A", j))
            if j - 1 in jsB:
                seq.append(("B", j - 1))
        if jsA[-1] in jsB:
            seq.append(("B", jsA[-1]))
        for ph, j in seq:
            if ph == "A":
                ws.plan += [(l, c) for c in A_GROUPS]
            else:
                ws.plan += [(l, c) for c in B_GROUPS[:4]]
                for F in range(2):
                    for br in range(3):
                        ws.plan.append((l, GATES + br * 1024 + F * 512))
        rng[l] = (rng[l][0], rng[l][1], seq)

    nph = [0]
    for l in range(L):
        (k0, k1), (o0, o1), seq = rng[l]
        last = (l == L - 1)
        dma("sp", WO[:, :, :], wb_o[l, :, :].rearrange("(k p) n -> p k n", p=128), [WO], [wbt_o[l]])
        dma("pool", RB[:, :, :, :], rbt[l, :, :].rearrange("p (o h q) -> p o h q", o=7, h=NH), [RB], [wsrc])
        for ph, j in seq:
            if stop is not None and nph[0] >= stop:
                break
            nph[0] += 1
            ctx = (j == "c")
            if ctx:
                p0, p1 = 0, 2
            elif ph == "A":
                p0, p1 = max(2 * j, k0), min(2 * j + 2, k1)
            else:
                p0, p1 = max(2 * j, o0), min(2 * j + 2, o1)
            if ph == "A":
                phase_a(l, j, p0, p1, ctx)
            else:
                phase_b(l, j, p0, p1, ctx, last)

    if not yevs:
        yevs.append(dma("pool", y[:, 0:256].rearrange("(c p) t -> p c t", p=128), xb[:, :, 0:256], [ytrk], [xb]))
    P.emit(yevs)
    return nc, P


def _consts():
    c = np.zeros((128, 512), np.float32)
    c[:, 0:128] = np.eye(128, dtype=np.float32)
    c[:, 128:256] = 1.0
    c[0:64, 256:320] = 1.0
    c[64:128, 320:384] = 1.0
    c[:, 384:512] = 1.0 / 512.0
    return c


def _rb_tables(rpb):
    kc = np.arange(GW)[:, None]
    qc = np.arange(GW)[None, :]
    cs = np.clip(qc - 8, 0, GW - 16)
    colv = (kc >= cs) & (kc < cs + 16)
    dcol = np.clip(kc - qc, -15, 15) + 15
    out = np.full((DEPTH, 2, GW, 7, NH, 2, GW), NEG, np.float32)
    for oi, o in enumerate(OFFS):
        for kr2 in range(2):
            for qr2 in range(2):
                d = 2 * o + kr2 - qr2 + 7
                if 0 <= d <= 14:
                    blk = rpb[:, :, d, :][:, :, dcol]
                    blk = np.where(colv[None, None], blk, np.float32(NEG))
                    out[:, kr2, :, oi, :, qr2, :] = np.transpose(blk, (0, 2, 1, 3))
    return np.ascontiguousarray(out.reshape(DEPTH, 128, 7 * NH * 128))


def _row_masks(a):
    m = np.full((2, NPAIR, 7, 2), NEG, np.float32)
    for pi in range(NPAIR):
        for q2 in range(2):
            qg = a - TOP + 2 * pi + q2
            if qg < 0 or qg >= ROWS:
                continue
            rs = min(max(qg - 4, 0), ROWS - 8)
            for oi, o in enumerate(OFFS):
                for k2 in range(2):
                    kg = a - TOP + 2 * (pi + o) + k2
                    if rs <= kg < rs + 8:
                        m[k2, pi, oi, q2] = 0.0
    return np.ascontiguousarray(np.repeat(m, 64, axis=0).reshape(128, NPAIR * 14))


def _params(inp):
    p = np.zeros((128, DEPTH, NPAR), np.float32)
    for l in range(DEPTH):
        p[:, l, P_NG:P_NG + 8] = inp["norm_g"][l].reshape(8, 128).T
        p[:, l, P_BADA:P_BADA + 24] = inp["b_ada"][l].reshape(24, 128).T
        p[:, l, P_GQ] = np.tile(inp["q_norm_g"][l], 2)
        p[:, l, P_GK] = np.tile(inp["k_norm_g"][l], 2)
        p[:, l, P_CSW:P_CSW + 12] = inp["conv_short_w"][l].reshape(3, 4, 128).transpose(2, 1, 0).reshape(128, 12)
        p[:, l, P_CCW:P_CCW + 124] = inp["conv_conf_w"][l].reshape(31, 4, 128).transpose(2, 1, 0).reshape(128, 124)
        p[:, l, P_CCB:P_CCB + 4] = inp["conv_conf_b"][l].reshape(4, 128).T
        p[:, l, P_LNG:P_LNG + 4] = inp["ln_conf_g"][l].reshape(4, 128).T
        p[:, l, P_LNB:P_LNB + 4] = inp["ln_conf_b"][l].reshape(4, 128).T
    return np.ascontiguousarray(p.reshape(128, DEPTH * NPAR))


_CACHE = {}


def make_in_maps(inp, L=DEPTH):
    inp = {k: np.asarray(v, dtype=np.float32) for k, v in inp.items()}
    x = inp["x"].reshape(2, ROWS, GW, D)
    shared = dict(par=_params(inp), cst=_consts(), rbt=np.ascontiguousarray(_rb_tables(inp["rpb"])[:L]),
                  w_ada=np.ascontiguousarray(inp["w_ada"][:L]), w_in=np.ascontiguousarray(inp["w_in"][:L]),
                  w_out_a=np.ascontiguousarray(inp["w_out_a"][:L]), w_out_b=np.ascontiguousarray(inp["w_out_b"][:L]),
                  w_out_c=np.ascontiguousarray(inp["w_out_c"][:L]), w_o=np.ascontiguousarray(inp["w_o"][:L]))
    maps = []
    for c in range(8):
        b, a = c // 4, 64 * (c % 4)
        slab = np.zeros((SROWS, GW, D), np.float32)
        g0, g1 = max(a - TOP, 0), min(a - TOP + SROWS, ROWS)
        slab[g0 - (a - TOP):g1 - (a - TOP)] = x[b, g0:g1]
        x0 = np.empty((D, NTOK), np.float32)
        x0[:, :STOK] = slab.reshape(STOK, D).T
        x0[:, STOK:] = inp["ctx"][b].T
        cond = np.stack([inp["c"][b].reshape(8, 128).T, inp["c_ctx"].reshape(8, 128).T], axis=-1).reshape(128, 16)
        flg = np.zeros((128, 2), np.float32)
        flg[:, 0] = 0.0 if a == 0 else 1.0
        flg[:, 1] = 0.0 if a + 64 == ROWS else 1.0
        m = dict(shared)
        m.update(x0=x0, cond=np.ascontiguousarray(cond), rm=_row_masks(a), flg=flg)
        maps.append(m)
    return maps


def assemble(results):
    out = np.empty((2, ROWS, GW, D), np.float32)
    for c in range(8):
        b, a = c // 4, 64 * (c % 4)
        out[b, a:a + 64] = results[c]["y"].T.reshape(64, GW, D)
    return out.reshape(2, ROWS * GW, D)


def kernel(**inputs):
    if "nc" not in _CACHE:
        _CACHE["nc"] = build_program(DEPTH)[0]
    nc = _CACHE["nc"]
    maps = make_in_maps(inputs)
    res = run_bass_kernel_spmd(nc, maps, core_ids=list(range(8)))
    return assemble(res.results)
```

```python
import numpy as np
from contextlib import ExitStack
import concourse.bass as bass
import concourse.mybir as mybir
from concourse.bass_utils import run_bass_kernel_spmd

F32 = mybir.dt.float32
BF16 = mybir.dt.bfloat16
AF = mybir.ActivationFunctionType
ALU = mybir.AluOpType
EPOCH = 30000

D = 1024
DB = 512
NH = 8
HD = 64
GW = 64
ROWS = 256
CTX = 256
DEPTH = 4
D_IN = 8704
A_Q, A_K, A_V, A_Z, B_B, B_C, B_V, B_Z, C_A, C_G, C_Z, GATES = [i * 512 for i in range(12)]
EPS = 1e-6
NEG = -30000.0
SROWS = 96
NPAIR = SROWS // 2
NST = NPAIR // 2
STOK = SROWS * GW
NTOK = STOK + CTX
TOP = 16
OFFS = [-3, -2, -1, 0, 1, 2, 3]
P_NG, P_BADA, P_GQ, P_GK, P_CSW, P_CCW, P_CCB, P_LNG, P_LNB = 0, 8, 32, 33, 34, 46, 170, 174, 178
NPAR = 182


class Buf:
    __slots__ = ("name", "t", "w", "r", "dsem", "dcnt", "sb")

    def __init__(self, name, t, sb):
        self.name = name; self.t = t; self.w = []; self.r = {}; self.dsem = None; self.dcnt = 0; self.sb = sb

    def __getitem__(self, k):
        return self.t[k]


class Prog:
    def __init__(self, nc):
        self.nc = nc
        self.es = ExitStack()
        self.streams = {k: [] for k in ("pe", "act", "dve", "pool", "sp")}
        self.cnt = {k: 0 for k in self.streams}
        self.known = {k: {} for k in self.streams}

    def sbuf(self, name, shape, dtype):
        return Buf(name, self.es.enter_context(self.nc.sbuf_tensor(name, list(shape), dtype)), True)

    def psum(self, name, shape, dtype=F32):
        return Buf(name, self.es.enter_context(self.nc.psum_tensor(name, list(shape), dtype)), True)

    def dram(self, name, shape, dtype, kind="Internal"):
        return Buf(name, self.nc.dram_tensor(name, list(shape), dtype, kind=kind), False)

    def track(self, name):
        return Buf(name, None, False)

    def _wait(self, e, ev):
        k, v = ev
        if self.known[e].get(k, 0) >= v:
            return
        self.known[e][k] = v
        self.streams[e].append(("w", k, v))

    def _deps(self, e, reads, writes, skip_self=False):
        for b in reads:
            for ev in b.w:
                if not (skip_self and ev[0][0] == e):
                    self._wait(e, ev)
        for b in writes:
            for ev in b.w:
                if not (skip_self and ev[0][0] == e):
                    self._wait(e, ev)
            for ev in b.r.values():
                if not (skip_self and ev[0][0] == e):
                    self._wait(e, ev)

    def op(self, e, fn, reads=(), writes=()):
        self._deps(e, reads, writes, skip_self=(e == "pe"))
        n = self.cnt[e]; self.cnt[e] += 1
        key = (e, n // EPOCH)
        ev = (key, n % EPOCH + 1)
        self.streams[e].append(("o", fn, key))
        for b in reads:
            b.r[e] = ev
        for b in writes:
            b.w = [ev]; b.r = {}
        return ev

    def dma(self, q, out_ap, in_ap, dsts, srcs, **kw):
        self._deps(q, srcs, dsts)
        owner = None
        for b in list(dsts) + list(srcs):
            if b.sb:
                owner = b; break
        if owner is None:
            owner = dsts[0]
        if owner.dsem is None:
            owner.dsem = {}; owner.dcnt = {}
        sw = (q == "pool")
        if sw not in owner.dsem:
            owner.dsem[sw] = ("d", owner.name, sw); owner.dcnt[sw] = 0
        owner.dcnt[sw] += 16
        ev = (owner.dsem[sw], owner.dcnt[sw])
        self.streams[q].append(("d", out_ap, in_ap, owner.dsem[sw], kw))
        for b in srcs:
            b.r["dma:" + owner.name] = ev
        for b in dsts:
            b.w = [ev]; b.r = {}
        return ev

    def emit(self, final_waits=()):
        nc = self.nc
        for ev in final_waits:
            self._wait("sp", ev)
        keys = []
        seen = set()
        for e, st in self.streams.items():
            for it in st:
                k = it[1] if it[0] == "w" else (it[2] if it[0] == "o" else it[3])
                if k not in seen:
                    seen.add(k); keys.append(k)
        sems = {k: self.es.enter_context(nc.semaphore("s%d" % i)) for i, k in enumerate(keys)}
        self.n_sems = len(sems)
        streams = self.streams
        with nc.Block() as block:
            def run(e):
                def f(eng):
                    for it in streams[e]:
                        if it[0] == "w":
                            eng.wait_ge(sems[it[1]], it[2])
                        elif it[0] == "o":
                            it[1](eng).then_inc(sems[it[2]], 1)
                        else:
                            eng.dma_start(out=it[1], in_=it[2], **it[4]).then_inc(sems[it[3]], 16)
                return f
            block.tensor(run("pe")); block.scalar(run("act")); block.vector(run("dve"))
            block.gpsimd(run("pool")); block.sync(run("sp"))
        self.es.close()


def layer_ranges(L):
    out = []
    for l in range(L):
        m = L - 1 - l
        o0, o1 = TOP - 4 * m, TOP + 64 + 3 * m
        k0, k1 = o0 - 4, o1 + 3
        out.append(((k0 // 2, (k1 + 1) // 2), (o0 // 2, (o1 + 1) // 2)))
    return out


def build_program(L=DEPTH, stop=None, bcut=99):
    nc = bass.Bass("TRN2", target_bir_lowering=False)
    P = Prog(nc)
    op, dma = P.op, P.dma

    x0 = P.dram("x0", [D, NTOK], F32, kind="ExternalInput")
    cond = P.dram("cond", [128, 16], F32, kind="ExternalInput")
    par = P.dram("par", [128, DEPTH * NPAR], F32, kind="ExternalInput")
    cst = P.dram("cst", [128, 512], F32, kind="ExternalInput")
    rm_in = P.dram("rm", [128, NPAIR * 14], F32, kind="ExternalInput")
    flg_in = P.dram("flg", [128, 2], F32, kind="ExternalInput")
    rbt = P.dram("rbt", [L, 128, 7 * NH * 128], F32, kind="ExternalInput")
    w_ada = P.dram("w_ada", [L, D, 3 * D], F32, kind="ExternalInput")
    w_in = P.dram("w_in", [L, D, D_IN], F32, kind="ExternalInput")
    w_oa = P.dram("w_out_a", [L, DB, D], F32, kind="ExternalInput")
    w_ob = P.dram("w_out_b", [L, DB, D], F32, kind="ExternalInput")
    w_oc = P.dram("w_out_c", [L, DB, D], F32, kind="ExternalInput")
    w_o = P.dram("w_o", [L, D, D], F32, kind="ExternalInput")
    y = P.dram("y", [D, 64 * GW], F32, kind="ExternalOutput")
    xs = [P.dram("xs0", [D, NTOK], F32), P.dram("xs1", [D, NTOK], F32)]
    wb_in = P.dram("wb_in", [L, D, D_IN], BF16)
    wb_o = P.dram("wb_o", [L, 2560, D], BF16)

    xtrk = {}

    def xt(bufid, st):
        k = (bufid, st)
        if k not in xtrk:
            xtrk[k] = P.track("xt_%s_%s" % k)
        return xtrk[k]

    cst_f = P.sbuf("cst_f", [128, 512], F32)
    ident = P.sbuf("ident", [128, 128], BF16)
    par_sb = P.sbuf("par_sb", [128, DEPTH * NPAR], F32)
    gk8 = P.sbuf("gk8", [128, DEPTH], F32)
    rm = P.sbuf("rm_sb", [128, NPAIR * 14], F32)
    flg = P.sbuf("flg_sb", [128, 2], F32)
    cnd = P.sbuf("cnd", [128, 16], F32)
    epsc = P.sbuf("epsc", [128, 4], F32)
    ada = P.sbuf("ada", [128, DEPTH, 24, 2], F32)
    gs = P.sbuf("gs", [128, DEPTH, 8, 2], F32)
    WR = [P.sbuf("WR%d" % i, [128, 8, 512], BF16) for i in range(3)]
    WO = P.sbuf("WO", [128, 20, D], BF16)
    RB = P.sbuf("RB", [128, 7, NH, 128], BF16)
    xa = P.sbuf("xa", [128, 8, 256], F32)
    xb = P.sbuf("xb", [128, 8, 256], F32)
    scr = P.sbuf("scr", [128, 4, 256], F32)
    wa = [xb, xa]
    tmpA = scr
    rstd = P.sbuf("rstd", [128, 256], F32)
    r2 = P.sbuf("r2", [128, 256], F32)
    hT = [P.sbuf("hT%d" % i, [128, 8, 256], BF16) for i in range(2)]
    hTc = hT[0]
    qT = [P.sbuf("qT%d" % i, [128, 4, 256], BF16) for i in range(2)]
    qTc = qT[0]
    kT = P.sbuf("kT", [128, 2, 4, 768], BF16)
    kTc = P.sbuf("kTc", [128, 2, 4, 256], BF16)
    v1 = P.sbuf("v1", [128, 6, NH, 65], BF16)
    v1c = P.sbuf("v1c", [128, 2, NH, 65], BF16)
    cg = P.sbuf("cg", [128, 4, 800], BF16)
    glu = P.sbuf("glu", [128, 4, 800], BF16)
    cgc = P.sbuf("cgc", [128, 4, 288], BF16)
    gluc = P.sbuf("gluc", [128, 4, 288], BF16)
    sq2 = [P.sbuf("sq2_%d" % i, [128, 256], F32) for i in range(2)]
    sAZ = P.sbuf("sAZ", [128, 2, 512], BF16)
    Sb = [P.sbuf("Sb%d" % i, [128, 512], F32) for i in range(2)]
    PT = P.sbuf("PT", [128, 8, 512], BF16)
    rec = P.sbuf("rec", [128, 4], F32)
    yat = P.sbuf("yat", [128, 512], BF16)
    yaT = P.sbuf("yaT", [128, 4, 256], BF16)
    ybT = P.sbuf("ybT", [128, 4, 256], BF16)
    ycT = P.sbuf("ycT", [128, 4, 256], BF16)
    mT = P.sbuf("mT", [128, 8, 256], BF16)
    sg = [P.sbuf("sg%d" % i, [128, 256], F32) for i in range(2)]
    acc = P.sbuf("acc", [128, 4, 256], F32)
    u = P.sbuf("u", [128, 4, 256], F32)
    gts = P.sbuf("gts", [128, 12, 256], BF16)
    mt = [P.sbuf("mt%d" % i, [128, 256], F32) for i in range(2)]

    zp = [P.psum("zp%d" % i, [128, 512]) for i in range(3)]
    stp = P.psum("stp", [128, 512])
    sp_ = [P.psum("sps%d" % i, [128, 512]) for i in range(2)]
    obp = P.psum("obp", [128, 512])
    trp = P.psum("trp", [128, 1024], BF16)
    zpi = [0]

    def nzp():
        zpi[0] = (zpi[0] + 1) % 3
        return zp[zpi[0]]

    ones_f = lambda: cst_f[:, 128:256]
    blk_f = lambda: cst_f[:, 256:384]
    o512_f = lambda: cst_f[:, 384:512]

    def mm(out, lhsT, rhs, start, stop, rd, wr):
        op("pe", lambda e: e.matmul(out, lhsT=lhsT, rhs=rhs, start=start, stop=stop), rd, wr)

    dma("sp", cst_f[:, :], cst[:, :], [cst_f], [cst])
    dma("sp", par_sb[:, :], par[:, :], [par_sb], [par])
    dma("sp", rm[:, :], rm_in[:, :], [rm], [rm_in])
    dma("sp", flg[:, :], flg_in[:, :], [flg], [flg_in])
    dma("sp", cnd[:, :], cond[:, :], [cnd], [cond])
    op("dve", lambda e: e.tensor_copy(out=ident[:, :], in_=cst_f[:, 0:128]), [cst_f], [ident])
    for i_, v_ in enumerate((D * EPS, HD * EPS, EPS)):
        op("dve", lambda e, i_=i_, v_=v_: e.memset(epsc[:, i_:i_ + 1], v_), [], [epsc])
    for b_ in (v1, v1c):
        op("dve", lambda e, b_=b_: e.memset(b_[:, :, :, :], 0.0), [], [b_])
        op("dve", lambda e, b_=b_: e.memset(b_[:, :, :, 64:65], 1.0), [], [b_])
    for b_ in (kT, kTc):
        op("dve", lambda e, b_=b_: e.memset(b_[:, :, :, :], 0.0), [], [b_])
    for b_ in (qT[0], qT[1]):
        op("dve", lambda e, b_=b_: e.memset(b_[:, :, :], 0.0), [], [b_])
    op("dve", lambda e: e.memset(cgc[:, :, :], 0.0), [], [cgc])
    op("dve", lambda e: e.memset(gluc[:, :, :], 0.0), [], [gluc])
    op("dve", lambda e: e.memset(cg[:, :, :], 0.0), [], [cg])
    op("dve", lambda e: e.memset(glu[:, :, :], 0.0), [], [glu])
    for l in range(L):
        op("dve", lambda e, l=l: e.tensor_scalar(out=gk8[:, l:l + 1], in0=par_sb[:, l * NPAR + P_GK:l * NPAR + P_GK + 1],
                                                  scalar1=8.0, scalar2=None, op0=ALU.mult), [par_sb], [gk8])
    op("act", lambda e: e.activation(out=cnd[:, :], in_=cnd[:, :], func=AF.Silu), [cnd], [cnd])

    wbt_in = [P.track("wbin%d" % l) for l in range(L)]
    wbt_o = [P.track("wbo%d" % l) for l in range(L)]
    wsrc = P.track("wsrc")

    def cast_layer(l):
        for r in range(8):
            dma("pool", wb_in[l, r * 128:(r + 1) * 128, :], w_in[l, r * 128:(r + 1) * 128, :], [wbt_in[l]], [wsrc])
        for i, wsrc_t in enumerate((w_oa, w_ob, w_oc)):
            for r in range(2):
                dma("pool", wb_o[l, i * 512 + r * 256:i * 512 + (r + 1) * 256, :], wsrc_t[l, r * 256:(r + 1) * 256, :],
                    [wbt_o[l]], [wsrc])
        for r in range(4):
            dma("pool", wb_o[l, 1536 + r * 256:1536 + (r + 1) * 256, :], w_o[l, r * 256:(r + 1) * 256, :], [wbt_o[l]], [wsrc])

    for l in range(L):
        cast_layer(l)

    wai = 0
    for l in range(L):
        for g in range(12):
            wbuf = wa[wai % 2]; wai += 1
            dma("sp", wbuf[:, :, :], w_ada[l, :, g * 256:(g + 1) * 256].rearrange("(k p) n -> p k n", p=128), [wbuf], [wsrc])
            for cc in range(2):
                ch = g * 2 + cc
                for kc in range(8):
                    mm(stp[:, ch * 2:ch * 2 + 2], wbuf[:, kc, cc * 128:(cc + 1) * 128], cnd[:, kc * 2:kc * 2 + 2],
                       kc == 0, kc == 7, [wbuf, cnd], [stp])
        for j in range(2):
            op("dve", lambda e, l=l, j=j: e.tensor_tensor(
                out=ada[:, l, :, j], in0=stp[:, j:48:2], in1=par_sb[:, l * NPAR + P_BADA:l * NPAR + P_BADA + 24], op=ALU.add),
               [stp, par_sb], [ada])
            op("dve", lambda e, l=l, j=j: e.scalar_tensor_tensor(
                out=gs[:, l, :, j], in0=ada[:, l, 8:16, j], scalar=1.0, in1=par_sb[:, l * NPAR + P_NG:l * NPAR + P_NG + 8],
                op0=ALU.add, op1=ALU.mult), [ada, par_sb], [gs])
            op("dve", lambda e, l=l, j=j: e.tensor_scalar(out=gs[:, l, :, j], in0=gs[:, l, :, j], scalar1=32.0, scalar2=None,
                                                           op0=ALU.mult), [gs], [gs])

    class WStream:
        def __init__(self):
            self.plan = []
            self.issued = 0
            self.used = 0

        def issue_upto(self, n):
            while self.issued < min(n, len(self.plan)):
                l, col0 = self.plan[self.issued]
                buf = WR[self.issued % 3]
                dma("sp", buf[:, :, :], wb_in[l, :, col0:col0 + 512].rearrange("(k p) n -> p k n", p=128), [buf], [wbt_in[l]])
                self.issued += 1

        def next(self):
            i = self.used
            self.issue_upto(i + 3)
            self.used += 1
            return WR[i % 3]

    ws = WStream()

    A_GROUPS = [B_C, B_V, C_A, C_G, A_V, A_K, A_Q]
    B_GROUPS = [A_Z, B_B, B_Z, C_Z] + [GATES + i * 512 for i in range(6)]

    def phase_a(l, j, p0, p1, ctx):
        cj = 1 if ctx else 0
        if ctx:
            t0, T, tl = STOK, CTX, 0
            h = hTc
        else:
            t0, T, tl = p0 * 128, (p1 - p0) * 128, (p0 % 2) * 128
            h = hT[j % 2]
        src = x0 if l == 0 else xs[l % 2]
        strk = xt("in" if l == 0 else l % 2, "c" if ctx else j)
        po = l * NPAR
        dma("sp", xa[:, :, 0:T], src[:, t0:t0 + T].rearrange("(c p) t -> p c t", p=128), [xa], [strk])
        for c in range(8):
            s2 = sq2[c % 2]
            op("act", lambda e, c=c, s2=s2: e.activation(out=s2[:, 0:T], in_=xa[:, c, 0:T], func=AF.Square), [xa], [s2])
            mm(stp[:, 0:T], ones_f(), s2[:, 0:T], c == 0, c == 7, [s2, cst_f], [stp])
        op("act", lambda e: e.activation(out=rstd[:, 0:T], in_=stp[:, 0:T], func=AF.Sqrt, bias=epsc[:, 0:1], scale=1.0), [stp, epsc], [rstd])
        op("dve", lambda e: e.reciprocal(out=rstd[:, 0:T], in_=rstd[:, 0:T]), [rstd], [rstd])
        for c in range(8):
            op("dve", lambda e, c=c: e.scalar_tensor_tensor(out=xa[:, c, 0:T], in0=xa[:, c, 0:T], scalar=gs[:, l, c, cj:cj + 1],
                                                            in1=rstd[:, 0:T], op0=ALU.mult, op1=ALU.mult), [xa, rstd, gs], [xa])
            op("act", lambda e, c=c: e.activation(out=h[:, c, tl:tl + T], in_=xa[:, c, 0:T], func=AF.Identity,
                                                   bias=ada[:, l, c, cj:cj + 1], scale=1.0), [xa, ada], [h])
        if ctx:
            qd, kd, cgd, gld = qTc, kTc, cgc, gluc
            qo, ko, co = 0, 0, 16
        else:
            qd, kd, cgd, gld = qT[j % 2], kT, cg, glu
            qo, ko, co = tl, (j % 3) * 256 + tl, 16 + (j % 3) * 256 + tl
        pend = []

        def flush(n):
            while len(pend) > n:
                pend.pop(0)()

        def postA(gcol, cc, z):
            if gcol in (A_Q, A_K):
                s2 = sq2[cc % 2]
                op("act", lambda e, z=z, s2=s2: e.activation(out=s2[:, 0:T], in_=z[:, 0:T], func=AF.Square), [z], [s2])
                mm(stp[:, 0:T], blk_f(), s2[:, 0:T], True, True, [s2, cst_f], [stp])
                op("act", lambda e: e.activation(out=r2[:, 0:T], in_=stp[:, 0:T], func=AF.Sqrt, bias=epsc[:, 1:2], scale=1.0), [stp, epsc], [r2])
                op("dve", lambda e: e.reciprocal(out=r2[:, 0:T], in_=r2[:, 0:T]), [r2], [r2])
                if gcol == A_Q:
                    op("dve", lambda e, z=z, cc=cc: e.scalar_tensor_tensor(
                        out=qd[:, cc, qo:qo + T], in0=z[:, 0:T], scalar=par_sb[:, po + P_GQ:po + P_GQ + 1], in1=r2[:, 0:T],
                        op0=ALU.mult, op1=ALU.mult), [z, r2, par_sb], [qd])
                else:
                    for hh in range(2):
                        ps_ = slice(hh * 64, (hh + 1) * 64)
                        op("dve", lambda e, z=z, cc=cc, hh=hh, ps_=ps_: e.scalar_tensor_tensor(
                            out=kd[ps_, hh, cc, ko:ko + T], in0=z[ps_, 0:T], scalar=gk8[ps_, l:l + 1], in1=r2[ps_, 0:T],
                            op0=ALU.mult, op1=ALU.mult), [z, r2, gk8], [kd])
            elif gcol in (B_C, C_A):
                op("act", lambda e, z=z, cc=cc: e.activation(out=tmpA[:, cc, 0:T], in_=z[:, 0:T], func=AF.Copy), [z], [tmpA])
            elif gcol == B_V:
                op("dve", lambda e, z=z, cc=cc: e.tensor_tensor(out=cgd[:, cc, co:co + T], in0=z[:, 0:T], in1=tmpA[:, cc, 0:T],
                                                                 op=ALU.mult), [z, tmpA], [cgd])
            elif gcol == C_G:
                s2 = sq2[cc % 2]
                op("act", lambda e, z=z, s2=s2: e.activation(out=s2[:, 0:T], in_=z[:, 0:T], func=AF.Sigmoid), [z], [s2])
                op("dve", lambda e, s2=s2, cc=cc: e.tensor_tensor(out=gld[:, cc, co:co + T], in0=s2[:, 0:T], in1=tmpA[:, cc, 0:T],
                                                                   op=ALU.mult), [s2, tmpA], [gld])

        for gcol in A_GROUPS:
            wg = ws.next()
            if gcol == A_V:
                flush(0)
                for pl in range(T // 128):
                    z = nzp()
                    for kc in range(8):
                        mm(z[:, :], h[:, kc, tl + pl * 128:tl + (pl + 1) * 128], wg[:, kc, :], kc == 0, kc == 7, [h, wg], [z])
                    if ctx:
                        vdst, vb = v1c[:, pl, :, 0:64], v1c
                    else:
                        vdst, vb = v1[:, (p0 + pl) % 6, :, 0:64], v1
                    op("act", lambda e, z=z, vdst=vdst: e.activation(out=vdst, in_=z[:, :].rearrange("p (h d) -> p h d", d=64),
                                                                      func=AF.Copy), [z], [vb])
                continue
            for cc in range(4):
                z = nzp()
                for kc in range(8):
                    mm(z[:, 0:T], wg[:, kc, cc * 128:(cc + 1) * 128], h[:, kc, tl:tl + T], kc == 0, kc == 7, [h, wg], [z])
                pend.append(lambda gcol=gcol, cc=cc, z=z: postA(gcol, cc, z))
                flush(1)
        flush(0)
        if not ctx:
            for fj, fi in ((TOP // 4 - 1, 0), ((TOP + 64) // 4, 1)):
                if j == fj:
                    for b_ in (cg, glu):
                        op("pool", lambda e, b_=b_, fi=fi: e.tensor_scalar(
                            out=b_[:, :, co:co + T], in0=b_[:, :, co:co + T], scalar1=flg[:, fi:fi + 1], scalar2=None, op0=ALU.mult),
                           [b_, flg], [b_])
            if j % 3 == 2:
                for b_ in (cg, glu):
                    op("pool", lambda e, b_=b_: e.tensor_copy(out=b_[:, :, 0:16], in_=b_[:, :, 768:784]), [b_], [b_])
            if j % 3 == 0:
                for b_ in (cg, glu):
                    op("pool", lambda e, b_=b_: e.tensor_copy(out=b_[:, :, 784:800], in_=b_[:, :, 16:32]), [b_], [b_])

    def phase_b(l, j, p0, p1, ctx, last):
        cj = 1 if ctx else 0
        po = l * NPAR
        if ctx:
            t0, T, tl = STOK, CTX, 0
            h, qd, cgd, gld, co = hTc, qTc, cgc, gluc, 16
            npl = 2
        else:
            t0, T, tl = p0 * 128, (p1 - p0) * 128, (p0 % 2) * 128
            h, qd, cgd, gld, co = hT[j % 2], qT[j % 2], cg, glu, 16 + (j % 3) * 256 + tl
            npl = p1 - p0
        src = x0 if l == 0 else xs[l % 2]
        strk = xt("in" if l == 0 else l % 2, "c" if ctx else j)
        dma("sp", xb[:, :, 0:T], src[:, t0:t0 + T].rearrange("(c p) t -> p c t", p=128), [xb], [strk])

        conv_th = conf_conv_thunks(l, T, gld, co)
        n_units = 2 * npl
        per_unit = (len(conv_th) + n_units - 1) // n_units
        for gcol in B_GROUPS[:4]:
            wg = ws.next()
            if gcol == A_Z:
                for pl in range(npl):
                    z = nzp()
                    for kc in range(8):
                        mm(z[:, :], h[:, kc, tl + pl * 128:tl + (pl + 1) * 128], wg[:, kc, :], kc == 0, kc == 7, [h, wg], [z])
                    op("act", lambda e, z=z, pl=pl: e.activation(out=sAZ[:, pl, :], in_=z[:, :], func=AF.Silu), [z], [sAZ])
                if bcut <= 1:
                    return
                for pl in range(npl):
                    pi = p0 + pl
                    if ctx:
                        chunks = [("c", 0), ("c", 1)]
                    else:
                        offs = [-2, -1, 0, 1, 2]
                        if pi == TOP // 2:
                            offs.append(3)
                        if pi == (TOP + 64) // 2 - 1:
                            offs.insert(0, -3)
                        chunks = [("l", o) for o in offs] + [("c", 0), ("c", 1)]
                    qs = tl + pl * 128
                    for half in range(2):
                        for ci, (kind, o) in enumerate(chunks):
                            s_ps = sp_[ci % 2]
                            for hh in range(4):
                                hd = half * 4 + hh
                                cc = hd // 2
                                if kind == "c":
                                    kap, kb = kTc[:, hd % 2, cc, o * 128:(o + 1) * 128], kTc
                                else:
                                    ks = ((pi + o) % 6) * 128
                                    kap, kb = kT[:, hd % 2, cc, ks:ks + 128], kT
                                mm(s_ps[:, hh * 128:(hh + 1) * 128], kap, qd[:, cc, qs:qs + 128], True, True, [kb, qd], [s_ps])
                            if kind == "c":
                                op("act", lambda e, s_ps=s_ps, ci=ci: e.activation(out=PT[:, ci, :], in_=s_ps[:, :], func=AF.Exp),
                                   [s_ps], [PT])
                            else:
                                sb = Sb[ci % 2]
                                oi = o + 3
                                op("dve", lambda e, s_ps=s_ps, sb=sb, oi=oi, half=half: e.tensor_tensor(
                                    out=sb[:, :].rearrange("p (h q) -> p h q", q=128), in0=s_ps[:, :].rearrange("p (h q) -> p h q", q=128),
                                    in1=RB[:, oi, half * 4:half * 4 + 4, :], op=ALU.add), [s_ps, RB], [sb])
                                for q2 in range(2):
                                    ri = pi * 14 + oi * 2 + q2
                                    op("act", lambda e, sb=sb, ci=ci, q2=q2, ri=ri: e.activation(
                                        out=PT[:, ci, :].rearrange("p (h q) -> p h q", q=128)[:, :, q2 * 64:(q2 + 1) * 64],
                                        in_=sb[:, :].rearrange("p (h q) -> p h q", q=128)[:, :, q2 * 64:(q2 + 1) * 64],
                                        func=AF.Exp, bias=rm[:, ri:ri + 1], scale=1.0), [sb, rm], [PT])
                        nch = len(chunks)
                        if bcut <= 2:
                            return
                        for hh in range(4):
                            hd = half * 4 + hh
                            for ci, (kind, o) in enumerate(chunks):
                                if kind == "c":
                                    vap, vb_ = v1c[:, o, hd, :], v1c
                                else:
                                    vap, vb_ = v1[:, (pi + o) % 6, hd, :], v1
                                mm(obp[:, hh * 65:(hh + 1) * 65], PT[:, ci, hh * 128:(hh + 1) * 128], vap, ci == 0, ci == nch - 1,
                                   [PT, vb_], [obp])
                        op("dve", lambda e: e.reciprocal(out=rec[:, 0:4], in_=obp[:, 64:260:65]), [obp], [rec])
                        for hh in range(4):
                            hd = half * 4 + hh
                            op("dve", lambda e, hh=hh, hd=hd, pl=pl: e.scalar_tensor_tensor(
                                out=yat[:, hd * 64:(hd + 1) * 64], in0=obp[:, hh * 65:hh * 65 + 64], scalar=rec[:, hh:hh + 1],
                                in1=sAZ[:, pl, hd * 64:(hd + 1) * 64], op0=ALU.mult, op1=ALU.mult), [obp, rec, sAZ], [yat])
                        for t_ in conv_th[:per_unit]:
                            t_()
                        del conv_th[:per_unit]
                    if bcut <= 3:
                        return
                    for c in range(4):
                        op("pe", lambda e, c=c: e.transpose(out=trp[:, c * 128:(c + 1) * 128], in_=yat[:, c * 128:(c + 1) * 128],
                                                            identity=ident[:, :]), [yat, ident], [trp])
                    op("act", lambda e, pl=pl: e.activation(out=yaT[:, :, pl * 128:(pl + 1) * 128],
                                                            in_=trp[:, 0:512].rearrange("p (c t) -> p c t", t=128), func=AF.Copy),
                       [trp], [yaT])
                if bcut <= 4:
                    return
                continue
            if gcol == C_Z and bcut <= 5:
                return
            if gcol == C_Z:
                conformer(l, T, gld, co, conv_th)
            for cc in range(4):
                z = nzp()
                for kc in range(8):
                    mm(z[:, 0:T], wg[:, kc, cc * 128:(cc + 1) * 128], h[:, kc, tl:tl + T], kc == 0, kc == 7, [h, wg], [z])
                if gcol == B_B:
                    wofs = po + P_CSW + cc * 3
                    op("dve", lambda e, cc=cc, wofs=wofs: e.tensor_scalar(
                        out=acc[:, cc, 0:T], in0=cgd[:, cc, co - 1:co - 1 + T], scalar1=par_sb[:, wofs:wofs + 1], scalar2=None,
                        op0=ALU.mult), [cgd, par_sb], [acc])
                    for k in (1, 2):
                        op("dve", lambda e, cc=cc, wofs=wofs, k=k: e.scalar_tensor_tensor(
                            out=acc[:, cc, 0:T], in0=cgd[:, cc, co - 1 + k:co - 1 + k + T], scalar=par_sb[:, wofs + k:wofs + k + 1],
                            in1=acc[:, cc, 0:T], op0=ALU.mult, op1=ALU.add), [cgd, par_sb, acc], [acc])
                    op("dve", lambda e, cc=cc, z=z: e.tensor_tensor(out=acc[:, cc, 0:T], in0=z[:, 0:T], in1=acc[:, cc, 0:T], op=ALU.mult),
                       [z, acc], [acc])
                elif gcol == B_Z:
                    s_ = sg[cc % 2]
                    op("act", lambda e, z=z, s_=s_: e.activation(out=s_[:, 0:T], in_=z[:, 0:T], func=AF.Silu), [z], [s_])
                    op("dve", lambda e, cc=cc, s_=s_: e.tensor_tensor(out=ybT[:, cc, 0:T], in0=acc[:, cc, 0:T], in1=s_[:, 0:T], op=ALU.mult),
                       [acc, s_], [ybT])
                elif gcol == C_Z:
                    s_ = sg[cc % 2]
                    op("act", lambda e, z=z, s_=s_: e.activation(out=s_[:, 0:T], in_=z[:, 0:T], func=AF.Silu), [z], [s_])
                    op("dve", lambda e, cc=cc, s_=s_: e.tensor_tensor(out=ycT[:, cc, 0:T], in0=u[:, cc, 0:T], in1=s_[:, 0:T], op=ALU.mult),
                       [u, s_], [ycT])
        if bcut <= 6:
            return
        for F in range(2):
            for br in range(3):
                wg = ws.next()
                for fi in range(4):
                    z = nzp()
                    for kc in range(8):
                        mm(z[:, 0:T], wg[:, kc, fi * 128:(fi + 1) * 128], h[:, kc, tl:tl + T], kc == 0, kc == 7, [h, wg], [z])
                    op("act", lambda e, z=z, br=br, fi=fi: e.activation(out=gts[:, br * 4 + fi, 0:T], in_=z[:, 0:T], func=AF.Sigmoid),
                       [z], [gts])
            for fi in range(4):
                f = F * 4 + fi
                zs = []
                for br, ysrc in enumerate((yaT, ybT, ycT)):
                    z = nzp(); zs.append(z)
                    for kc in range(4):
                        mm(z[:, 0:T], WO[:, br * 4 + kc, f * 128:(f + 1) * 128], ysrc[:, kc, 0:T], kc == 0, kc == 3, [WO, ysrc], [z])
                m0, m1 = mt
                op("dve", lambda e, z=zs[0], fi=fi: e.tensor_tensor(out=m0[:, 0:T], in0=z[:, 0:T], in1=gts[:, fi, 0:T], op=ALU.mult),
                   [zs[0], gts], [m0])
                op("dve", lambda e, z=zs[1], fi=fi: e.tensor_tensor(out=m1[:, 0:T], in0=z[:, 0:T], in1=gts[:, 4 + fi, 0:T], op=ALU.mult),
                   [zs[1], gts], [m1])
                op("pool", lambda e: e.tensor_tensor(out=m0[:, 0:T], in0=m0[:, 0:T], in1=m1[:, 0:T], op=ALU.add), [m0, m1], [m0])
                op("dve", lambda e, z=zs[2], fi=fi: e.tensor_tensor(out=m1[:, 0:T], in0=z[:, 0:T], in1=gts[:, 8 + fi, 0:T], op=ALU.mult),
                   [zs[2], gts], [m1])
                op("pool", lambda e, f=f: e.tensor_tensor(out=mT[:, f, 0:T], in0=m0[:, 0:T], in1=m1[:, 0:T], op=ALU.add), [m0, m1], [mT])
        if bcut <= 7:
            return
        for f2 in range(8):
            z = nzp()
            for f in range(8):
                mm(z[:, 0:T], WO[:, 12 + f, f2 * 128:(f2 + 1) * 128], mT[:, f, 0:T], f == 0, f == 7, [WO, mT], [z])
            op("dve", lambda e, z=z, f2=f2: e.scalar_tensor_tensor(
                out=xb[:, f2, 0:T], in0=z[:, 0:T], scalar=ada[:, l, 16 + f2, cj:cj + 1], in1=xb[:, f2, 0:T], op0=ALU.mult, op1=ALU.add),
               [z, ada, xb], [xb])
        if last:
            lo = max(p0, TOP // 2); hi = min(p1, (TOP + 64) // 2)
            if hi > lo:
                a0 = (lo - p0) * 128; n = (hi - lo) * 128
                yo = (lo - TOP // 2) * 128
                ev = dma("pool", y[:, yo:yo + n].rearrange("(c p) t -> p c t", p=128), xb[:, :, a0:a0 + n], [ytrk], [xb])
                yevs.append(ev)
        else:
            dtrk = xt((l + 1) % 2, "c" if ctx else j)
            dma("pool", xs[(l + 1) % 2][:, t0:t0 + T].rearrange("(c p) t -> p c t", p=128), xb[:, :, 0:T], [dtrk], [xb])

    ytrk = P.track("ytrk")
    yevs = []

    def conf_conv_thunks(l, T, gld, co):
        po = l * NPAR
        th = []
        for k in range(31):
            for cc in range(4):
                wofs = po + P_CCW + cc * 31 + k
                src_ap = gld[:, cc, co - 15 + k:co - 15 + k + T]
                if k == 0:
                    th.append(lambda cc=cc, wofs=wofs, src_ap=src_ap: op("dve", lambda e: e.tensor_scalar(
                        out=u[:, cc, 0:T], in0=src_ap, scalar1=par_sb[:, wofs:wofs + 1],
                        scalar2=par_sb[:, po + P_CCB + cc:po + P_CCB + cc + 1], op0=ALU.mult, op1=ALU.add), [gld, par_sb], [u]))
                else:
                    th.append(lambda cc=cc, wofs=wofs, src_ap=src_ap: op("dve", lambda e: e.scalar_tensor_tensor(
                        out=u[:, cc, 0:T], in0=src_ap, scalar=par_sb[:, wofs:wofs + 1], in1=u[:, cc, 0:T],
                        op0=ALU.mult, op1=ALU.add), [gld, par_sb, u], [u]))
        return th

    def conformer(l, T, gld, co, pending):
        po = l * NPAR
        for t_ in pending:
            t_()
        del pending[:]
        for cc in range(4):
            mm(stp[:, 0:T], o512_f(), u[:, cc, 0:T], cc == 0, cc == 3, [u, cst_f], [stp])
        for cc in range(4):
            op("dve", lambda e, cc=cc: e.tensor_tensor(out=u[:, cc, 0:T], in0=u[:, cc, 0:T], in1=stp[:, 0:T], op=ALU.subtract),
               [u, stp], [u])
        op("act", lambda e: e.activation(out=scr[:, 0:4, 0:T], in_=u[:, :, 0:T], func=AF.Square), [u], [scr])
        for cc in range(4):
            mm(stp[:, 0:T], o512_f(), scr[:, cc, 0:T], cc == 0, cc == 3, [scr, cst_f], [stp])
        op("act", lambda e: e.activation(out=r2[:, 0:T], in_=stp[:, 0:T], func=AF.Sqrt, bias=epsc[:, 2:3], scale=1.0), [stp, epsc], [r2])
        op("dve", lambda e: e.reciprocal(out=r2[:, 0:T], in_=r2[:, 0:T]), [r2], [r2])
        for cc in range(4):
            op("dve", lambda e, cc=cc: e.tensor_tensor(out=u[:, cc, 0:T], in0=u[:, cc, 0:T], in1=r2[:, 0:T], op=ALU.mult), [u, r2], [u])
            op("act", lambda e, cc=cc: e.activation(out=u[:, cc, 0:T], in_=u[:, cc, 0:T], func=AF.Silu,
                                                     bias=par_sb[:, po + P_LNB + cc:po + P_LNB + cc + 1],
                                                     scale=par_sb[:, po + P_LNG + cc:po + P_LNG + cc + 1]), [u, par_sb], [u])

    rng = layer_ranges(L)
    for l in range(L):
        (k0, k1), (o0, o1) = rng[l]
        last = (l == L - 1)
        seq = [("A", "c")]
        if not last:
            seq.append(("B", "c"))
        jsA = list(range(k0 // 2, (k1 + 1) // 2))
        jsB = list(range(o0 // 2, (o1 + 1) // 2))
        seq.append(("A", jsA[0]))
        for j in jsA[1:]:
            seq.append(("A", j))
            if j - 1 in jsB:
                seq.append(("B", j - 1))
        if jsA[-1] in jsB:
            seq.append(("B", jsA[-1]))
        for ph, j in seq:
            if ph == "A":
                ws.plan += [(l, c) for c in A_GROUPS]
            else:
                ws.plan += [(l, c) for c in B_GROUPS[:4]]
                for F in range(2):
                    for br in range(3):
                        ws.plan.append((l, GATES + br * 1024 + F * 512))
        rng[l] = (rng[l][0], rng[l][1], seq)

    nph = [0]
    for l in range(L):
        (k0, k1), (o0, o1), seq = rng[l]
        last = (l == L - 1)
        dma("sp", WO[:, :, :], wb_o[l, :, :].rearrange("(k p) n -> p k n", p=128), [WO], [wbt_o[l]])
        dma("pool", RB[:, :, :, :], rbt[l, :, :].rearrange("p (o h q) -> p o h q", o=7, h=NH), [RB], [wsrc])
        for ph, j in seq:
            if stop is not None and nph[0] >= stop:
                break
            nph[0] += 1
            ctx = (j == "c")
            if ctx:
                p0, p1 = 0, 2
            elif ph == "A":
                p0, p1 = max(2 * j, k0), min(2 * j + 2, k1)
            else:
                p0, p1 = max(2 * j, o0), min(2 * j + 2, o1)
            if ph == "A":
                phase_a(l, j, p0, p1, ctx)
            else:
                phase_b(l, j, p0, p1, ctx, last)

    if not yevs:
        yevs.append(dma("pool", y[:, 0:256].rearrange("(c p) t -> p c t", p=128), xb[:, :, 0:256], [ytrk], [xb]))
    P.emit(yevs)
    return nc, P


def _consts():
    c = np.zeros((128, 512), np.float32)
    c[:, 0:128] = np.eye(128, dtype=np.float32)
    c[:, 128:256] = 1.0
    c[0:64, 256:320] = 1.0
    c[64:128, 320:384] = 1.0
    c[:, 384:512] = 1.0 / 512.0
    return c


def _rb_tables(rpb):
    kc = np.arange(GW)[:, None]
    qc = np.arange(GW)[None, :]
    cs = np.clip(qc - 8, 0, GW - 16)
    colv = (kc >= cs) & (kc < cs + 16)
    dcol = np.clip(kc - qc, -15, 15) + 15
    out = np.full((DEPTH, 2, GW, 7, NH, 2, GW), NEG, np.float32)
    for oi, o in enumerate(OFFS):
        for kr2 in range(2):
            for qr2 in range(2):
                d = 2 * o + kr2 - qr2 + 7
                if 0 <= d <= 14:
                    blk = rpb[:, :, d, :][:, :, dcol]
                    blk = np.where(colv[None, None], blk, np.float32(NEG))
                    out[:, kr2, :, oi, :, qr2, :] = np.transpose(blk, (0, 2, 1, 3))
    return np.ascontiguousarray(out.reshape(DEPTH, 128, 7 * NH * 128))


def _row_masks(a):
    m = np.full((2, NPAIR, 7, 2), NEG, np.float32)
    for pi in range(NPAIR):
        for q2 in range(2):
            qg = a - TOP + 2 * pi + q2
            if qg < 0 or qg >= ROWS:
                continue
            rs = min(max(qg - 4, 0), ROWS - 8)
            for oi, o in enumerate(OFFS):
                for k2 in range(2):
                    kg = a - TOP + 2 * (pi + o) + k2
                    if rs <= kg < rs + 8:
                        m[k2, pi, oi, q2] = 0.0
    return np.ascontiguousarray(np.repeat(m, 64, axis=0).reshape(128, NPAIR * 14))


def _params(inp):
    p = np.zeros((128, DEPTH, NPAR), np.float32)
    for l in range(DEPTH):
        p[:, l, P_NG:P_NG + 8] = inp["norm_g"][l].reshape(8, 128).T
        p[:, l, P_BADA:P_BADA + 24] = inp["b_ada"][l].reshape(24, 128).T
        p[:, l, P_GQ] = np.tile(inp["q_norm_g"][l], 2)
        p[:, l, P_GK] = np.tile(inp["k_norm_g"][l], 2)
        p[:, l, P_CSW:P_CSW + 12] = inp["conv_short_w"][l].reshape(3, 4, 128).transpose(2, 1, 0).reshape(128, 12)
        p[:, l, P_CCW:P_CCW + 124] = inp["conv_conf_w"][l].reshape(31, 4, 128).transpose(2, 1, 0).reshape(128, 124)
        p[:, l, P_CCB:P_CCB + 4] = inp["conv_conf_b"][l].reshape(4, 128).T
        p[:, l, P_LNG:P_LNG + 4] = inp["ln_conf_g"][l].reshape(4, 128).T
        p[:, l, P_LNB:P_LNB + 4] = inp["ln_conf_b"][l].reshape(4, 128).T
    return np.ascontiguousarray(p.reshape(128, DEPTH * NPAR))


_CACHE = {}


def make_in_maps(inp, L=DEPTH):
    inp = {k: np.asarray(v, dtype=np.float32) for k, v in inp.items()}
    x = inp["x"].reshape(2, ROWS, GW, D)
    shared = dict(par=_params(inp), cst=_consts(), rbt=np.ascontiguousarray(_rb_tables(inp["rpb"])[:L]),
                  w_ada=np.ascontiguousarray(inp["w_ada"][:L]), w_in=np.ascontiguousarray(inp["w_in"][:L]),
                  w_out_a=np.ascontiguousarray(inp["w_out_a"][:L]), w_out_b=np.ascontiguousarray(inp["w_out_b"][:L]),
                  w_out_c=np.ascontiguousarray(inp["w_out_c"][:L]), w_o=np.ascontiguousarray(inp["w_o"][:L]))
    maps = []
    for c in range(8):
        b, a = c // 4, 64 * (c % 4)
        slab = np.zeros((SROWS, GW, D), np.float32)
        g0, g1 = max(a - TOP, 0), min(a - TOP + SROWS, ROWS)
        slab[g0 - (a - TOP):g1 - (a - TOP)] = x[b, g0:g1]
        x0 = np.empty((D, NTOK), np.float32)
        x0[:, :STOK] = slab.reshape(STOK, D).T
        x0[:, STOK:] = inp["ctx"][b].T
        cond = np.stack([inp["c"][b].reshape(8, 128).T, inp["c_ctx"].reshape(8, 128).T], axis=-1).reshape(128, 16)
        flg = np.zeros((128, 2), np.float32)
        flg[:, 0] = 0.0 if a == 0 else 1.0
        flg[:, 1] = 0.0 if a + 64 == ROWS else 1.0
        m = dict(shared)
        m.update(x0=x0, cond=np.ascontiguousarray(cond), rm=_row_masks(a), flg=flg)
        maps.append(m)
    return maps


def assemble(results):
    out = np.empty((2, ROWS, GW, D), np.float32)
    for c in range(8):
        b, a = c // 4, 64 * (c % 4)
        out[b, a:a + 64] = results[c]["y"].T.reshape(64, GW, D)
    return out.reshape(2, ROWS * GW, D)


def kernel(**inputs):
    if "nc" not in _CACHE:
        _CACHE["nc"] = build_program(DEPTH)[0]
    nc = _CACHE["nc"]
    maps = make_in_maps(inputs)
    res = run_bass_kernel_spmd(nc, maps, core_ids=list(range(8)))
    return assemble(res.results)
```

```python
import numpy as np
from contextlib import ExitStack
import concourse.bass as bass
import concourse.mybir as mybir
from concourse.bass_utils import run_bass_kernel_spmd

F32 = mybir.dt.float32
BF16 = mybir.dt.bfloat16
AF = mybir.ActivationFunctionType
ALU = mybir.AluOpType
EPOCH = 30000

D = 1024
DB = 512
NH = 8
HD = 64
GW = 64
ROWS = 256
CTX = 256
DEPTH = 4
D_IN = 8704
A_Q, A_K, A_V, A_Z, B_B, B_C, B_V, B_Z, C_A, C_G, C_Z, GATES = [i * 512 for i in range(12)]
EPS = 1e-6
NEG = -30000.0
SROWS = 96
NPAIR = SROWS // 2
NST = NPAIR // 2
STOK = SROWS * GW
NTOK = STOK + CTX
TOP = 16
OFFS = [-3, -2, -1, 0, 1, 2, 3]
P_NG, P_BADA, P_GQ, P_GK, P_CSW, P_CCW, P_CCB, P_LNG, P_LNB = 0, 8, 32, 33, 34, 46, 170, 174, 178
NPAR = 182


class Buf:
    __slots__ = ("name", "t", "w", "r", "dsem", "dcnt", "sb")

    def __init__(self, name, t, sb):
        self.name = name; self.t = t; self.w = []; self.r = {}; self.dsem = None; self.dcnt = 0; self.sb = sb

    def __getitem__(self, k):
        return self.t[k]


class Prog:
    def __init__(self, nc):
        self.nc = nc
        self.es = ExitStack()
        self.streams = {k: [] for k in ("pe", "act", "dve", "pool", "sp")}
        self.cnt = {k: 0 for k in self.streams}
        self.known = {k: {} for k in self.streams}

    def sbuf(self, name, shape, dtype):
        return Buf(name, self.es.enter_context(self.nc.sbuf_tensor(name, list(shape), dtype)), True)

    def psum(self, name, shape, dtype=F32):
        return Buf(name, self.es.enter_context(self.nc.psum_tensor(name, list(shape), dtype)), True)

    def dram(self, name, shape, dtype, kind="Internal"):
        return Buf(name, self.nc.dram_tensor(name, list(shape), dtype, kind=kind), False)

    def track(self, name):
        return Buf(name, None, False)

    def _wait(self, e, ev):
        k, v = ev
        if self.known[e].get(k, 0) >= v:
            return
        self.known[e][k] = v
        self.streams[e].append(("w", k, v))

    def _deps(self, e, reads, writes, skip_self=False):
        for b in reads:
            for ev in b.w:
                if not (skip_self and ev[0][0] == e):
                    self._wait(e, ev)
        for b in writes:
            for ev in b.w:
                if not (skip_self and ev[0][0] == e):
                    self._wait(e, ev)
            for ev in b.r.values():
                if not (skip_self and ev[0][0] == e):
                    self._wait(e, ev)

    def op(self, e, fn, reads=(), writes=()):
        self._deps(e, reads, writes, skip_self=(e == "pe"))
        n = self.cnt[e]; self.cnt[e] += 1
        key = (e, n // EPOCH)
        ev = (key, n % EPOCH + 1)
        self.streams[e].append(("o", fn, key))
        for b in reads:
            b.r[e] = ev
        for b in writes:
            b.w = [ev]; b.r = {}
        return ev

    def dma(self, q, out_ap, in_ap, dsts, srcs, **kw):
        self._deps(q, srcs, dsts)
        owner = None
        for b in list(dsts) + list(srcs):
            if b.sb:
                owner = b; break
        if owner is None:
            owner = dsts[0]
        if owner.dsem is None:
            owner.dsem = {}; owner.dcnt = {}
        sw = (q == "pool")
        if sw not in owner.dsem:
            owner.dsem[sw] = ("d", owner.name, sw); owner.dcnt[sw] = 0
        owner.dcnt[sw] += 16
        ev = (owner.dsem[sw], owner.dcnt[sw])
        self.streams[q].append(("d", out_ap, in_ap, owner.dsem[sw], kw))
        for b in srcs:
            b.r["dma:" + owner.name] = ev
        for b in dsts:
            b.w = [ev]; b.r = {}
        return ev

    def emit(self, final_waits=()):
        nc = self.nc
        for ev in final_waits:
            self._wait("sp", ev)
        keys = []
        seen = set()
        for e, st in self.streams.items():
            for it in st:
                k = it[1] if it[0] == "w" else (it[2] if it[0] == "o" else it[3])
                if k not in seen:
                    seen.add(k); keys.append(k)
        sems = {k: self.es.enter_context(nc.semaphore("s%d" % i)) for i, k in enumerate(keys)}
        self.n_sems = len(sems)
        streams = self.streams
        with nc.Block() as block:
            def run(e):
                def f(eng):
                    for it in streams[e]:
                        if it[0] == "w":
                            eng.wait_ge(sems[it[1]], it[2])
                        elif it[0] == "o":
                            it[1](eng).then_inc(sems[it[2]], 1)
                        else:
                            eng.dma_start(out=it[1], in_=it[2], **it[4]).then_inc(sems[it[3]], 16)
                return f
            block.tensor(run("pe")); block.scalar(run("act")); block.vector(run("dve"))
            block.gpsimd(run("pool")); block.sync(run("sp"))
        self.es.close()


def layer_ranges(L):
    out = []
    for l in range(L):
        m = L - 1 - l
        o0, o1 = TOP - 4 * m, TOP + 64 + 3 * m
        k0, k1 = o0 - 4, o1 + 3
        out.append(((k0 // 2, (k1 + 1) // 2), (o0 // 2, (o1 + 1) // 2)))
    return out


def build_program(L=DEPTH, stop=None, bcut=99):
    nc = bass.Bass("TRN2", target_bir_lowering=False)
    P = Prog(nc)
    op, dma = P.op, P.dma

    x0 = P.dram("x0", [D, NTOK], F32, kind="ExternalInput")
    cond = P.dram("cond", [128, 16], F32, kind="ExternalInput")
    par = P.dram("par", [128, DEPTH * NPAR], F32, kind="ExternalInput")
    cst = P.dram("cst", [128, 512], F32, kind="ExternalInput")
    rm_in = P.dram("rm", [128, NPAIR * 14], F32, kind="ExternalInput")
    flg_in = P.dram("flg", [128, 2], F32, kind="ExternalInput")
    rbt = P.dram("rbt", [L, 128, 7 * NH * 128], F32, kind="ExternalInput")
    w_ada = P.dram("w_ada", [L, D, 3 * D], F32, kind="ExternalInput")
    w_in = P.dram("w_in", [L, D, D_IN], F32, kind="ExternalInput")
    w_oa = P.dram("w_out_a", [L, DB, D], F32, kind="ExternalInput")
    w_ob = P.dram("w_out_b", [L, DB, D], F32, kind="ExternalInput")
    w_oc = P.dram("w_out_c", [L, DB, D], F32, kind="ExternalInput")
    w_o = P.dram("w_o", [L, D, D], F32, kind="ExternalInput")
    y = P.dram("y", [D, 64 * GW], F32, kind="ExternalOutput")
    xs = [P.dram("xs0", [D, NTOK], F32), P.dram("xs1", [D, NTOK], F32)]
    wb_in = P.dram("wb_in", [L, D, D_IN], BF16)
    wb_o = P.dram("wb_o", [L, 2560, D], BF16)

    xtrk = {}

    def xt(bufid, st):
        k = (bufid, st)
        if k not in xtrk:
            xtrk[k] = P.track("xt_%s_%s" % k)
        return xtrk[k]

    cst_f = P.sbuf("cst_f", [128, 512], F32)
    ident = P.sbuf("ident", [128, 128], BF16)
    par_sb = P.sbuf("par_sb", [128, DEPTH * NPAR], F32)
    gk8 = P.sbuf("gk8", [128, DEPTH], F32)
    rm = P.sbuf("rm_sb", [128, NPAIR * 14], F32)
    flg = P.sbuf("flg_sb", [128, 2], F32)
    cnd = P.sbuf("cnd", [128, 16], F32)
    epsc = P.sbuf("epsc", [128, 4], F32)
    ada = P.sbuf("ada", [128, DEPTH, 24, 2], F32)
    gs = P.sbuf("gs", [128, DEPTH, 8, 2], F32)
    WR = [P.sbuf("WR%d" % i, [128, 8, 512], BF16) for i in range(3)]
    WO = P.sbuf("WO", [128, 20, D], BF16)
    RB = P.sbuf("RB", [128, 7, NH, 128], BF16)
    xa = P.sbuf("xa", [128, 8, 256], F32)
    xb = P.sbuf("xb", [128, 8, 256], F32)
    scr = P.sbuf("scr", [128, 4, 256], F32)
    wa = [xb, xa]
    tmpA = scr
    rstd = P.sbuf("rstd", [128, 256], F32)
    r2 = P.sbuf("r2", [128, 256], F32)
    hT = [P.sbuf("hT%d" % i, [128, 8, 256], BF16) for i in range(2)]
    hTc = hT[0]
    qT = [P.sbuf("qT%d" % i, [128, 4, 256], BF16) for i in range(2)]
    qTc = qT[0]
    kT = P.sbuf("kT", [128, 2, 4, 768], BF16)
    kTc = P.sbuf("kTc", [128, 2, 4, 256], BF16)
    v1 = P.sbuf("v1", [128, 6, NH, 65], BF16)
    v1c = P.sbuf("v1c", [128, 2, NH, 65], BF16)
    cg = P.sbuf("cg", [128, 4, 800], BF16)
    glu = P.sbuf("glu", [128, 4, 800], BF16)
    cgc = P.sbuf("cgc", [128, 4, 288], BF16)
    gluc = P.sbuf("gluc", [128, 4, 288], BF16)
    sq2 = [P.sbuf("sq2_%d" % i, [128, 256], F32) for i in range(2)]
    sAZ = P.sbuf("sAZ", [128, 2, 512], BF16)
    Sb = [P.sbuf("Sb%d" % i, [128, 512], F32) for i in range(2)]
    PT = P.sbuf("PT", [128, 8, 512], BF16)
    rec = P.sbuf("rec", [128, 4], F32)
    yat = P.sbuf("yat", [128, 512], BF16)
    yaT = P.sbuf("yaT", [128, 4, 256], BF16)
    ybT = P.sbuf("ybT", [128, 4, 256], BF16)
    ycT = P.sbuf("ycT", [128, 4, 256], BF16)
    mT = P.sbuf("mT", [128, 8, 256], BF16)
    sg = [P.sbuf("sg%d" % i, [128, 256], F32) for i in range(2)]
    acc = P.sbuf("acc", [128, 4, 256], F32)
    u = P.sbuf("u", [128, 4, 256], F32)
    gts = P.sbuf("gts", [128, 12, 256], BF16)
    mt = [P.sbuf("mt%d" % i, [128, 256], F32) for i in range(2)]

    zp = [P.psum("zp%d" % i, [128, 512]) for i in range(3)]
    stp = P.psum("stp", [128, 512])
    sp_ = [P.psum("sps%d" % i, [128, 512]) for i in range(2)]
    obp = P.psum("obp", [128, 512])
    trp = P.psum("trp", [128, 1024], BF16)
    zpi = [0]

    def nzp():
        zpi[0] = (zpi[0] + 1) % 3
        return zp[zpi[0]]

    ones_f = lambda: cst_f[:, 128:256]
    blk_f = lambda: cst_f[:, 256:384]
    o512_f = lambda: cst_f[:, 384:512]

    def mm(out, lhsT, rhs, start, stop, rd, wr):
        op("pe", lambda e: e.matmul(out, lhsT=lhsT, rhs=rhs, start=start, stop=stop), rd, wr)

    dma("sp", cst_f[:, :], cst[:, :], [cst_f], [cst])
    dma("sp", par_sb[:, :], par[:, :], [par_sb], [par])
    dma("sp", rm[:, :], rm_in[:, :], [rm], [rm_in])
    dma("sp", flg[:, :], flg_in[:, :], [flg], [flg_in])
    dma("sp", cnd[:, :], cond[:, :], [cnd], [cond])
    op("dve", lambda e: e.tensor_copy(out=ident[:, :], in_=cst_f[:, 0:128]), [cst_f], [ident])
    for i_, v_ in enumerate((D * EPS, HD * EPS, EPS)):
        op("dve", lambda e, i_=i_, v_=v_: e.memset(epsc[:, i_:i_ + 1], v_), [], [epsc])
    for b_ in (v1, v1c):
        op("dve", lambda e, b_=b_: e.memset(b_[:, :, :, :], 0.0), [], [b_])
        op("dve", lambda e, b_=b_: e.memset(b_[:, :, :, 64:65], 1.0), [], [b_])
    for b_ in (kT, kTc):
        op("dve", lambda e, b_=b_: e.memset(b_[:, :, :, :], 0.0), [], [b_])
    for b_ in (qT[0], qT[1]):
        op("dve", lambda e, b_=b_: e.memset(b_[:, :, :], 0.0), [], [b_])
    op("dve", lambda e: e.memset(cgc[:, :, :], 0.0), [], [cgc])
    op("dve", lambda e: e.memset(gluc[:, :, :], 0.0), [], [gluc])
    op("dve", lambda e: e.memset(cg[:, :, :], 0.0), [], [cg])
    op("dve", lambda e: e.memset(glu[:, :, :], 0.0), [], [glu])
    for l in range(L):
        op("dve", lambda e, l=l: e.tensor_scalar(out=gk8[:, l:l + 1], in0=par_sb[:, l * NPAR + P_GK:l * NPAR + P_GK + 1],
                                                  scalar1=8.0, scalar2=None, op0=ALU.mult), [par_sb], [gk8])
    op("act", lambda e: e.activation(out=cnd[:, :], in_=cnd[:, :], func=AF.Silu), [cnd], [cnd])

    wbt_in = [P.track("wbin%d" % l) for l in range(L)]
    wbt_o = [P.track("wbo%d" % l) for l in range(L)]
    wsrc = P.track("wsrc")

    def cast_layer(l):
        for r in range(8):
            dma("pool", wb_in[l, r * 128:(r + 1) * 128, :], w_in[l, r * 128:(r + 1) * 128, :], [wbt_in[l]], [wsrc])
        for i, wsrc_t in enumerate((w_oa, w_ob, w_oc)):
            for r in range(2):
                dma("pool", wb_o[l, i * 512 + r * 256:i * 512 + (r + 1) * 256, :], wsrc_t[l, r * 256:(r + 1) * 256, :],
                    [wbt_o[l]], [wsrc])
        for r in range(4):
            dma("pool", wb_o[l, 1536 + r * 256:1536 + (r + 1) * 256, :], w_o[l, r * 256:(r + 1) * 256, :], [wbt_o[l]], [wsrc])

    for l in range(L):
        cast_layer(l)

    wai = 0
    for l in range(L):
        for g in range(12):
            wbuf = wa[wai % 2]; wai += 1
            dma("sp", wbuf[:, :, :], w_ada[l, :, g * 256:(g + 1) * 256].rearrange("(k p) n -> p k n", p=128), [wbuf], [wsrc])
            for cc in range(2):
                ch = g * 2 + cc
                for kc in range(8):
                    mm(stp[:, ch * 2:ch * 2 + 2], wbuf[:, kc, cc * 128:(cc + 1) * 128], cnd[:, kc * 2:kc * 2 + 2],
                       kc == 0, kc == 7, [wbuf, cnd], [stp])
        for j in range(2):
            op("dve", lambda e, l=l, j=j: e.tensor_tensor(
                out=ada[:, l, :, j], in0=stp[:, j:48:2], in1=par_sb[:, l * NPAR + P_BADA:l * NPAR + P_BADA + 24], op=ALU.add),
               [stp, par_sb], [ada])
            op("dve", lambda e, l=l, j=j: e.scalar_tensor_tensor(
                out=gs[:, l, :, j], in0=ada[:, l, 8:16, j], scalar=1.0, in1=par_sb[:, l * NPAR + P_NG:l * NPAR + P_NG + 8],
                op0=ALU.add, op1=ALU.mult), [ada, par_sb], [gs])
            op("dve", lambda e, l=l, j=j: e.tensor_scalar(out=gs[:, l, :, j], in0=gs[:, l, :, j], scalar1=32.0, scalar2=None,
                                                           op0=ALU.mult), [gs], [gs])

    class WStream:
        def __init__(self):
            self.plan = []
            self.issued = 0
            self.used = 0

        def issue_upto(self, n):
            while self.issued < min(n, len(self.plan)):
                l, col0 = self.plan[self.issued]
                buf = WR[self.issued % 3]
                dma("sp", buf[:, :, :], wb_in[l, :, col0:col0 + 512].rearrange("(k p) n -> p k n", p=128), [buf], [wbt_in[l]])
                self.issued += 1

        def next(self):
            i = self.used
            self.issue_upto(i + 3)
            self.used += 1
            return WR[i % 3]

    ws = WStream()

    A_GROUPS = [B_C, B_V, C_A, C_G, A_V, A_K, A_Q]
    B_GROUPS = [A_Z, B_B, B_Z, C_Z] + [GATES + i * 512 for i in range(6)]

    def phase_a(l, j, p0, p1, ctx):
        cj = 1 if ctx else 0
        if ctx:
            t0, T, tl = STOK, CTX, 0
            h = hTc
        else:
            t0, T, tl = p0 * 128, (p1 - p0) * 128, (p0 % 2) * 128
            h = hT[j % 2]
        src = x0 if l == 0 else xs[l % 2]
        strk = xt("in" if l == 0 else l % 2, "c" if ctx else j)
        po = l * NPAR
        dma("sp", xa[:, :, 0:T], src[:, t0:t0 + T].rearrange("(c p) t -> p c t", p=128), [xa], [strk])
        for c in range(8):
            s2 = sq2[c % 2]
            op("act", lambda e, c=c, s2=s2: e.activation(out=s2[:, 0:T], in_=xa[:, c, 0:T], func=AF.Square), [xa], [s2])
            mm(stp[:, 0:T], ones_f(), s2[:, 0:T], c == 0, c == 7, [s2, cst_f], [stp])
        op("act", lambda e: e.activation(out=rstd[:, 0:T], in_=stp[:, 0:T], func=AF.Sqrt, bias=epsc[:, 0:1], scale=1.0), [stp, epsc], [rstd])
        op("dve", lambda e: e.reciprocal(out=rstd[:, 0:T], in_=rstd[:, 0:T]), [rstd], [rstd])
        for c in range(8):
            op("dve", lambda e, c=c: e.scalar_tensor_tensor(out=xa[:, c, 0:T], in0=xa[:, c, 0:T], scalar=gs[:, l, c, cj:cj + 1],
                                                            in1=rstd[:, 0:T], op0=ALU.mult, op1=ALU.mult), [xa, rstd, gs], [xa])
            op("act", lambda e, c=c: e.activation(out=h[:, c, tl:tl + T], in_=xa[:, c, 0:T], func=AF.Identity,
                                                   bias=ada[:, l, c, cj:cj + 1], scale=1.0), [xa, ada], [h])
        if ctx:
            qd, kd, cgd, gld = qTc, kTc, cgc, gluc
            qo, ko, co = 0, 0, 16
        else:
            qd, kd, cgd, gld = qT[j % 2], kT, cg, glu
            qo, ko, co = tl, (j % 3) * 256 + tl, 16 + (j % 3) * 256 + tl
        pend = []

        def flush(n):
            while len(pend) > n:
                pend.pop(0)()

        def postA(gcol, cc, z):
            if gcol in (A_Q, A_K):
                s2 = sq2[cc % 2]
                op("act", lambda e, z=z, s2=s2: e.activation(out=s2[:, 0:T], in_=z[:, 0:T], func=AF.Square), [z], [s2])
                mm(stp[:, 0:T], blk_f(), s2[:, 0:T], True, True, [s2, cst_f], [stp])
                op("act", lambda e: e.activation(out=r2[:, 0:T], in_=stp[:, 0:T], func=AF.Sqrt, bias=epsc[:, 1:2], scale=1.0), [stp, epsc], [r2])
                op("dve", lambda e: e.reciprocal(out=r2[:, 0:T], in_=r2[:, 0:T]), [r2], [r2])
                if gcol == A_Q:
                    op("dve", lambda e, z=z, cc=cc: e.scalar_tensor_tensor(
                        out=qd[:, cc, qo:qo + T], in0=z[:, 0:T], scalar=par_sb[:, po + P_GQ:po + P_GQ + 1], in1=r2[:, 0:T],
                        op0=ALU.mult, op1=ALU.mult), [z, r2, par_sb], [qd])
                else:
                    for hh in range(2):
                        ps_ = slice(hh * 64, (hh + 1) * 64)
                        op("dve", lambda e, z=z, cc=cc, hh=hh, ps_=ps_: e.scalar_tensor_tensor(
                            out=kd[ps_, hh, cc, ko:ko + T], in0=z[ps_, 0:T], scalar=gk8[ps_, l:l + 1], in1=r2[ps_, 0:T],
                            op0=ALU.mult, op1=ALU.mult), [z, r2, gk8], [kd])
            elif gcol in (B_C, C_A):
                op("act", lambda e, z=z, cc=cc: e.activation(out=tmpA[:, cc, 0:T], in_=z[:, 0:T], func=AF.Copy), [z], [tmpA])
            elif gcol == B_V:
                op("dve", lambda e, z=z, cc=cc: e.tensor_tensor(out=cgd[:, cc, co:co + T], in0=z[:, 0:T], in1=tmpA[:, cc, 0:T],
                                                                 op=ALU.mult), [z, tmpA], [cgd])
            elif gcol == C_G:
                s2 = sq2[cc % 2]
                op("act", lambda e, z=z, s2=s2: e.activation(out=s2[:, 0:T], in_=z[:, 0:T], func=AF.Sigmoid), [z], [s2])
                op("dve", lambda e, s2=s2, cc=cc: e.tensor_tensor(out=gld[:, cc, co:co + T], in0=s2[:, 0:T], in1=tmpA[:, cc, 0:T],
                                                                   op=ALU.mult), [s2, tmpA], [gld])

        for gcol in A_GROUPS:
            wg = ws.next()
            if gcol == A_V:
                flush(0)
                for pl in range(T // 128):
                    z = nzp()
                    for kc in range(8):
                        mm(z[:, :], h[:, kc, tl + pl * 128:tl + (pl + 1) * 128], wg[:, kc, :], kc == 0, kc == 7, [h, wg], [z])
                    if ctx:
                        vdst, vb = v1c[:, pl, :, 0:64], v1c
                    else:
                        vdst, vb = v1[:, (p0 + pl) % 6, :, 0:64], v1
                    op("act", lambda e, z=z, vdst=vdst: e.activation(out=vdst, in_=z[:, :].rearrange("p (h d) -> p h d", d=64),
                                                                      func=AF.Copy), [z], [vb])
                continue
            for cc in range(4):
                z = nzp()
                for kc in range(8):
                    mm(z[:, 0:T], wg[:, kc, cc * 128:(cc + 1) * 128], h[:, kc, tl:tl + T], kc == 0, kc == 7, [h, wg], [z])
                pend.append(lambda gcol=gcol, cc=cc, z=z: postA(gcol, cc, z))
                flush(1)
        flush(0)
        if not ctx:
            for fj, fi in ((TOP // 4 - 1, 0), ((TOP + 64) // 4, 1)):
                if j == fj:
                    for b_ in (cg, glu):
                        op("pool", lambda e, b_=b_, fi=fi: e.tensor_scalar(
                            out=b_[:, :, co:co + T], in0=b_[:, :, co:co + T], scalar1=flg[:, fi:fi + 1], scalar2=None, op0=ALU.mult),
                           [b_, flg], [b_])
            if j % 3 == 2:
                for b_ in (cg, glu):
                    op("pool", lambda e, b_=b_: e.tensor_copy(out=b_[:, :, 0:16], in_=b_[:, :, 768:784]), [b_], [b_])
            if j % 3 == 0:
                for b_ in (cg, glu):
                    op("pool", lambda e, b_=b_: e.tensor_copy(out=b_[:, :, 784:800], in_=b_[:, :, 16:32]), [b_], [b_])

    def phase_b(l, j, p0, p1, ctx, last):
        cj = 1 if ctx else 0
        po = l * NPAR
        if ctx:
            t0, T, tl = STOK, CTX, 0
            h, qd, cgd, gld, co = hTc, qTc, cgc, gluc, 16
            npl = 2
        else:
            t0, T, tl = p0 * 128, (p1 - p0) * 128, (p0 % 2) * 128
            h, qd, cgd, gld, co = hT[j % 2], qT[j % 2], cg, glu, 16 + (j % 3) * 256 + tl
            npl = p1 - p0
        src = x0 if l == 0 else xs[l % 2]
        strk = xt("in" if l == 0 else l % 2, "c" if ctx else j)
        dma("sp", xb[:, :, 0:T], src[:, t0:t0 + T].rearrange("(c p) t -> p c t", p=128), [xb], [strk])

        conv_th = conf_conv_thunks(l, T, gld, co)
        n_units = 2 * npl
        per_unit = (len(conv_th) + n_units - 1) // n_units
        for gcol in B_GROUPS[:4]:
            wg = ws.next()
            if gcol == A_Z:
                for pl in range(npl):
                    z = nzp()
                    for kc in range(8):
                        mm(z[:, :], h[:, kc, tl + pl * 128:tl + (pl + 1) * 128], wg[:, kc, :], kc == 0, kc == 7, [h, wg], [z])
                    op("act", lambda e, z=z, pl=pl: e.activation(out=sAZ[:, pl, :], in_=z[:, :], func=AF.Silu), [z], [sAZ])
                if bcut <= 1:
                    return
                for pl in range(npl):
                    pi = p0 + pl
                    if ctx:
                        chunks = [("c", 0), ("c", 1)]
                    else:
                        offs = [-2, -1, 0, 1, 2]
                        if pi == TOP // 2:
                            offs.append(3)
                        if pi == (TOP + 64) // 2 - 1:
                            offs.insert(0, -3)
                        chunks = [("l", o) for o in offs] + [("c", 0), ("c", 1)]
                    qs = tl + pl * 128
                    for half in range(2):
                        for ci, (kind, o) in enumerate(chunks):
                            s_ps = sp_[ci % 2]
                            for hh in range(4):
                                hd = half * 4 + hh
                                cc = hd // 2
                                if kind == "c":
                                    kap, kb = kTc[:, hd % 2, cc, o * 128:(o + 1) * 128], kTc
                                else:
                                    ks = ((pi + o) % 6) * 128
                                    kap, kb = kT[:, hd % 2, cc, ks:ks + 128], kT
                                mm(s_ps[:, hh * 128:(hh + 1) * 128], kap, qd[:, cc, qs:qs + 128], True, True, [kb, qd], [s_ps])
                            if kind == "c":
                                op("act", lambda e, s_ps=s_ps, ci=ci: e.activation(out=PT[:, ci, :], in_=s_ps[:, :], func=AF.Exp),
                                   [s_ps], [PT])
                            else:
                                sb = Sb[ci % 2]
                                oi = o + 3
                                op("dve", lambda e, s_ps=s_ps, sb=sb, oi=oi, half=half: e.tensor_tensor(
                                    out=sb[:, :].rearrange("p (h q) -> p h q", q=128), in0=s_ps[:, :].rearrange("p (h q) -> p h q", q=128),
                                    in1=RB[:, oi, half * 4:half * 4 + 4, :], op=ALU.add), [s_ps, RB], [sb])
                                for q2 in range(2):
                                    ri = pi * 14 + oi * 2 + q2
                                    op("act", lambda e, sb=sb, ci=ci, q2=q2, ri=ri: e.activation(
                                        out=PT[:, ci, :].rearrange("p (h q) -> p h q", q=128)[:, :, q2 * 64:(q2 + 1) * 64],
                                        in_=sb[:, :].rearrange("p (h q) -> p h q", q=128)[:, :, q2 * 64:(q2 + 1) * 64],
                                        func=AF.Exp, bias=rm[:, ri:ri + 1], scale=1.0), [sb, rm], [PT])
                        nch = len(chunks)
                        if bcut <= 2:
                            return
                        for hh in range(4):
                            hd = half * 4 + hh
                            for ci, (kind, o) in enumerate(chunks):
                                if kind == "c":
                                    vap, vb_ = v1c[:, o, hd, :], v1c
                                else:
                                    vap, vb_ = v1[:, (pi + o) % 6, hd, :], v1
                                mm(obp[:, hh * 65:(hh + 1) * 65], PT[:, ci, hh * 128:(hh + 1) * 128], vap, ci == 0, ci == nch - 1,
                                   [PT, vb_], [obp])
                        op("dve", lambda e: e.reciprocal(out=rec[:, 0:4], in_=obp[:, 64:260:65]), [obp], [rec])
                        for hh in range(4):
                            hd = half * 4 + hh
                            op("dve", lambda e, hh=hh, hd=hd, pl=pl: e.scalar_tensor_tensor(
                                out=yat[:, hd * 64:(hd + 1) * 64], in0=obp[:, hh * 65:hh * 65 + 64], scalar=rec[:, hh:hh + 1],
                                in1=sAZ[:, pl, hd * 64:(hd + 1) * 64], op0=ALU.mult, op1=ALU.mult), [obp, rec, sAZ], [yat])
                        for t_ in conv_th[:per_unit]:
                            t_()
                        del conv_th[:per_unit]
                    if bcut <= 3:
                        return
                    for c in range(4):
                        op("pe", lambda e, c=c: e.transpose(out=trp[:, c * 128:(c + 1) * 128], in_=yat[:, c * 128:(c + 1) * 128],
                                                            identity=ident[:, :]), [yat, ident], [trp])
                    op("act", lambda e, pl=pl: e.activation(out=yaT[:, :, pl * 128:(pl + 1) * 128],
                                                            in_=trp[:, 0:512].rearrange("p (c t) -> p c t", t=128), func=AF.Copy),
                       [trp], [yaT])
                if bcut <= 4:
                    return
                continue
            if gcol == C_Z and bcut <= 5:
                return
            if gcol == C_Z:
                conformer(l, T, gld, co, conv_th)
            for cc in range(4):
                z = nzp()
                for kc in range(8):
                    mm(z[:, 0:T], wg[:, kc, cc * 128:(cc + 1) * 128], h[:, kc, tl:tl + T], kc == 0, kc == 7, [h, wg], [z])
                if gcol == B_B:
                    wofs = po + P_CSW + cc * 3
                    op("dve", lambda e, cc=cc, wofs=wofs: e.tensor_scalar(
                        out=acc[:, cc, 0:T], in0=cgd[:, cc, co - 1:co - 1 + T], scalar1=par_sb[:, wofs:wofs + 1], scalar2=None,
                        op0=ALU.mult), [cgd, par_sb], [acc])
                    for k in (1, 2):
                        op("dve", lambda e, cc=cc, wofs=wofs, k=k: e.scalar_tensor_tensor(
                            out=acc[:, cc, 0:T], in0=cgd[:, cc, co - 1 + k:co - 1 + k + T], scalar=par_sb[:, wofs + k:wofs + k + 1],
                            in1=acc[:, cc, 0:T], op0=ALU.mult, op1=ALU.add), [cgd, par_sb, acc], [acc])
                    op("dve", lambda e, cc=cc, z=z: e.tensor_tensor(out=acc[:, cc, 0:T], in0=z[:, 0:T], in1=acc[:, cc, 0:T], op=ALU.mult),
                       [z, acc], [acc])
                elif gcol == B_Z:
                    s_ = sg[cc % 2]
                    op("act", lambda e, z=z, s_=s_: e.activation(out=s_[:, 0:T], in_=z[:, 0:T], func=AF.Silu), [z], [s_])
                    op("dve", lambda e, cc=cc, s_=s_: e.tensor_tensor(out=ybT[:, cc, 0:T], in0=acc[:, cc, 0:T], in1=s_[:, 0:T], op=ALU.mult),
                       [acc, s_], [ybT])
                elif gcol == C_Z:
                    s_ = sg[cc % 2]
                    op("act", lambda e, z=z, s_=s_: e.activation(out=s_[:, 0:T], in_=z[:, 0:T], func=AF.Silu), [z], [s_])
                    op("dve", lambda e, cc=cc, s_=s_: e.tensor_tensor(out=ycT[:, cc, 0:T], in0=u[:, cc, 0:T], in1=s_[:, 0:T], op=ALU.mult),
                       [u, s_], [ycT])
        if bcut <= 6:
            return
        for F in range(2):
            for br in range(3):
                wg = ws.next()
                for fi in range(4):
                    z = nzp()
                    for kc in range(8):
                        mm(z[:, 0:T], wg[:, kc, fi * 128:(fi + 1) * 128], h[:, kc, tl:tl + T], kc == 0, kc == 7, [h, wg], [z])
                    op("act", lambda e, z=z, br=br, fi=fi: e.activation(out=gts[:, br * 4 + fi, 0:T], in_=z[:, 0:T], func=AF.Sigmoid),
                       [z], [gts])
            for fi in range(4):
                f = F * 4 + fi
                zs = []
                for br, ysrc in enumerate((yaT, ybT, ycT)):
                    z = nzp(); zs.append(z)
                    for kc in range(4):
                        mm(z[:, 0:T], WO[:, br * 4 + kc, f * 128:(f + 1) * 128], ysrc[:, kc, 0:T], kc == 0, kc == 3, [WO, ysrc], [z])
                m0, m1, m2 = (mt[0], mt[1], sg[0]) if fi % 2 == 0 else (sq2[0], sq2[1], sg[1])
                op("dve", lambda e, z=zs[0], fi=fi, m0=m0: e.tensor_tensor(out=m0[:, 0:T], in0=z[:, 0:T], in1=gts[:, fi, 0:T], op=ALU.mult),
                   [zs[0], gts], [m0])
                op("dve", lambda e, z=zs[1], fi=fi, m1=m1: e.tensor_tensor(out=m1[:, 0:T], in0=z[:, 0:T], in1=gts[:, 4 + fi, 0:T], op=ALU.mult),
                   [zs[1], gts], [m1])
                op("dve", lambda e, z=zs[2], fi=fi, m2=m2: e.tensor_tensor(out=m2[:, 0:T], in0=z[:, 0:T], in1=gts[:, 8 + fi, 0:T], op=ALU.mult),
                   [zs[2], gts], [m2])
                op("pool", lambda e, m0=m0, m1=m1: e.tensor_tensor(out=m0[:, 0:T], in0=m0[:, 0:T], in1=m1[:, 0:T], op=ALU.add), [m0, m1], [m0])
                op("pool", lambda e, f=f, m0=m0, m2=m2: e.tensor_tensor(out=mT[:, f, 0:T], in0=m0[:, 0:T], in1=m2[:, 0:T], op=ALU.add),
                   [m0, m2], [mT])
        if bcut <= 7:
            return
        for f2 in range(8):
            z = nzp()
            for f in range(8):
                mm(z[:, 0:T], WO[:, 12 + f, f2 * 128:(f2 + 1) * 128], mT[:, f, 0:T], f == 0, f == 7, [WO, mT], [z])
            op("dve", lambda e, z=z, f2=f2: e.scalar_tensor_tensor(
                out=xb[:, f2, 0:T], in0=z[:, 0:T], scalar=ada[:, l, 16 + f2, cj:cj + 1], in1=xb[:, f2, 0:T], op0=ALU.mult, op1=ALU.add),
               [z, ada, xb], [xb])
        if last:
            lo = max(p0, TOP // 2); hi = min(p1, (TOP + 64) // 2)
            if hi > lo:
                a0 = (lo - p0) * 128; n = (hi - lo) * 128
                yo = (lo - TOP // 2) * 128
                ev = dma("pool", y[:, yo:yo + n].rearrange("(c p) t -> p c t", p=128), xb[:, :, a0:a0 + n], [ytrk], [xb])
                yevs.append(ev)
        else:
            dtrk = xt((l + 1) % 2, "c" if ctx else j)
            dma("pool", xs[(l + 1) % 2][:, t0:t0 + T].rearrange("(c p) t -> p c t", p=128), xb[:, :, 0:T], [dtrk], [xb])

    ytrk = P.track("ytrk")
    yevs = []

    def conf_conv_thunks(l, T, gld, co):
        po = l * NPAR
        th = []
        for k in range(31):
            for cc in range(4):
                wofs = po + P_CCW + cc * 31 + k
                src_ap = gld[:, cc, co - 15 + k:co - 15 + k + T]
                if k == 0:
                    th.append(lambda cc=cc, wofs=wofs, src_ap=src_ap: op("dve", lambda e: e.tensor_scalar(
                        out=u[:, cc, 0:T], in0=src_ap, scalar1=par_sb[:, wofs:wofs + 1],
                        scalar2=par_sb[:, po + P_CCB + cc:po + P_CCB + cc + 1], op0=ALU.mult, op1=ALU.add), [gld, par_sb], [u]))
                else:
                    th.append(lambda cc=cc, wofs=wofs, src_ap=src_ap: op("dve", lambda e: e.scalar_tensor_tensor(
                        out=u[:, cc, 0:T], in0=src_ap, scalar=par_sb[:, wofs:wofs + 1], in1=u[:, cc, 0:T],
                        op0=ALU.mult, op1=ALU.add), [gld, par_sb, u], [u]))
        return th

    def conformer(l, T, gld, co, pending):
        po = l * NPAR
        for t_ in pending:
            t_()
        del pending[:]
        for cc in range(4):
            mm(stp[:, 0:T], o512_f(), u[:, cc, 0:T], cc == 0, cc == 3, [u, cst_f], [stp])
        for cc in range(4):
            op("dve", lambda e, cc=cc: e.tensor_tensor(out=u[:, cc, 0:T], in0=u[:, cc, 0:T], in1=stp[:, 0:T], op=ALU.subtract),
               [u, stp], [u])
        op("act", lambda e: e.activation(out=scr[:, 0:4, 0:T], in_=u[:, :, 0:T], func=AF.Square), [u], [scr])
        for cc in range(4):
            mm(stp[:, 0:T], o512_f(), scr[:, cc, 0:T], cc == 0, cc == 3, [scr, cst_f], [stp])
        op("act", lambda e: e.activation(out=r2[:, 0:T], in_=stp[:, 0:T], func=AF.Sqrt, bias=epsc[:, 2:3], scale=1.0), [stp, epsc], [r2])
        op("dve", lambda e: e.reciprocal(out=r2[:, 0:T], in_=r2[:, 0:T]), [r2], [r2])
        for cc in range(4):
            op("dve", lambda e, cc=cc: e.tensor_tensor(out=u[:, cc, 0:T], in0=u[:, cc, 0:T], in1=r2[:, 0:T], op=ALU.mult), [u, r2], [u])
            op("act", lambda e, cc=cc: e.activation(out=u[:, cc, 0:T], in_=u[:, cc, 0:T], func=AF.Silu,
                                                     bias=par_sb[:, po + P_LNB + cc:po + P_LNB + cc + 1],
                                                     scale=par_sb[:, po + P_LNG + cc:po + P_LNG + cc + 1]), [u, par_sb], [u])

    rng = layer_ranges(L)
    for l in range(L):
        (k0, k1), (o0, o1) = rng[l]
        last = (l == L - 1)
        seq = [("A", "c")]
        if not last:
            seq.append(("B", "c"))
        jsA = list(range(k0 // 2, (k1 + 1) // 2))
        jsB = list(range(o0 // 2, (o1 + 1) // 2))
        seq.append(("A", jsA[0]))
        for j in jsA[1:]:
            seq.append(("A", j))
            if j - 1 in jsB:
                seq.append(("B", j - 1))
        if jsA[-1] in jsB:
            seq.append(("B", jsA[-1]))
        for ph, j in seq:
            if ph == "A":
                ws.plan += [(l, c) for c in A_GROUPS]
            else:
                ws.plan += [(l, c) for c in B_GROUPS[:4]]
                for F in range(2):
                    for br in range(3):
                        ws.plan.append((l, GATES + br * 1024 + F * 512))
        rng[l] = (rng[l][0], rng[l][1], seq)

    nph = [0]
    for l in range(L):
        (k0, k1), (o0, o1), seq = rng[l]
        last = (l == L - 1)
        dma("sp", WO[:, :, :], wb_o[l, :, :].rearrange("(k p) n -> p k n", p=128), [WO], [wbt_o[l]])
        dma("pool", RB[:, :, :, :], rbt[l, :, :].rearrange("p (o h q) -> p o h q", o=7, h=NH), [RB], [wsrc])
        for ph, j in seq:
            if stop is not None and nph[0] >= stop:
                break
            nph[0] += 1
            ctx = (j == "c")
            if ctx:
                p0, p1 = 0, 2
            elif ph == "A":
                p0, p1 = max(2 * j, k0), min(2 * j + 2, k1)
            else:
                p0, p1 = max(2 * j, o0), min(2 * j + 2, o1)
            if ph == "A":
                phase_a(l, j, p0, p1, ctx)
            else:
                phase_b(l, j, p0, p1, ctx, last)

    if not yevs:
        yevs.append(dma("pool", y[:, 0:256].rearrange("(c p) t -> p c t", p=128), xb[:, :, 0:256], [ytrk], [xb]))
    P.emit(yevs)
    return nc, P


def _consts():
    c = np.zeros((128, 512), np.float32)
    c[:, 0:128] = np.eye(128, dtype=np.float32)
    c[:, 128:256] = 1.0
    c[0:64, 256:320] = 1.0
    c[64:128, 320:384] = 1.0
    c[:, 384:512] = 1.0 / 512.0
    return c


def _rb_tables(rpb):
    kc = np.arange(GW)[:, None]
    qc = np.arange(GW)[None, :]
    cs = np.clip(qc - 8, 0, GW - 16)
    colv = (kc >= cs) & (kc < cs + 16)
    dcol = np.clip(kc - qc, -15, 15) + 15
    out = np.full((DEPTH, 2, GW, 7, NH, 2, GW), NEG, np.float32)
    for oi, o in enumerate(OFFS):
        for kr2 in range(2):
            for qr2 in range(2):
                d = 2 * o + kr2 - qr2 + 7
                if 0 <= d <= 14:
                    blk = rpb[:, :, d, :][:, :, dcol]
                    blk = np.where(colv[None, None], blk, np.float32(NEG))
                    out[:, kr2, :, oi, :, qr2, :] = np.transpose(blk, (0, 2, 1, 3))
    return np.ascontiguousarray(out.reshape(DEPTH, 128, 7 * NH * 128))


def _row_masks(a):
    m = np.full((2, NPAIR, 7, 2), NEG, np.float32)
    for pi in range(NPAIR):
        for q2 in range(2):
            qg = a - TOP + 2 * pi + q2
            if qg < 0 or qg >= ROWS:
                continue
            rs = min(max(qg - 4, 0), ROWS - 8)
            for oi, o in enumerate(OFFS):
                for k2 in range(2):
                    kg = a - TOP + 2 * (pi + o) + k2
                    if rs <= kg < rs + 8:
                        m[k2, pi, oi, q2] = 0.0
    return np.ascontiguousarray(np.repeat(m, 64, axis=0).reshape(128, NPAIR * 14))


def _params(inp):
    p = np.zeros((128, DEPTH, NPAR), np.float32)
    for l in range(DEPTH):
        p[:, l, P_NG:P_NG + 8] = inp["norm_g"][l].reshape(8, 128).T
        p[:, l, P_BADA:P_BADA + 24] = inp["b_ada"][l].reshape(24, 128).T
        p[:, l, P_GQ] = np.tile(inp["q_norm_g"][l], 2)
        p[:, l, P_GK] = np.tile(inp["k_norm_g"][l], 2)
        p[:, l, P_CSW:P_CSW + 12] = inp["conv_short_w"][l].reshape(3, 4, 128).transpose(2, 1, 0).reshape(128, 12)
        p[:, l, P_CCW:P_CCW + 124] = inp["conv_conf_w"][l].reshape(31, 4, 128).transpose(2, 1, 0).reshape(128, 124)
        p[:, l, P_CCB:P_CCB + 4] = inp["conv_conf_b"][l].reshape(4, 128).T
        p[:, l, P_LNG:P_LNG + 4] = inp["ln_conf_g"][l].reshape(4, 128).T
        p[:, l, P_LNB:P_LNB + 4] = inp["ln_conf_b"][l].reshape(4, 128).T
    return np.ascontiguousarray(p.reshape(128, DEPTH * NPAR))


_CACHE = {}


def make_in_maps(inp, L=DEPTH):
    inp = {k: np.asarray(v, dtype=np.float32) for k, v in inp.items()}
    x = inp["x"].reshape(2, ROWS, GW, D)
    shared = dict(par=_params(inp), cst=_consts(), rbt=np.ascontiguousarray(_rb_tables(inp["rpb"])[:L]),
                  w_ada=np.ascontiguousarray(inp["w_ada"][:L]), w_in=np.ascontiguousarray(inp["w_in"][:L]),
                  w_out_a=np.ascontiguousarray(inp["w_out_a"][:L]), w_out_b=np.ascontiguousarray(inp["w_out_b"][:L]),
                  w_out_c=np.ascontiguousarray(inp["w_out_c"][:L]), w_o=np.ascontiguousarray(inp["w_o"][:L]))
    maps = []
    for c in range(8):
        b, a = c // 4, 64 * (c % 4)
        slab = np.zeros((SROWS, GW, D), np.float32)
        g0, g1 = max(a - TOP, 0), min(a - TOP + SROWS, ROWS)
        slab[g0 - (a - TOP):g1 - (a - TOP)] = x[b, g0:g1]
        x0 = np.empty((D, NTOK), np.float32)
        x0[:, :STOK] = slab.reshape(STOK, D).T
        x0[:, STOK:] = inp["ctx"][b].T
        cond = np.stack([inp["c"][b].reshape(8, 128).T, inp["c_ctx"].reshape(8, 128).T], axis=-1).reshape(128, 16)
        flg = np.zeros((128, 2), np.float32)
        flg[:, 0] = 0.0 if a == 0 else 1.0
        flg[:, 1] = 0.0 if a + 64 == ROWS else 1.0
        m = dict(shared)
        m.update(x0=x0, cond=np.ascontiguousarray(cond), rm=_row_masks(a), flg=flg)
        maps.append(m)
    return maps


def assemble(results):
    out = np.empty((2, ROWS, GW, D), np.float32)
    for c in range(8):
        b, a = c // 4, 64 * (c % 4)
        out[b, a:a + 64] = results[c]["y"].T.reshape(64, GW, D)
    return out.reshape(2, ROWS * GW, D)


def kernel(**inputs):
    if "nc" not in _CACHE:
        _CACHE["nc"] = build_program(DEPTH)[0]
    nc = _CACHE["nc"]
    maps = make_in_maps(inputs)
    res = run_bass_kernel_spmd(nc, maps, core_ids=list(range(8)))
    return assemble(res.results)
```

```python
import numpy as np
from contextlib import ExitStack
import concourse.bass as bass
import concourse.mybir as mybir
from concourse.bass_utils import run_bass_kernel_spmd

F32 = mybir.dt.float32
BF16 = mybir.dt.bfloat16
AF = mybir.ActivationFunctionType
ALU = mybir.AluOpType
EPOCH = 30000

D = 1024
DB = 512
NH = 8
HD = 64
GW = 64
ROWS = 256
CTX = 256
DEPTH = 4
D_IN = 8704
A_Q, A_K, A_V, A_Z, B_B, B_C, B_V, B_Z, C_A, C_G, C_Z, GATES = [i * 512 for i in range(12)]
EPS = 1e-6
NEG = -30000.0
SROWS = 96
NPAIR = SROWS // 2
NST = NPAIR // 2
STOK = SROWS * GW
NTOK = STOK + CTX
TOP = 16
OFFS = [-3, -2, -1, 0, 1, 2, 3]
P_NG, P_BADA, P_GQ, P_GK, P_CSW, P_CCW, P_CCB, P_LNG, P_LNB = 0, 8, 32, 33, 34, 46, 170, 174, 178
NPAR = 182


class Buf:
    __slots__ = ("name", "t", "w", "r", "dsem", "dcnt", "sb")

    def __init__(self, name, t, sb):
        self.name = name; self.t = t; self.w = []; self.r = {}; self.dsem = None; self.dcnt = 0; self.sb = sb

    def __getitem__(self, k):
        return self.t[k]


class Prog:
    def __init__(self, nc):
        self.nc = nc
        self.es = ExitStack()
        self.streams = {k: [] for k in ("pe", "act", "dve", "pool", "sp")}
        self.cnt = {k: 0 for k in self.streams}
        self.known = {k: {} for k in self.streams}

    def sbuf(self, name, shape, dtype):
        return Buf(name, self.es.enter_context(self.nc.sbuf_tensor(name, list(shape), dtype)), True)

    def psum(self, name, shape, dtype=F32):
        return Buf(name, self.es.enter_context(self.nc.psum_tensor(name, list(shape), dtype)), True)

    def dram(self, name, shape, dtype, kind="Internal"):
        return Buf(name, self.nc.dram_tensor(name, list(shape), dtype, kind=kind), False)

    def track(self, name):
        return Buf(name, None, False)

    def _wait(self, e, ev):
        k, v = ev
        if self.known[e].get(k, 0) >= v:
            return
        self.known[e][k] = v
        self.streams[e].append(("w", k, v))

    def _deps(self, e, reads, writes, skip_self=False):
        skip_ww = skip_self or e in ("dve", "act")
        for b in reads:
            for ev in b.w:
                if not (skip_self and ev[0][0] == e):
                    self._wait(e, ev)
        for b in writes:
            for ev in b.w:
                if not (skip_ww and ev[0][0] == e):
                    self._wait(e, ev)
            for ev in b.r.values():
                if not (skip_ww and ev[0][0] == e):
                    self._wait(e, ev)

    def op(self, e, fn, reads=(), writes=()):
        self._deps(e, reads, writes, skip_self=(e == "pe"))
        n = self.cnt[e]; self.cnt[e] += 1
        key = (e, n // EPOCH)
        ev = (key, n % EPOCH + 1)
        self.streams[e].append(("o", fn, key))
        for b in reads:
            b.r[e] = ev
        for b in writes:
            b.w = [ev]; b.r = {}
        return ev

    def dma(self, q, out_ap, in_ap, dsts, srcs, **kw):
        self._deps(q, srcs, dsts)
        owner = None
        for b in list(dsts) + list(srcs):
            if b.sb:
                owner = b; break
        if owner is None:
            owner = dsts[0]
        if owner.dsem is None:
            owner.dsem = {}; owner.dcnt = {}
        sw = (q == "pool")
        if sw not in owner.dsem:
            owner.dsem[sw] = ("d", owner.name, sw); owner.dcnt[sw] = 0
        owner.dcnt[sw] += 16
        ev = (owner.dsem[sw], owner.dcnt[sw])
        self.streams[q].append(("d", out_ap, in_ap, owner.dsem[sw], kw))
        for b in srcs:
            b.r["dma:" + owner.name] = ev
        for b in dsts:
            b.w = [ev]; b.r = {}
        return ev

    def emit(self, final_waits=()):
        nc = self.nc
        for ev in final_waits:
            self._wait("sp", ev)
        keys = []
        seen = set()
        for e, st in self.streams.items():
            for it in st:
                k = it[1] if it[0] == "w" else (it[2] if it[0] == "o" else it[3])
                if k not in seen:
                    seen.add(k); keys.append(k)
        sems = {k: self.es.enter_context(nc.semaphore("s%d" % i)) for i, k in enumerate(keys)}
        self.n_sems = len(sems)
        streams = self.streams
        with nc.Block() as block:
            def run(e):
                def f(eng):
                    for it in streams[e]:
                        if it[0] == "w":
                            eng.wait_ge(sems[it[1]], it[2])
                        elif it[0] == "o":
                            it[1](eng).then_inc(sems[it[2]], 1)
                        else:
                            eng.dma_start(out=it[1], in_=it[2], **it[4]).then_inc(sems[it[3]], 16)
                return f
            block.tensor(run("pe")); block.scalar(run("act")); block.vector(run("dve"))
            block.gpsimd(run("pool")); block.sync(run("sp"))
        self.es.close()


def layer_ranges(L):
    out = []
    for l in range(L):
        m = L - 1 - l
        o0, o1 = TOP - 4 * m, TOP + 64 + 3 * m
        k0, k1 = o0 - 4, o1 + 3
        out.append(((k0 // 2, (k1 + 1) // 2), (o0 // 2, (o1 + 1) // 2)))
    return out


def build_program(L=DEPTH, stop=None, bcut=99):
    nc = bass.Bass("TRN2", target_bir_lowering=False)
    P = Prog(nc)
    op, dma = P.op, P.dma

    x0 = P.dram("x0", [D, NTOK], F32, kind="ExternalInput")
    cond = P.dram("cond", [128, 16], F32, kind="ExternalInput")
    par = P.dram("par", [128, DEPTH * NPAR], F32, kind="ExternalInput")
    cst = P.dram("cst", [128, 512], F32, kind="ExternalInput")
    rm_in = P.dram("rm", [128, NPAIR * 14], F32, kind="ExternalInput")
    flg_in = P.dram("flg", [128, 2], F32, kind="ExternalInput")
    rbt = P.dram("rbt", [L, 128, 7 * NH * 128], F32, kind="ExternalInput")
    w_ada = P.dram("w_ada", [L, D, 3 * D], F32, kind="ExternalInput")
    w_in = P.dram("w_in", [L, D, D_IN], F32, kind="ExternalInput")
    w_oa = P.dram("w_out_a", [L, DB, D], F32, kind="ExternalInput")
    w_ob = P.dram("w_out_b", [L, DB, D], F32, kind="ExternalInput")
    w_oc = P.dram("w_out_c", [L, DB, D], F32, kind="ExternalInput")
    w_o = P.dram("w_o", [L, D, D], F32, kind="ExternalInput")
    y = P.dram("y", [D, 64 * GW], F32, kind="ExternalOutput")
    xs = [P.dram("xs0", [D, NTOK], F32), P.dram("xs1", [D, NTOK], F32)]
    wb_in = P.dram("wb_in", [L, D, D_IN], BF16)
    wb_o = P.dram("wb_o", [L, 2560, D], BF16)

    xtrk = {}

    def xt(bufid, st):
        k = (bufid, st)
        if k not in xtrk:
            xtrk[k] = P.track("xt_%s_%s" % k)
        return xtrk[k]

    cst_f = P.sbuf("cst_f", [128, 512], F32)
    ident = P.sbuf("ident", [128, 128], BF16)
    par_sb = P.sbuf("par_sb", [128, DEPTH * NPAR], F32)
    gk8 = P.sbuf("gk8", [128, DEPTH], F32)
    rm = P.sbuf("rm_sb", [128, NPAIR * 14], F32)
    flg = P.sbuf("flg_sb", [128, 2], F32)
    cnd = P.sbuf("cnd", [128, 16], F32)
    epsc = P.sbuf("epsc", [128, 4], F32)
    ada = P.sbuf("ada", [128, DEPTH, 24, 2], F32)
    gs = P.sbuf("gs", [128, DEPTH, 8, 2], F32)
    WR = [P.sbuf("WR%d" % i, [128, 8, 512], BF16) for i in range(3)]
    WO = P.sbuf("WO", [128, 20, D], BF16)
    RB = P.sbuf("RB", [128, 7, NH, 128], BF16)
    xa = P.sbuf("xa", [128, 8, 256], F32)
    xb = P.sbuf("xb", [128, 8, 256], F32)
    scr = P.sbuf("scr", [128, 4, 256], F32)
    wa = [xb, xa]
    tmpA = scr
    rstd = P.sbuf("rstd", [128, 256], F32)
    r2 = P.sbuf("r2", [128, 256], F32)
    hT = [P.sbuf("hT%d" % i, [128, 8, 256], BF16) for i in range(2)]
    hTc = hT[0]
    qT = [P.sbuf("qT%d" % i, [128, 4, 256], BF16) for i in range(2)]
    qTc = qT[0]
    kT = P.sbuf("kT", [128, 2, 4, 768], BF16)
    kTc = P.sbuf("kTc", [128, 2, 4, 256], BF16)
    v1 = P.sbuf("v1", [128, 6, NH, 65], BF16)
    v1c = P.sbuf("v1c", [128, 2, NH, 65], BF16)
    cg = P.sbuf("cg", [128, 4, 800], BF16)
    glu = P.sbuf("glu", [128, 4, 800], BF16)
    cgc = P.sbuf("cgc", [128, 4, 288], BF16)
    gluc = P.sbuf("gluc", [128, 4, 288], BF16)
    sq2 = [P.sbuf("sq2_%d" % i, [128, 256], F32) for i in range(2)]
    sAZ = P.sbuf("sAZ", [128, 2, 512], BF16)
    Sb = [P.sbuf("Sb%d" % i, [128, 512], F32) for i in range(2)]
    PT = P.sbuf("PT", [128, 8, 512], BF16)
    rec = P.sbuf("rec", [128, 4], F32)
    yat = P.sbuf("yat", [128, 512], BF16)
    yaT = P.sbuf("yaT", [128, 4, 256], BF16)
    ybT = P.sbuf("ybT", [128, 4, 256], BF16)
    ycT = P.sbuf("ycT", [128, 4, 256], BF16)
    mT = P.sbuf("mT", [128, 8, 256], BF16)
    sg = [P.sbuf("sg%d" % i, [128, 256], F32) for i in range(2)]
    acc = P.sbuf("acc", [128, 4, 256], F32)
    u = P.sbuf("u", [128, 4, 256], F32)
    gts = P.sbuf("gts", [128, 12, 256], BF16)
    mt = [P.sbuf("mt%d" % i, [128, 256], F32) for i in range(2)]

    zp = [P.psum("zp%d" % i, [128, 512]) for i in range(3)]
    stp = P.psum("stp", [128, 512])
    sp_ = [P.psum("sps%d" % i, [128, 512]) for i in range(2)]
    obp = P.psum("obp", [128, 512])
    trp = P.psum("trp", [128, 1024], BF16)
    zpi = [0]

    def nzp():
        zpi[0] = (zpi[0] + 1) % 3
        return zp[zpi[0]]

    ones_f = lambda: cst_f[:, 128:256]
    blk_f = lambda: cst_f[:, 256:384]
    o512_f = lambda: cst_f[:, 384:512]

    def mm(out, lhsT, rhs, start, stop, rd, wr):
        op("pe", lambda e: e.matmul(out, lhsT=lhsT, rhs=rhs, start=start, stop=stop), rd, wr)

    dma("sp", cst_f[:, :], cst[:, :], [cst_f], [cst])
    dma("sp", par_sb[:, :], par[:, :], [par_sb], [par])
    dma("sp", rm[:, :], rm_in[:, :], [rm], [rm_in])
    dma("sp", flg[:, :], flg_in[:, :], [flg], [flg_in])
    dma("sp", cnd[:, :], cond[:, :], [cnd], [cond])
    op("dve", lambda e: e.tensor_copy(out=ident[:, :], in_=cst_f[:, 0:128]), [cst_f], [ident])
    for i_, v_ in enumerate((D * EPS, HD * EPS, EPS)):
        op("dve", lambda e, i_=i_, v_=v_: e.memset(epsc[:, i_:i_ + 1], v_), [], [epsc])
    for b_ in (v1, v1c):
        op("dve", lambda e, b_=b_: e.memset(b_[:, :, :, :], 0.0), [], [b_])
        op("dve", lambda e, b_=b_: e.memset(b_[:, :, :, 64:65], 1.0), [], [b_])
    for b_ in (kT, kTc):
        op("dve", lambda e, b_=b_: e.memset(b_[:, :, :, :], 0.0), [], [b_])
    for b_ in (qT[0], qT[1]):
        op("dve", lambda e, b_=b_: e.memset(b_[:, :, :], 0.0), [], [b_])
    op("dve", lambda e: e.memset(cgc[:, :, :], 0.0), [], [cgc])
    op("dve", lambda e: e.memset(gluc[:, :, :], 0.0), [], [gluc])
    op("dve", lambda e: e.memset(cg[:, :, :], 0.0), [], [cg])
    op("dve", lambda e: e.memset(glu[:, :, :], 0.0), [], [glu])
    for l in range(L):
        op("dve", lambda e, l=l: e.tensor_scalar(out=gk8[:, l:l + 1], in0=par_sb[:, l * NPAR + P_GK:l * NPAR + P_GK + 1],
                                                  scalar1=8.0, scalar2=None, op0=ALU.mult), [par_sb], [gk8])
    op("act", lambda e: e.activation(out=cnd[:, :], in_=cnd[:, :], func=AF.Silu), [cnd], [cnd])

    wbt_in = [P.track("wbin%d" % l) for l in range(L)]
    wbt_o = [P.track("wbo%d" % l) for l in range(L)]
    wsrc = P.track("wsrc")

    def cast_layer(l):
        for r in range(8):
            dma("pool", wb_in[l, r * 128:(r + 1) * 128, :], w_in[l, r * 128:(r + 1) * 128, :], [wbt_in[l]], [wsrc])
        for i, wsrc_t in enumerate((w_oa, w_ob, w_oc)):
            for r in range(2):
                dma("pool", wb_o[l, i * 512 + r * 256:i * 512 + (r + 1) * 256, :], wsrc_t[l, r * 256:(r + 1) * 256, :],
                    [wbt_o[l]], [wsrc])
        for r in range(4):
            dma("pool", wb_o[l, 1536 + r * 256:1536 + (r + 1) * 256, :], w_o[l, r * 256:(r + 1) * 256, :], [wbt_o[l]], [wsrc])

    for l in range(L):
        cast_layer(l)

    wai = 0
    for l in range(L):
        for g in range(12):
            wbuf = wa[wai % 2]; wai += 1
            dma("sp", wbuf[:, :, :], w_ada[l, :, g * 256:(g + 1) * 256].rearrange("(k p) n -> p k n", p=128), [wbuf], [wsrc])
            for cc in range(2):
                ch = g * 2 + cc
                for kc in range(8):
                    mm(stp[:, ch * 2:ch * 2 + 2], wbuf[:, kc, cc * 128:(cc + 1) * 128], cnd[:, kc * 2:kc * 2 + 2],
                       kc == 0, kc == 7, [wbuf, cnd], [stp])
        for j in range(2):
            op("dve", lambda e, l=l, j=j: e.tensor_tensor(
                out=ada[:, l, :, j], in0=stp[:, j:48:2], in1=par_sb[:, l * NPAR + P_BADA:l * NPAR + P_BADA + 24], op=ALU.add),
               [stp, par_sb], [ada])
            op("dve", lambda e, l=l, j=j: e.scalar_tensor_tensor(
                out=gs[:, l, :, j], in0=ada[:, l, 8:16, j], scalar=1.0, in1=par_sb[:, l * NPAR + P_NG:l * NPAR + P_NG + 8],
                op0=ALU.add, op1=ALU.mult), [ada, par_sb], [gs])
            op("dve", lambda e, l=l, j=j: e.tensor_scalar(out=gs[:, l, :, j], in0=gs[:, l, :, j], scalar1=32.0, scalar2=None,
                                                           op0=ALU.mult), [gs], [gs])

    class WStream:
        def __init__(self):
            self.plan = []
            self.issued = 0
            self.used = 0

        def issue_upto(self, n):
            while self.issued < min(n, len(self.plan)):
                l, col0 = self.plan[self.issued]
                buf = WR[self.issued % 3]
                dma("sp", buf[:, :, :], wb_in[l, :, col0:col0 + 512].rearrange("(k p) n -> p k n", p=128), [buf], [wbt_in[l]])
                self.issued += 1

        def next(self):
            i = self.used
            self.issue_upto(i + 3)
            self.used += 1
            return WR[i % 3]

    ws = WStream()

    A_GROUPS = [B_C, B_V, C_A, C_G, A_V, A_K, A_Q]
    B_GROUPS = [A_Z, B_B, B_Z, C_Z] + [GATES + i * 512 for i in range(6)]

    def phase_a(l, j, p0, p1, ctx):
        cj = 1 if ctx else 0
        if ctx:
            t0, T, tl = STOK, CTX, 0
            h = hTc
        else:
            t0, T, tl = p0 * 128, (p1 - p0) * 128, (p0 % 2) * 128
            h = hT[j % 2]
        src = x0 if l == 0 else xs[l % 2]
        strk = xt("in" if l == 0 else l % 2, "c" if ctx else j)
        po = l * NPAR
        dma("sp", xa[:, :, 0:T], src[:, t0:t0 + T].rearrange("(c p) t -> p c t", p=128), [xa], [strk])
        for c in range(8):
            s2 = sq2[c % 2]
            op("act", lambda e, c=c, s2=s2: e.activation(out=s2[:, 0:T], in_=xa[:, c, 0:T], func=AF.Square), [xa], [s2])
            mm(stp[:, 0:T], ones_f(), s2[:, 0:T], c == 0, c == 7, [s2, cst_f], [stp])
        op("act", lambda e: e.activation(out=rstd[:, 0:T], in_=stp[:, 0:T], func=AF.Sqrt, bias=epsc[:, 0:1], scale=1.0), [stp, epsc], [rstd])
        op("dve", lambda e: e.reciprocal(out=rstd[:, 0:T], in_=rstd[:, 0:T]), [rstd], [rstd])
        for c in range(8):
            op("dve", lambda e, c=c: e.scalar_tensor_tensor(out=xa[:, c, 0:T], in0=xa[:, c, 0:T], scalar=gs[:, l, c, cj:cj + 1],
                                                            in1=rstd[:, 0:T], op0=ALU.mult, op1=ALU.mult), [xa, rstd, gs], [xa])
            op("act", lambda e, c=c: e.activation(out=h[:, c, tl:tl + T], in_=xa[:, c, 0:T], func=AF.Identity,
                                                   bias=ada[:, l, c, cj:cj + 1], scale=1.0), [xa, ada], [h])
        if ctx:
            qd, kd, cgd, gld = qTc, kTc, cgc, gluc
            qo, ko, co = 0, 0, 16
        else:
            qd, kd, cgd, gld = qT[j % 2], kT, cg, glu
            qo, ko, co = tl, (j % 3) * 256 + tl, 16 + (j % 3) * 256 + tl
        pend = []

        def flush(n):
            while len(pend) > n:
                pend.pop(0)()

        def postA(gcol, cc, z):
            if gcol in (A_Q, A_K):
                s2 = sq2[cc % 2]
                op("act", lambda e, z=z, s2=s2: e.activation(out=s2[:, 0:T], in_=z[:, 0:T], func=AF.Square), [z], [s2])
                mm(stp[:, 0:T], blk_f(), s2[:, 0:T], True, True, [s2, cst_f], [stp])
                op("act", lambda e: e.activation(out=r2[:, 0:T], in_=stp[:, 0:T], func=AF.Sqrt, bias=epsc[:, 1:2], scale=1.0), [stp, epsc], [r2])
                op("dve", lambda e: e.reciprocal(out=r2[:, 0:T], in_=r2[:, 0:T]), [r2], [r2])
                if gcol == A_Q:
                    op("dve", lambda e, z=z, cc=cc: e.scalar_tensor_tensor(
                        out=qd[:, cc, qo:qo + T], in0=z[:, 0:T], scalar=par_sb[:, po + P_GQ:po + P_GQ + 1], in1=r2[:, 0:T],
                        op0=ALU.mult, op1=ALU.mult), [z, r2, par_sb], [qd])
                else:
                    for hh in range(2):
                        ps_ = slice(hh * 64, (hh + 1) * 64)
                        op("dve", lambda e, z=z, cc=cc, hh=hh, ps_=ps_: e.scalar_tensor_tensor(
                            out=kd[ps_, hh, cc, ko:ko + T], in0=z[ps_, 0:T], scalar=gk8[ps_, l:l + 1], in1=r2[ps_, 0:T],
                            op0=ALU.mult, op1=ALU.mult), [z, r2, gk8], [kd])
            elif gcol in (B_C, C_A):
                op("act", lambda e, z=z, cc=cc: e.activation(out=tmpA[:, cc, 0:T], in_=z[:, 0:T], func=AF.Copy), [z], [tmpA])
            elif gcol == B_V:
                op("dve", lambda e, z=z, cc=cc: e.tensor_tensor(out=cgd[:, cc, co:co + T], in0=z[:, 0:T], in1=tmpA[:, cc, 0:T],
                                                                 op=ALU.mult), [z, tmpA], [cgd])
            elif gcol == C_G:
                s2 = sq2[cc % 2]
                op("act", lambda e, z=z, s2=s2: e.activation(out=s2[:, 0:T], in_=z[:, 0:T], func=AF.Sigmoid), [z], [s2])
                op("dve", lambda e, s2=s2, cc=cc: e.tensor_tensor(out=gld[:, cc, co:co + T], in0=s2[:, 0:T], in1=tmpA[:, cc, 0:T],
                                                                   op=ALU.mult), [s2, tmpA], [gld])

        for gcol in A_GROUPS:
            wg = ws.next()
            if gcol == A_V:
                flush(0)
                for pl in range(T // 128):
                    z = nzp()
                    for kc in range(8):
                        mm(z[:, :], h[:, kc, tl + pl * 128:tl + (pl + 1) * 128], wg[:, kc, :], kc == 0, kc == 7, [h, wg], [z])
                    if ctx:
                        vdst, vb = v1c[:, pl, :, 0:64], v1c
                    else:
                        vdst, vb = v1[:, (p0 + pl) % 6, :, 0:64], v1
                    op("act", lambda e, z=z, vdst=vdst: e.activation(out=vdst, in_=z[:, :].rearrange("p (h d) -> p h d", d=64),
                                                                      func=AF.Copy), [z], [vb])
                continue
            for cc in range(4):
                z = nzp()
                for kc in range(8):
                    mm(z[:, 0:T], wg[:, kc, cc * 128:(cc + 1) * 128], h[:, kc, tl:tl + T], kc == 0, kc == 7, [h, wg], [z])
                pend.append(lambda gcol=gcol, cc=cc, z=z: postA(gcol, cc, z))
                flush(1)
        flush(0)
        if not ctx:
            for fj, fi in ((TOP // 4 - 1, 0), ((TOP + 64) // 4, 1)):
                if j == fj:
                    for b_ in (cg, glu):
                        op("pool", lambda e, b_=b_, fi=fi: e.tensor_scalar(
                            out=b_[:, :, co:co + T], in0=b_[:, :, co:co + T], scalar1=flg[:, fi:fi + 1], scalar2=None, op0=ALU.mult),
                           [b_, flg], [b_])
            if j % 3 == 2:
                for b_ in (cg, glu):
                    op("pool", lambda e, b_=b_: e.tensor_copy(out=b_[:, :, 0:16], in_=b_[:, :, 768:784]), [b_], [b_])
            if j % 3 == 0:
                for b_ in (cg, glu):
                    op("pool", lambda e, b_=b_: e.tensor_copy(out=b_[:, :, 784:800], in_=b_[:, :, 16:32]), [b_], [b_])

    def phase_b(l, j, p0, p1, ctx, last):
        cj = 1 if ctx else 0
        po = l * NPAR
        if ctx:
            t0, T, tl = STOK, CTX, 0
            h, qd, cgd, gld, co = hTc, qTc, cgc, gluc, 16
            npl = 2
        else:
            t0, T, tl = p0 * 128, (p1 - p0) * 128, (p0 % 2) * 128
            h, qd, cgd, gld, co = hT[j % 2], qT[j % 2], cg, glu, 16 + (j % 3) * 256 + tl
            npl = p1 - p0
        src = x0 if l == 0 else xs[l % 2]
        strk = xt("in" if l == 0 else l % 2, "c" if ctx else j)
        dma("sp", xb[:, :, 0:T], src[:, t0:t0 + T].rearrange("(c p) t -> p c t", p=128), [xb], [strk])

        conv_th = conf_conv_thunks(l, T, gld, co)
        n_units = 2 * npl
        per_unit = (len(conv_th) + n_units - 1) // n_units
        for gcol in B_GROUPS[:4]:
            wg = ws.next()
            if gcol == A_Z:
                for pl in range(npl):
                    z = nzp()
                    for kc in range(8):
                        mm(z[:, :], h[:, kc, tl + pl * 128:tl + (pl + 1) * 128], wg[:, kc, :], kc == 0, kc == 7, [h, wg], [z])
                    op("act", lambda e, z=z, pl=pl: e.activation(out=sAZ[:, pl, :], in_=z[:, :], func=AF.Silu), [z], [sAZ])
                if bcut <= 1:
                    return
                for pl in range(npl):
                    pi = p0 + pl
                    if ctx:
                        chunks = [("c", 0), ("c", 1)]
                    else:
                        offs = [-2, -1, 0, 1, 2]
                        if pi == TOP // 2:
                            offs.append(3)
                        if pi == (TOP + 64) // 2 - 1:
                            offs.insert(0, -3)
                        chunks = [("l", o) for o in offs] + [("c", 0), ("c", 1)]
                    qs = tl + pl * 128
                    for half in range(2):
                        for ci, (kind, o) in enumerate(chunks):
                            s_ps = sp_[ci % 2]
                            for hh in range(4):
                                hd = half * 4 + hh
                                cc = hd // 2
                                if kind == "c":
                                    kap, kb = kTc[:, hd % 2, cc, o * 128:(o + 1) * 128], kTc
                                else:
                                    ks = ((pi + o) % 6) * 128
                                    kap, kb = kT[:, hd % 2, cc, ks:ks + 128], kT
                                mm(s_ps[:, hh * 128:(hh + 1) * 128], kap, qd[:, cc, qs:qs + 128], True, True, [kb, qd], [s_ps])
                            if kind == "c":
                                op("act", lambda e, s_ps=s_ps, ci=ci: e.activation(out=PT[:, ci, :], in_=s_ps[:, :], func=AF.Exp),
                                   [s_ps], [PT])
                            else:
                                sb = Sb[ci % 2]
                                oi = o + 3
                                op("dve", lambda e, s_ps=s_ps, sb=sb, oi=oi, half=half: e.tensor_tensor(
                                    out=sb[:, :].rearrange("p (h q) -> p h q", q=128), in0=s_ps[:, :].rearrange("p (h q) -> p h q", q=128),
                                    in1=RB[:, oi, half * 4:half * 4 + 4, :], op=ALU.add), [s_ps, RB], [sb])
                                for q2 in range(2):
                                    ri = pi * 14 + oi * 2 + q2
                                    op("act", lambda e, sb=sb, ci=ci, q2=q2, ri=ri: e.activation(
                                        out=PT[:, ci, :].rearrange("p (h q) -> p h q", q=128)[:, :, q2 * 64:(q2 + 1) * 64],
                                        in_=sb[:, :].rearrange("p (h q) -> p h q", q=128)[:, :, q2 * 64:(q2 + 1) * 64],
                                        func=AF.Exp, bias=rm[:, ri:ri + 1], scale=1.0), [sb, rm], [PT])
                        nch = len(chunks)
                        if bcut <= 2:
                            return
                        for hh in range(4):
                            hd = half * 4 + hh
                            for ci, (kind, o) in enumerate(chunks):
                                if kind == "c":
                                    vap, vb_ = v1c[:, o, hd, :], v1c
                                else:
                                    vap, vb_ = v1[:, (pi + o) % 6, hd, :], v1
                                mm(obp[:, hh * 65:(hh + 1) * 65], PT[:, ci, hh * 128:(hh + 1) * 128], vap, ci == 0, ci == nch - 1,
                                   [PT, vb_], [obp])
                        op("dve", lambda e: e.reciprocal(out=rec[:, 0:4], in_=obp[:, 64:260:65]), [obp], [rec])
                        for hh in range(4):
                            hd = half * 4 + hh
                            op("dve", lambda e, hh=hh, hd=hd, pl=pl: e.scalar_tensor_tensor(
                                out=yat[:, hd * 64:(hd + 1) * 64], in0=obp[:, hh * 65:hh * 65 + 64], scalar=rec[:, hh:hh + 1],
                                in1=sAZ[:, pl, hd * 64:(hd + 1) * 64], op0=ALU.mult, op1=ALU.mult), [obp, rec, sAZ], [yat])
                        for t_ in conv_th[:per_unit]:
                            t_()
                        del conv_th[:per_unit]
                    if bcut <= 3:
                        return
                    for c in range(4):
                        op("pe", lambda e, c=c: e.transpose(out=trp[:, c * 128:(c + 1) * 128], in_=yat[:, c * 128:(c + 1) * 128],
                                                            identity=ident[:, :]), [yat, ident], [trp])
                    op("act", lambda e, pl=pl: e.activation(out=yaT[:, :, pl * 128:(pl + 1) * 128],
                                                            in_=trp[:, 0:512].rearrange("p (c t) -> p c t", t=128), func=AF.Copy),
                       [trp], [yaT])
                if bcut <= 4:
                    return
                continue
            if gcol == C_Z and bcut <= 5:
                return
            if gcol == C_Z:
                conformer(l, T, gld, co, conv_th)
            for cc in range(4):
                z = nzp()
                for kc in range(8):
                    mm(z[:, 0:T], wg[:, kc, cc * 128:(cc + 1) * 128], h[:, kc, tl:tl + T], kc == 0, kc == 7, [h, wg], [z])
                if gcol == B_B:
                    wofs = po + P_CSW + cc * 3
                    op("dve", lambda e, cc=cc, wofs=wofs: e.tensor_scalar(
                        out=acc[:, cc, 0:T], in0=cgd[:, cc, co - 1:co - 1 + T], scalar1=par_sb[:, wofs:wofs + 1], scalar2=None,
                        op0=ALU.mult), [cgd, par_sb], [acc])
                    for k in (1, 2):
                        op("dve", lambda e, cc=cc, wofs=wofs, k=k: e.scalar_tensor_tensor(
                            out=acc[:, cc, 0:T], in0=cgd[:, cc, co - 1 + k:co - 1 + k + T], scalar=par_sb[:, wofs + k:wofs + k + 1],
                            in1=acc[:, cc, 0:T], op0=ALU.mult, op1=ALU.add), [cgd, par_sb, acc], [acc])
                    op("dve", lambda e, cc=cc, z=z: e.tensor_tensor(out=acc[:, cc, 0:T], in0=z[:, 0:T], in1=acc[:, cc, 0:T], op=ALU.mult),
                       [z, acc], [acc])
                elif gcol == B_Z:
                    s_ = sg[cc % 2]
                    op("act", lambda e, z=z, s_=s_: e.activation(out=s_[:, 0:T], in_=z[:, 0:T], func=AF.Silu), [z], [s_])
                    op("dve", lambda e, cc=cc, s_=s_: e.tensor_tensor(out=ybT[:, cc, 0:T], in0=acc[:, cc, 0:T], in1=s_[:, 0:T], op=ALU.mult),
                       [acc, s_], [ybT])
                elif gcol == C_Z:
                    s_ = sg[cc % 2]
                    op("act", lambda e, z=z, s_=s_: e.activation(out=s_[:, 0:T], in_=z[:, 0:T], func=AF.Silu), [z], [s_])
                    op("dve", lambda e, cc=cc, s_=s_: e.tensor_tensor(out=ycT[:, cc, 0:T], in0=u[:, cc, 0:T], in1=s_[:, 0:T], op=ALU.mult),
                       [u, s_], [ycT])
        if bcut <= 6:
            return
        for F in range(2):
            for br in range(3):
                wg = ws.next()
                for fi in range(4):
                    z = nzp()
                    for kc in range(8):
                        mm(z[:, 0:T], wg[:, kc, fi * 128:(fi + 1) * 128], h[:, kc, tl:tl + T], kc == 0, kc == 7, [h, wg], [z])
                    op("act", lambda e, z=z, br=br, fi=fi: e.activation(out=gts[:, br * 4 + fi, 0:T], in_=z[:, 0:T], func=AF.Sigmoid),
                       [z], [gts])
            for fi in range(4):
                f = F * 4 + fi
                zs = []
                for br, ysrc in enumerate((yaT, ybT, ycT)):
                    z = nzp(); zs.append(z)
                    for kc in range(4):
                        mm(z[:, 0:T], WO[:, br * 4 + kc, f * 128:(f + 1) * 128], ysrc[:, kc, 0:T], kc == 0, kc == 3, [WO, ysrc], [z])
                m0, m1, m2 = (mt[0], mt[1], sg[0]) if fi % 2 == 0 else (sq2[0], sq2[1], sg[1])
                op("dve", lambda e, z=zs[0], fi=fi, m0=m0: e.tensor_tensor(out=m0[:, 0:T], in0=z[:, 0:T], in1=gts[:, fi, 0:T], op=ALU.mult),
                   [zs[0], gts], [m0])
                op("dve", lambda e, z=zs[1], fi=fi, m1=m1: e.tensor_tensor(out=m1[:, 0:T], in0=z[:, 0:T], in1=gts[:, 4 + fi, 0:T], op=ALU.mult),
                   [zs[1], gts], [m1])
                op("dve", lambda e, z=zs[2], fi=fi, m2=m2: e.tensor_tensor(out=m2[:, 0:T], in0=z[:, 0:T], in1=gts[:, 8 + fi, 0:T], op=ALU.mult),
                   [zs[2], gts], [m2])
                op("pool", lambda e, m0=m0, m1=m1: e.tensor_tensor(out=m0[:, 0:T], in0=m0[:, 0:T], in1=m1[:, 0:T], op=ALU.add), [m0, m1], [m0])
                op("pool", lambda e, f=f, m0=m0, m2=m2: e.tensor_tensor(out=mT[:, f, 0:T], in0=m0[:, 0:T], in1=m2[:, 0:T], op=ALU.add),
                   [m0, m2], [mT])
        if bcut <= 7:
            return
        for f2 in range(8):
            z = nzp()
            for f in range(8):
                mm(z[:, 0:T], WO[:, 12 + f, f2 * 128:(f2 + 1) * 128], mT[:, f, 0:T], f == 0, f == 7, [WO, mT], [z])
            op("dve", lambda e, z=z, f2=f2: e.scalar_tensor_tensor(
                out=xb[:, f2, 0:T], in0=z[:, 0:T], scalar=ada[:, l, 16 + f2, cj:cj + 1], in1=xb[:, f2, 0:T], op0=ALU.mult, op1=ALU.add),
               [z, ada, xb], [xb])
        if last:
            lo = max(p0, TOP // 2); hi = min(p1, (TOP + 64) // 2)
            if hi > lo:
                a0 = (lo - p0) * 128; n = (hi - lo) * 128
                yo = (lo - TOP // 2) * 128
                ev = dma("pool", y[:, yo:yo + n].rearrange("(c p) t -> p c t", p=128), xb[:, :, a0:a0 + n], [ytrk], [xb])
                yevs.append(ev)
        else:
            dtrk = xt((l + 1) % 2, "c" if ctx else j)
            dma("pool", xs[(l + 1) % 2][:, t0:t0 + T].rearrange("(c p) t -> p c t", p=128), xb[:, :, 0:T], [dtrk], [xb])

    ytrk = P.track("ytrk")
    yevs = []

    def conf_conv_thunks(l, T, gld, co):
        po = l * NPAR
        th = []
        for k in range(31):
            for cc in range(4):
                wofs = po + P_CCW + cc * 31 + k
                src_ap = gld[:, cc, co - 15 + k:co - 15 + k + T]
                if k == 0:
                    th.append(lambda cc=cc, wofs=wofs, src_ap=src_ap: op("dve", lambda e: e.tensor_scalar(
                        out=u[:, cc, 0:T], in0=src_ap, scalar1=par_sb[:, wofs:wofs + 1],
                        scalar2=par_sb[:, po + P_CCB + cc:po + P_CCB + cc + 1], op0=ALU.mult, op1=ALU.add), [gld, par_sb], [u]))
                else:
                    th.append(lambda cc=cc, wofs=wofs, src_ap=src_ap: op("dve", lambda e: e.scalar_tensor_tensor(
                        out=u[:, cc, 0:T], in0=src_ap, scalar=par_sb[:, wofs:wofs + 1], in1=u[:, cc, 0:T],
                        op0=ALU.mult, op1=ALU.add), [gld, par_sb, u], [u]))
        return th

    def conformer(l, T, gld, co, pending):
        po = l * NPAR
        for t_ in pending:
            t_()
        del pending[:]
        for cc in range(4):
            mm(stp[:, 0:T], o512_f(), u[:, cc, 0:T], cc == 0, cc == 3, [u, cst_f], [stp])
        for cc in range(4):
            op("dve", lambda e, cc=cc: e.tensor_tensor(out=u[:, cc, 0:T], in0=u[:, cc, 0:T], in1=stp[:, 0:T], op=ALU.subtract),
               [u, stp], [u])
        op("act", lambda e: e.activation(out=scr[:, 0:4, 0:T], in_=u[:, :, 0:T], func=AF.Square), [u], [scr])
        for cc in range(4):
            mm(stp[:, 0:T], o512_f(), scr[:, cc, 0:T], cc == 0, cc == 3, [scr, cst_f], [stp])
        op("act", lambda e: e.activation(out=r2[:, 0:T], in_=stp[:, 0:T], func=AF.Sqrt, bias=epsc[:, 2:3], scale=1.0), [stp, epsc], [r2])
        op("dve", lambda e: e.reciprocal(out=r2[:, 0:T], in_=r2[:, 0:T]), [r2], [r2])
        for cc in range(4):
            op("dve", lambda e, cc=cc: e.tensor_tensor(out=u[:, cc, 0:T], in0=u[:, cc, 0:T], in1=r2[:, 0:T], op=ALU.mult), [u, r2], [u])
            op("act", lambda e, cc=cc: e.activation(out=u[:, cc, 0:T], in_=u[:, cc, 0:T], func=AF.Silu,
                                                     bias=par_sb[:, po + P_LNB + cc:po + P_LNB + cc + 1],
                                                     scale=par_sb[:, po + P_LNG + cc:po + P_LNG + cc + 1]), [u, par_sb], [u])

    rng = layer_ranges(L)
    for l in range(L):
        (k0, k1), (o0, o1) = rng[l]
        last = (l == L - 1)
        seq = [("A", "c")]
        if not last:
            seq.append(("B", "c"))
        jsA = list(range(k0 // 2, (k1 + 1) // 2))
        jsB = list(range(o0 // 2, (o1 + 1) // 2))
        seq.append(("A", jsA[0]))
        for j in jsA[1:]:
            seq.append(("A", j))
            if j - 1 in jsB:
                seq.append(("B", j - 1))
        if jsA[-1] in jsB:
            seq.append(("B", jsA[-1]))
        for ph, j in seq:
            if ph == "A":
                ws.plan += [(l, c) for c in A_GROUPS]
            else:
                ws.plan += [(l, c) for c in B_GROUPS[:4]]
                for F in range(2):
                    for br in range(3):
                        ws.plan.append((l, GATES + br * 1024 + F * 512))
        rng[l] = (rng[l][0], rng[l][1], seq)

    nph = [0]
    for l in range(L):
        (k0, k1), (o0, o1), seq = rng[l]
        last = (l == L - 1)
        dma("sp", WO[:, :, :], wb_o[l, :, :].rearrange("(k p) n -> p k n", p=128), [WO], [wbt_o[l]])
        dma("pool", RB[:, :, :, :], rbt[l, :, :].rearrange("p (o h q) -> p o h q", o=7, h=NH), [RB], [wsrc])
        for ph, j in seq:
            if stop is not None and nph[0] >= stop:
                break
            nph[0] += 1
            ctx = (j == "c")
            if ctx:
                p0, p1 = 0, 2
            elif ph == "A":
                p0, p1 = max(2 * j, k0), min(2 * j + 2, k1)
            else:
                p0, p1 = max(2 * j, o0), min(2 * j + 2, o1)
            if ph == "A":
                phase_a(l, j, p0, p1, ctx)
            else:
                phase_b(l, j, p0, p1, ctx, last)

    if not yevs:
        yevs.append(dma("pool", y[:, 0:256].rearrange("(c p) t -> p c t", p=128), xb[:, :, 0:256], [ytrk], [xb]))
    P.emit(yevs)
    return nc, P


def _consts():
    c = np.zeros((128, 512), np.float32)
    c[:, 0:128] = np.eye(128, dtype=np.float32)
    c[:, 128:256] = 1.0
    c[0:64, 256:320] = 1.0
    c[64:128, 320:384] = 1.0
    c[:, 384:512] = 1.0 / 512.0
    return c


def _rb_tables(rpb):
    kc = np.arange(GW)[:, None]
    qc = np.arange(GW)[None, :]
    cs = np.clip(qc - 8, 0, GW - 16)
    colv = (kc >= cs) & (kc < cs + 16)
    dcol = np.clip(kc - qc, -15, 15) + 15
    out = np.full((DEPTH, 2, GW, 7, NH, 2, GW), NEG, np.float32)
    for oi, o in enumerate(OFFS):
        for kr2 in range(2):
            for qr2 in range(2):
                d = 2 * o + kr2 - qr2 + 7
                if 0 <= d <= 14:
                    blk = rpb[:, :, d, :][:, :, dcol]
                    blk = np.where(colv[None, None], blk, np.float32(NEG))
                    out[:, kr2, :, oi, :, qr2, :] = np.transpose(blk, (0, 2, 1, 3))
    return np.ascontiguousarray(out.reshape(DEPTH, 128, 7 * NH * 128))


def _row_masks(a):
    m = np.full((2, NPAIR, 7, 2), NEG, np.float32)
    for pi in range(NPAIR):
        for q2 in range(2):
            qg = a - TOP + 2 * pi + q2
            if qg < 0 or qg >= ROWS:
                continue
            rs = min(max(qg - 4, 0), ROWS - 8)
            for oi, o in enumerate(OFFS):
                for k2 in range(2):
                    kg = a - TOP + 2 * (pi + o) + k2
                    if rs <= kg < rs + 8:
                        m[k2, pi, oi, q2] = 0.0
    return np.ascontiguousarray(np.repeat(m, 64, axis=0).reshape(128, NPAIR * 14))


def _params(inp):
    p = np.zeros((128, DEPTH, NPAR), np.float32)
    for l in range(DEPTH):
        p[:, l, P_NG:P_NG + 8] = inp["norm_g"][l].reshape(8, 128).T
        p[:, l, P_BADA:P_BADA + 24] = inp["b_ada"][l].reshape(24, 128).T
        p[:, l, P_GQ] = np.tile(inp["q_norm_g"][l], 2)
        p[:, l, P_GK] = np.tile(inp["k_norm_g"][l], 2)
        p[:, l, P_CSW:P_CSW + 12] = inp["conv_short_w"][l].reshape(3, 4, 128).transpose(2, 1, 0).reshape(128, 12)
        p[:, l, P_CCW:P_CCW + 124] = inp["conv_conf_w"][l].reshape(31, 4, 128).transpose(2, 1, 0).reshape(128, 124)
        p[:, l, P_CCB:P_CCB + 4] = inp["conv_conf_b"][l].reshape(4, 128).T
        p[:, l, P_LNG:P_LNG + 4] = inp["ln_conf_g"][l].reshape(4, 128).T
        p[:, l, P_LNB:P_LNB + 4] = inp["ln_conf_b"][l].reshape(4, 128).T
    return np.ascontiguousarray(p.reshape(128, DEPTH * NPAR))


_CACHE = {}


def make_in_maps(inp, L=DEPTH):
    inp = {k: np.asarray(v, dtype=np.float32) for k, v in inp.items()}
    x = inp["x"].reshape(2, ROWS, GW, D)
    shared = dict(par=_params(inp), cst=_consts(), rbt=np.ascontiguousarray(_rb_tables(inp["rpb"])[:L]),
                  w_ada=np.ascontiguousarray(inp["w_ada"][:L]), w_in=np.ascontiguousarray(inp["w_in"][:L]),
                  w_out_a=np.ascontiguousarray(inp["w_out_a"][:L]), w_out_b=np.ascontiguousarray(inp["w_out_b"][:L]),
                  w_out_c=np.ascontiguousarray(inp["w_out_c"][:L]), w_o=np.ascontiguousarray(inp["w_o"][:L]))
    maps = []
    for c in range(8):
        b, a = c // 4, 64 * (c % 4)
        slab = np.zeros((SROWS, GW, D), np.float32)
        g0, g1 = max(a - TOP, 0), min(a - TOP + SROWS, ROWS)
        slab[g0 - (a - TOP):g1 - (a - TOP)] = x[b, g0:g1]
        x0 = np.empty((D, NTOK), np.float32)
        x0[:, :STOK] = slab.reshape(STOK, D).T
        x0[:, STOK:] = inp["ctx"][b].T
        cond = np.stack([inp["c"][b].reshape(8, 128).T, inp["c_ctx"].reshape(8, 128).T], axis=-1).reshape(128, 16)
        flg = np.zeros((128, 2), np.float32)
        flg[:, 0] = 0.0 if a == 0 else 1.0
        flg[:, 1] = 0.0 if a + 64 == ROWS else 1.0
        m = dict(shared)
        m.update(x0=x0, cond=np.ascontiguousarray(cond), rm=_row_masks(a), flg=flg)
        maps.append(m)
    return maps


def assemble(results):
    out = np.empty((2, ROWS, GW, D), np.float32)
    for c in range(8):
        b, a = c // 4, 64 * (c % 4)
        out[b, a:a + 64] = results[c]["y"].T.reshape(64, GW, D)
    return out.reshape(2, ROWS * GW, D)


def kernel(**inputs):
    if "nc" not in _CACHE:
        _CACHE["nc"] = build_program(DEPTH)[0]
    nc = _CACHE["nc"]
    maps = make_in_maps(inputs)
    res = run_bass_kernel_spmd(nc, maps, core_ids=list(range(8)))
    return assemble(res.results)
```
